# Optimizing a Trainium2 kernel written in Bass

```python
import math
import jax, jax.numpy as jnp
from jax import lax
import numpy as np

D_MODEL = 2048
BATCH = 4
SEQ = 2048
DEPTH = 2
DEC_BATCH = 128
DEC_SEQ = 8
PAST_LEN = 16384
PAGE_SIZE = 128

POOL_WIDTH = D_MODEL
POOL_WINDOWS = (2, 4, 8, 16)
POOL_GROUPS = len(POOL_WINDOWS)
POOL_GROUP_DIM = POOL_WIDTH // POOL_GROUPS
POOL_BUF = max(POOL_WINDOWS) - 1
SSM_INNER = 2 * D_MODEL
SSM_HEAD_DIM = 64
SSM_HEADS = SSM_INNER // SSM_HEAD_DIM
SSM_STATE = 128
SSM_GROUPS = 8
SSM_CONV = 4
SSM_CHUNK = 128
SSM_CONV_DIM = SSM_INNER + 2 * SSM_GROUPS * SSM_STATE
MEM_LEN = 256
MEM_HEADS = 4
MEM_HEAD_DIM = D_MODEL // MEM_HEADS
FFN_DIM = 256 * ((8 * D_MODEL // 3 + 255) // 256)
FFN_CONV = 3
IN_WIDTHS = (POOL_WIDTH, SSM_INNER, SSM_CONV_DIM, SSM_HEADS, D_MODEL, D_MODEL)
IN_SPLITS = tuple(int(s) for s in np.cumsum(IN_WIDTHS)[:-1])
N_IN = sum(IN_WIDTHS)
EPS = 1e-6

kernel_name = "pool_ssd_gated_hybrid_decode_step"


def _rmsnorm(x, g):
    xf = x.astype(jnp.float32)
    y = xf * lax.rsqrt(jnp.mean(xf * xf, axis=-1, keepdims=True) + EPS)
    return (y * g.astype(jnp.float32)).astype(x.dtype)


def _causal_dwconv(u, prev, w, b):
    k = w.shape[0]
    l = u.shape[1]
    ext = jnp.concatenate([prev.astype(u.dtype), u], axis=1)
    y = ext[:, :l] * w[0]
    for j in range(1, k):
        y = y + ext[:, j:j + l] * w[j]
    return y + b, ext[:, l:]


def _pool_mixer(u, prev, pos, w_group, scale):
    b, l, _ = u.shape
    ext = jnp.concatenate([prev.astype(u.dtype), u], axis=1)
    cs = jnp.cumsum(ext.astype(jnp.float32), axis=1)
    cs = jnp.concatenate([jnp.zeros((b, 1, POOL_WIDTH), jnp.float32), cs], axis=1)
    end = cs[:, POOL_BUF + 1:]
    parts = []
    for k, w in enumerate(POOL_WINDOWS):
        ch = slice(k * POOL_GROUP_DIM, (k + 1) * POOL_GROUP_DIM)
        start = cs[:, POOL_BUF + 1 - w:POOL_BUF + 1 - w + l, ch]
        count = jnp.minimum(pos + 1, w).astype(jnp.float32)[None, :, None]
        parts.append((end[:, :, ch] - start) / count)
    mean = jnp.stack(parts, axis=2)
    diff = mean - u.astype(jnp.float32).reshape(b, l, POOL_GROUPS, POOL_GROUP_DIM)
    mixed = jnp.einsum('blgc,gcd->blgd', diff.astype(u.dtype), w_group).reshape(b, l, POOL_WIDTH)
    return mixed * scale, ext[:, l:]


def _ssd_scan(x, dt, a, bm, cm, h0):
    b, l = x.shape[0], x.shape[1]
    q = SSM_CHUNK if l % SSM_CHUNK == 0 else l
    c = l // q
    r = SSM_HEADS // SSM_GROUPS
    xd = (x * dt[..., None]).reshape(b, c, q, SSM_GROUPS, r, SSM_HEAD_DIM)
    la_cs = jnp.cumsum((dt * a).reshape(b, c, q, SSM_GROUPS, r), axis=2)
    bm = bm.reshape(b, c, q, SSM_GROUPS, SSM_STATE)
    cm = cm.reshape(b, c, q, SSM_GROUPS, SSM_STATE)
    causal = jnp.tril(jnp.ones((q, q), bool))[None, None, :, :, None, None]
    seg = la_cs[:, :, :, None] - la_cs[:, :, None, :]
    decay = jnp.exp(jnp.where(causal, seg, -jnp.inf))
    cb = jnp.einsum('bctgn,bcsgn->bctsg', cm, bm)
    y_diag = jnp.einsum('bctsgr,bcsgrp->bctgrp', cb[..., None] * decay, xd)
    to_end = jnp.exp(la_cs[:, :, -1:] - la_cs)
    chunk_states = jnp.einsum('bcsgn,bcsgr,bcsgrp->bcgrpn', bm, to_end, xd)
    chunk_decay = jnp.exp(la_cs[:, :, -1])

    def step(h, inp):
        s, d = inp
        return h * d[..., None, None] + s, h

    h_last, h_in = lax.scan(step, h0.reshape(b, SSM_GROUPS, r, SSM_HEAD_DIM, SSM_STATE),
                            (jnp.moveaxis(chunk_states, 1, 0), jnp.moveaxis(chunk_decay, 1, 0)))
    h_in = jnp.moveaxis(h_in, 0, 1)
    y_off = jnp.einsum('bctgn,bcgrpn,bctgr->bctgrp', cm, h_in, jnp.exp(la_cs))
    y = (y_diag + y_off).reshape(b, l, SSM_HEADS, SSM_HEAD_DIM)
    return y, h_last.reshape(b, SSM_HEADS, SSM_HEAD_DIM, SSM_STATE)


def _mixer(h, pos, pool_prev, conv_prev, ssm_prev, w_in, w_pool_group, pool_scale, w_pool_out,
           ssm_conv_w, ssm_conv_b, ssm_dt_bias, ssm_a_log, ssm_d, ssm_norm, w_ssm_out, w_out):
    f32 = jnp.float32
    b, l, _ = h.shape
    u, z, xbc, dt_raw, g_pool, g_ssm = jnp.split(h @ w_in, IN_SPLITS, axis=-1)
    pooled, pool_new = _pool_mixer(u, pool_prev, pos, w_pool_group, pool_scale)
    out_pool = pooled @ w_pool_out
    xbc, conv_new = _causal_dwconv(xbc, conv_prev, ssm_conv_w, ssm_conv_b)
    xbc = jax.nn.silu(xbc)
    xs, bm, cm = jnp.split(xbc, (SSM_INNER, SSM_INNER + SSM_GROUPS * SSM_STATE), axis=-1)
    dt = jax.nn.softplus(dt_raw.astype(f32) + ssm_dt_bias.astype(f32))
    a = -jnp.exp(ssm_a_log.astype(f32))
    xh = xs.astype(f32).reshape(b, l, SSM_HEADS, SSM_HEAD_DIM)
    y, ssm_new = _ssd_scan(xh, dt, a,
                           bm.astype(f32).reshape(b, l, SSM_GROUPS, SSM_STATE),
                           cm.astype(f32).reshape(b, l, SSM_GROUPS, SSM_STATE),
                           ssm_prev.astype(f32))
    y = y + ssm_d.astype(f32)[:, None] * xh
    y = y.reshape(b, l, SSM_INNER) * jax.nn.silu(z.astype(f32))
    yg = y.reshape(b, l, SSM_GROUPS, SSM_INNER // SSM_GROUPS)
    yg = yg * lax.rsqrt(jnp.mean(yg * yg, axis=-1, keepdims=True) + EPS)
    y = (yg.reshape(b, l, SSM_INNER) * ssm_norm.astype(f32)).astype(h.dtype)
    out_ssm = y @ w_ssm_out
    merged = jax.nn.sigmoid(g_pool) * out_pool + jax.nn.sigmoid(g_ssm) * out_ssm
    return merged @ w_out, pool_new, conv_new, ssm_new.astype(ssm_prev.dtype)


def _mem_kv(mem, norm_g, w_k, w_v):
    b, m, _ = mem.shape
    mn = _rmsnorm(mem, norm_g)
    k = (mn @ w_k).reshape(b, m, MEM_HEADS, MEM_HEAD_DIM)
    v = (mn @ w_v).reshape(b, m, MEM_HEADS, MEM_HEAD_DIM)
    return k, v


def _cross_attn(h, k, v, w_q, w_o):
    b, l, _ = h.shape
    q = (h @ w_q).reshape(b, l, MEM_HEADS, MEM_HEAD_DIM)
    s = jnp.einsum('blhd,bmhd->bhlm', q, k.astype(q.dtype)).astype(jnp.float32) / math.sqrt(MEM_HEAD_DIM)
    p = jax.nn.softmax(s, axis=-1).astype(h.dtype)
    o = jnp.einsum('bhlm,bmhd->blhd', p, v.astype(h.dtype)).reshape(b, l, D_MODEL)
    return o @ w_o


def _conv_ffn(h, prev, w_up, conv_w, conv_b, w_down):
    up, new = _causal_dwconv(h @ w_up, prev, conv_w, conv_b)
    g, v = jnp.split(up, 2, axis=-1)
    return (jax.nn.silu(g) * v) @ w_down, new


def _layer(x, pos, pool_prev, conv_prev, ssm_prev, ffn_prev, mem_k, mem_v, lw):
    (norm_mix, w_in, w_pool_group, pool_scale, w_pool_out, ssm_conv_w, ssm_conv_b, ssm_dt_bias,
     ssm_a_log, ssm_d, ssm_norm, w_ssm_out, w_out, norm_mem_q, w_mem_q, w_mem_o,
     norm_ffn, w_ffn_up, ffn_conv_w, ffn_conv_b, w_ffn_down) = lw
    mix, pool_new, conv_new, ssm_new = _mixer(
        _rmsnorm(x, norm_mix), pos, pool_prev, conv_prev, ssm_prev, w_in, w_pool_group, pool_scale,
        w_pool_out, ssm_conv_w, ssm_conv_b, ssm_dt_bias, ssm_a_log, ssm_d, ssm_norm, w_ssm_out, w_out)
    x = x + mix
    x = x + _cross_attn(_rmsnorm(x, norm_mem_q), mem_k, mem_v, w_mem_q, w_mem_o)
    ffn, ffn_new = _conv_ffn(_rmsnorm(x, norm_ffn), ffn_prev, w_ffn_up, ffn_conv_w, ffn_conv_b, w_ffn_down)
    x = x + ffn
    return x, pool_new, conv_new, ssm_new, ffn_new


def setup_inputs(seed: int = 0) -> dict:
    key = jax.random.key(seed)
    ks = jax.random.split(key, 40)
    f32 = jnp.float32

    def nrm(i, shape, scale):
        return scale * jax.random.normal(ks[i], shape, f32)

    def gain(i, shape):
        return 1.0 + 0.02 * jax.random.normal(ks[i], shape, f32)

    dt0 = jnp.exp(jax.random.uniform(ks[30], (DEPTH, SSM_HEADS), f32) * (math.log(0.1) - math.log(0.001)) + math.log(0.001))
    return {
        "x_prompt": nrm(0, (BATCH, SEQ, D_MODEL), 1.0),
        "x_sample": nrm(1, (DEC_BATCH, DEC_SEQ, D_MODEL), 1.0),
        "state_pool": nrm(2, (DEPTH, DEC_BATCH, POOL_BUF, POOL_WIDTH), 1.0),
        "state_ssm_conv": nrm(3, (DEPTH, DEC_BATCH, SSM_CONV - 1, SSM_CONV_DIM), 1.0),
        "state_ssm": nrm(4, (DEPTH, DEC_BATCH, SSM_HEADS, SSM_HEAD_DIM, SSM_STATE), 0.1),
        "state_ffn_conv": nrm(5, (DEPTH, DEC_BATCH, FFN_CONV - 1, 2 * FFN_DIM), 1.0),
        "cache_mem_k": nrm(6, (DEPTH, DEC_BATCH, MEM_LEN, MEM_HEADS, MEM_HEAD_DIM), 1.0),
        "cache_mem_v": nrm(7, (DEPTH, DEC_BATCH, MEM_LEN, MEM_HEADS, MEM_HEAD_DIM), 1.0),
        "mem_prompt": nrm(8, (BATCH, MEM_LEN, D_MODEL), 1.0),
        "norm_mix": gain(9, (DEPTH, D_MODEL)),
        "w_in": nrm(10, (DEPTH, D_MODEL, N_IN), D_MODEL ** -0.5),
        "w_pool_group": nrm(11, (DEPTH, POOL_GROUPS, POOL_GROUP_DIM, POOL_GROUP_DIM), POOL_GROUP_DIM ** -0.5),
        "pool_scale": 1.0 + 0.1 * jax.random.normal(ks[12], (DEPTH, POOL_WIDTH), f32),
        "w_pool_out": nrm(13, (DEPTH, POOL_WIDTH, D_MODEL), POOL_WIDTH ** -0.5),
        "ssm_conv_w": nrm(14, (DEPTH, SSM_CONV, SSM_CONV_DIM), SSM_CONV ** -0.5),
        "ssm_conv_b": nrm(15, (DEPTH, SSM_CONV_DIM), 0.01),
        "ssm_dt_bias": dt0 + jnp.log(-jnp.expm1(-dt0)),
        "ssm_a_log": jnp.log(jax.random.uniform(ks[16], (DEPTH, SSM_HEADS), f32, 1.0, 16.0)),
        "ssm_d": gain(17, (DEPTH, SSM_HEADS)),
        "ssm_norm": gain(18, (DEPTH, SSM_INNER)),
        "w_ssm_out": nrm(19, (DEPTH, SSM_INNER, D_MODEL), SSM_INNER ** -0.5),
        "w_out": nrm(20, (DEPTH, D_MODEL, D_MODEL), D_MODEL ** -0.5),
        "norm_mem_q": gain(21, (DEPTH, D_MODEL)),
        "w_mem_q": nrm(22, (DEPTH, D_MODEL, D_MODEL), D_MODEL ** -0.5),
        "w_mem_o": nrm(23, (DEPTH, D_MODEL, D_MODEL), D_MODEL ** -0.5),
        "norm_mem_kv": gain(24, (DEPTH, D_MODEL)),
        "w_mem_k": nrm(25, (DEPTH, D_MODEL, D_MODEL), D_MODEL ** -0.5),
        "w_mem_v": nrm(26, (DEPTH, D_MODEL, D_MODEL), D_MODEL ** -0.5),
        "norm_ffn": gain(27, (DEPTH, D_MODEL)),
        "w_ffn_up": nrm(28, (DEPTH, D_MODEL, 2 * FFN_DIM), D_MODEL ** -0.5),
        "ffn_conv_w": nrm(29, (DEPTH, FFN_CONV, 2 * FFN_DIM), FFN_CONV ** -0.5),
        "ffn_conv_b": nrm(31, (DEPTH, 2 * FFN_DIM), 0.01),
        "w_ffn_down": nrm(32, (DEPTH, FFN_DIM, D_MODEL), FFN_DIM ** -0.5),
        "norm_final": gain(33, (D_MODEL,)),
    }


def reference(x_prompt, x_sample, state_pool, state_ssm_conv, state_ssm, state_ffn_conv, cache_mem_k, cache_mem_v,
              mem_prompt, norm_mix, w_in, w_pool_group, pool_scale, w_pool_out, ssm_conv_w, ssm_conv_b, ssm_dt_bias,
              ssm_a_log, ssm_d, ssm_norm, w_ssm_out, w_out, norm_mem_q, w_mem_q, w_mem_o, norm_mem_kv, w_mem_k,
              w_mem_v, norm_ffn, w_ffn_up, ffn_conv_w, ffn_conv_b, w_ffn_down, norm_final):
    bp, lp, _ = x_prompt.shape
    ls = x_sample.shape[1]
    dtp = x_prompt.dtype
    pos_p = jnp.arange(lp, dtype=jnp.int32)
    pos_s = PAST_LEN + jnp.arange(ls, dtype=jnp.int32)
    shared = (norm_mix, w_in, w_pool_group, pool_scale, w_pool_out, ssm_conv_w, ssm_conv_b, ssm_dt_bias,
              ssm_a_log, ssm_d, ssm_norm, w_ssm_out, w_out, norm_mem_q, w_mem_q, w_mem_o,
              norm_ffn, w_ffn_up, ffn_conv_w, ffn_conv_b, w_ffn_down)
    zero_pool = jnp.zeros((bp, POOL_BUF, POOL_WIDTH), dtp)
    zero_conv = jnp.zeros((bp, SSM_CONV - 1, SSM_CONV_DIM), dtp)
    zero_ssm = jnp.zeros((bp, SSM_HEADS, SSM_HEAD_DIM, SSM_STATE), dtp)
    zero_ffn = jnp.zeros((bp, FFN_CONV - 1, 2 * FFN_DIM), dtp)
    yp, ys = x_prompt, x_sample
    pool_p, pool_s, conv_p, conv_s, ssm_p, ssm_s, ffn_p, ffn_s, mk_p, mv_p = ([] for _ in range(10))
    for i in range(DEPTH):
        lw = tuple(w[i] for w in shared)
        k_i, v_i = _mem_kv(mem_prompt, norm_mem_kv[i], w_mem_k[i], w_mem_v[i])
        yp, a0, a1, a2, a3 = _layer(yp, pos_p, zero_pool, zero_conv, zero_ssm, zero_ffn, k_i, v_i, lw)
        ys, b0, b1, b2, b3 = _layer(ys, pos_s, state_pool[i], state_ssm_conv[i], state_ssm[i], state_ffn_conv[i],
                                    cache_mem_k[i], cache_mem_v[i], lw)
        pool_p.append(a0); conv_p.append(a1); ssm_p.append(a2); ffn_p.append(a3)
        pool_s.append(b0); conv_s.append(b1); ssm_s.append(b2); ffn_s.append(b3)
        mk_p.append(k_i); mv_p.append(v_i)
    y_prompt = _rmsnorm(yp, norm_final)
    y_sample = _rmsnorm(ys, norm_final)
    return (y_prompt, y_sample,
            jnp.stack(pool_p), jnp.stack(pool_s),
            jnp.stack(conv_p), jnp.stack(conv_s),
            jnp.stack(ssm_p), jnp.stack(ssm_s),
            jnp.stack(ffn_p), jnp.stack(ffn_s),
            jnp.stack(mk_p), jnp.stack(mv_p))
```

```python
import numpy as np
import ml_dtypes
from contextlib import ExitStack
import concourse.bass as bass
import concourse.mybir as mybir
from concourse.bass_utils import run_bass_kernel_spmd

F32 = mybir.dt.float32
BF16 = mybir.dt.bfloat16
AF = mybir.ActivationFunctionType
ALU = mybir.AluOpType
AX = mybir.AxisListType

D = 2048
NPT = 16
NL = 2
STOP = 99
DEBUG = False
DBG = {}


class StopBuild(Exception):
    pass
NSEQ = 16
LS = 8
EPS = 1e-6
NIN = 16448
FFN = 5632
C_NMIX, C_NQ, C_NF, C_NKV, C_PSC, C_CW, C_CB, C_SN, C_FW, C_FB, C_DC, NCOL = 0, 16, 32, 48, 64, 80, 272, 320, 352, 616, 704, 736
M_UP, M_LP, M_CP, M_ONE, M_US, M_LS, M_CS, M_BLK, M_ID = range(9)
EPOCH = 30000


class Res:
    __slots__ = ("name", "w", "r", "dsem", "dcount", "excl")

    def __init__(self, name):
        self.name = name
        self.excl = name.startswith("pf") or name.startswith("pb")
        self.w = None
        self.r = {}
        self.dsem = None
        self.dcount = 0


class Eng:
    def __init__(self, tr, name, h):
        self.tr, self.name, self.h = tr, name, h
        self.sem = None
        self.count = 0
        self.known = {}

    def next_event(self):
        if self.sem is None or self.count >= EPOCH:
            self.sem = self.tr.new_sem()
            self.count = 0
        self.count += 1
        return (self.sem, self.count)


class Tracker:
    def __init__(self, nc, es):
        self.nc, self.es = nc, es
        self.nsem = 0
        self.E = {
            "pe": Eng(self, "pe", nc.tensor),
            "act": Eng(self, "act", nc.scalar),
            "dve": Eng(self, "dve", nc.vector),
            "pool": Eng(self, "pool", nc.gpsimd),
            "sp": Eng(self, "sp", nc.sync),
        }
        self.out_events = {}

    def new_sem(self):
        self.nsem += 1
        return self.es.enter_context(self.nc.semaphore("s%d" % self.nsem))

    def _wait(self, eng, R, W):
        evs = []
        for r in R:
            if r.w is not None:
                evs.append((r.w, True))
        for w in W:
            if w.w is not None:
                evs.append((w.w, False))
            for ev in w.r.values():
                evs.append((ev, False))
        for (sem, val), raw in evs:
            if sem is eng.sem and (eng.name == "pe" or not raw):
                continue
            k = id(sem)
            if eng.known.get(k, 0) >= val:
                continue
            eng.h.wait_ge(sem, val)
            eng.known[k] = val

    def emit(self, e, fn, R=(), W=()):
        eng = self.E[e]
        if e != "pe":
            W = list(W) + [r for r in R if r.excl and r not in W]
        self._wait(eng, R, W)
        ins = fn()
        ev = eng.next_event()
        ins.then_inc(ev[0], 1)
        for w in W:
            w.w = ev
            w.r = {}
        for r in R:
            if r not in W:
                r.r[id(ev[0])] = ev
        return ins

    def dma(self, q, out, in_, R=(), W=(), key=None, is_out=False):
        eng = self.E[q]
        res = key if key is not None else (W[0] if W else R[0])
        if res.dsem is not None and res.w is not None and res.w[0] is res.dsem and res in W:
            eng.known[id(res.dsem)] = max(eng.known.get(id(res.dsem), 0), res.w[1])
        self._wait(eng, R, W)
        if res.dsem is None or res.dcount + 16 > EPOCH:
            res.dsem = self.new_sem()
            res.dcount = 0
        ins = eng.h.dma_start(out=out, in_=in_)
        ins.then_inc(res.dsem, 16)
        res.dcount += 16
        ev = (res.dsem, res.dcount)
        for w in W:
            w.w = ev
            w.r = {}
        for r in R:
            if r not in W:
                r.r[id(ev[0])] = ev
        if is_out:
            self.out_events[id(res.dsem)] = ev

    def finish(self):
        sp = self.E["sp"]
        for sem, val in self.out_events.values():
            sp.h.wait_ge(sem, val)


def build_program():
    nc = bass.Bass("TRN2", target_bir_lowering=False)

    def din(name, shape, dt=F32):
        return nc.dram_tensor(name, list(shape), dt, kind="ExternalInput").ap()

    def dout(name, shape):
        return nc.dram_tensor(name, list(shape), F32, kind="ExternalOutput").ap()

    xp = din("xp", [2048, D]); xs = din("xs", [128, D])
    spool = din("spool", [2, 240, D]); sconv = din("sconv", [2, 128, 48, NSEQ, 3])
    sffn = din("sffn", [2, 128, 88, NSEQ, 2]); sssm = din("sssm", [2, NSEQ, 128, 4096])
    ck = din("ck", [2, NSEQ, 128, 16, 256]); cv = din("cv", [2, NSEQ, 256, D])
    mem = din("mem", [256, D])
    w_in = din("w_in", [2, D, NIN]); w_pg = din("w_pg", [2, 4, 512, 512]); w_po = din("w_po", [2, D, D])
    w_so = din("w_so", [2, 4096, D]); w_out = din("w_out", [2, D, D]); w_q = din("w_q", [2, D, D])
    w_o = din("w_o", [2, D, D]); w_k = din("w_k", [2, D, D]); w_v = din("w_v", [2, D, D])
    w_up = din("w_up", [2, D, 2 * FFN]); w_dn = din("w_dn", [2, FFN, D])
    pcols_d = din("pcols", [2, 128, NCOL]); prow_d = din("prow", [2, 128, 128]); gfin_d = din("gfin", [128, D])
    masks_d = din("masks", [128, 9, 128], BF16); bands_d = din("bands", [128, 24, 128], BF16)
    blkc_d = din("blkc", [128, NSEQ, 128], BF16); mcol_d = din("mcol", [128, NSEQ])

    yp = dout("yp", [2048, D]); ys = dout("ys", [128, D])
    pool_p = dout("pool_p", [2, 15, D]); pool_s = dout("pool_s", [2, NSEQ, 15, D])
    conv_p = dout("conv_p", [2, 128, 48, 3]); conv_s = dout("conv_s", [2, 128, 48, NSEQ, 3])
    ssm_p = dout("ssm_p", [2, 128, 4096]); ssm_s = dout("ssm_s", [2, NSEQ, 128, 4096])
    ffn_p = dout("ffn_p", [2, 128, 88, 2]); ffn_s = dout("ffn_s", [2, 128, 88, NSEQ, 2])
    mk_p = dout("mk_p", [2, 256, D]); mv_p = dout("mv_p", [2, 256, D])
    xscr = nc.dram_tensor("xscr", [NPT + 1, 128, D], F32, kind="Internal").ap()
    if DEBUG:
        dbg = dout("dbg", [4, 128, D])
        dbgT = dout("dbgT", [6, 128, 32, 128])

    with ExitStack() as es:
        tr = Tracker(nc, es)

        sbtot = [0]

        def sb(name, shape, dt=BF16):
            sbtot[0] += int(np.prod(shape[1:])) * (2 if dt == BF16 else 4)
            return es.enter_context(nc.sbuf_tensor("sb_" + name, list(shape), dt)), Res(name)

        masks, r_masks = sb("masks", [128, 9, 128]); bands, r_bands = sb("bands", [128, 24, 128])
        blkc, r_blkc = sb("blkc", [128, NSEQ, 128]); mcol, r_mcol = sb("mcol", [128, NSEQ], F32)
        pcols, r_pcols = sb("pcols", [128, 2, NCOL], F32); prow, r_prow = sb("prow", [128, 2, 128], F32)
        arow, r_arow = sb("arow", [128, 2, 64], F32)
        epsc, r_epsc = sb("epsc", [128, 1], F32)
        WSL = [sb("wslot%d" % i, [128, 16, 512]) for i in range(2)]
        xres, r_xres = sb("xres", [128, D], F32)
        xn, r_xn = sb("xn", [128, D])
        junk, r_junk = xn, r_xn
        st4, r_st4 = sb("st4", [128, 8], F32)
        hT, r_hT = sb("hT", [128, 16, 128])
        A1, r_A1 = sb("A1", [128, 16, 128]); A2, r_A2 = sb("A2", [128, 16, 128])
        opT, r_opT = sb("opT", [128, 16, 128]); osT, r_osT = A1, r_A1
        utok = [sb("utok%d" % i, [128, D]) for i in range(2)]
        ufp, r_ufp = sb("ufp", [128, D], F32)
        szT, r_szT = sb("szT", [128, 32, 128]); xsT, r_xsT = sb("xsT", [128, 32, 128])
        BCT, r_BCT = sb("BCT", [128, 16, 128])
        STG = [sb("stg%d" % i, [128, 176], F32) for i in range(3)]
        ACC = [sb("acc%d" % i, [128, 128], F32) for i in range(3)]
        chalo, r_chalo = sb("chalo", [128, 48, 3], F32); fhalo, r_fhalo = sb("fhalo", [128, 88, 2], F32)
        sst, r_sst = sb("sst", [128, 4, NSEQ, 3], F32)
        sso, r_sso = sb("sso", [128, 4, NSEQ, 3], F32)
        dtt, r_dtt = sb("dtt", [128, 8, 64], F32)
        lah, r_lah = sb("lah", [128, 2, 64])
        lahf, r_lahf = sb("lahf", [128, 2, 64], F32)
        Btok, r_Btok = sb("Btok", [128, 1024]); Bmsk, r_Bmsk = sb("Bmsk", [128, 1024])
        CBm, r_CBm = sb("CBm", [128, 8, 128], F32)
        xdA, r_xdA = sb("xdA", [128, 4096])
        xdwg, r_xdwg = sb("xdwg", [128, 512])
        yacc, r_yacc = sb("yacc", [128, 4096])
        spl, r_spl = yacc[:, :].rearrange("p (i d) -> p i d", i=2), r_yacc
        LSG = [sb("lsg%d" % i, [128, 2, 128]) for i in range(3)]
        DEC = [sb("dec%d" % i, [128, 128], F32) for i in range(3)]
        MTB = [sb("mtb%d" % i, [128, 128]) for i in range(3)]
        yo_t, r_yo_t = sb("yo_t", [128, 512])
        ytg, r_ytg = sb("ytg", [128, 512])
        yz, r_yz = sb("yz", [128, 4, 128], F32); sq, r_sq = sb("sq", [128, 4, 128])
        rsb, r_rsb = sb("rsb", [128, 128], F32)
        S, r_S = sb("S", [128, 4096], F32); Sb, r_Sb = sb("Sb", [128, 4096])
        Em, r_Em = sb("Em", [128, 64], F32)
        kT, r_kT = sb("kT", [128, 16, 256]); vb, r_vb = sb("vb", [128, 2, D])
        kfp, r_kfp = ufp[:, 0:512], r_ufp
        QM = [(opT, r_opT), (A2, r_A2)]
        PM = [(hT[:, 0:8, :], r_hT), (hT[:, 8:16, :], r_hT)]
        gTc = lambda c: (szT[:, c, :], r_szT) if c < 32 else (xsT[:, c - 32, :], r_xsT)
        pp, r_pp = sb("pp", [128, 4, 256]); pT, r_pT = sb("pT", [128, 8, 128])
        otok, r_otok = xn, r_xn
        sstf, r_sstf = sb("sstf", [128, 4, NSEQ, 2], F32); ssof, r_ssof = sb("ssof", [128, 4, NSEQ, 2], F32)
        PF = [(es.enter_context(nc.psum_tensor("pf%d" % i, [128, 512], F32)), Res("pf%d" % i)) for i in range(6)]
        PB = [(es.enter_context(nc.psum_tensor("pb%d" % i, [128, 1024], BF16)), Res("pb%d" % i)) for i in range(2)]
        rot = {"pf": 0, "pb": 0, "stg": 0, "acc": 0, "lsg": 0, "dec": 0, "mtb": 0}

        def nxt(lst, key):
            i = rot[key]
            rot[key] = (i + 1) % len(lst)
            return lst[i]

        PY = PF.pop()
        psf = lambda: nxt(PF, "pf")
        psb = lambda: nxt(PB, "pb")
        r_din = Res("din")
        r_xscr = [Res("xscr%d" % i) for i in range(NPT + 1)]

        E = tr.emit
        V, A, P = nc.vector, nc.scalar, nc.tensor
        ACOPY = lambda out, in_: A.activation(out=out, in_=in_, func=AF.Identity)

        tr.dma("sp", masks[:], masks_d, W=[r_masks]); tr.dma("sp", bands[:], bands_d, W=[r_bands])
        tr.dma("sp", blkc[:], blkc_d, W=[r_blkc]); tr.dma("sp", mcol[:], mcol_d, W=[r_mcol])
        for l in range(2):
            tr.dma("sp", pcols[:, l, :], pcols_d[l], W=[r_pcols])
            tr.dma("sp", prow[:, l, :], prow_d[l], W=[r_prow])
        for l in range(2):
            E("act", lambda: A.activation(out=arow[:, l, :], in_=prow[:, l, 64:128], func=AF.Exp), R=[r_prow], W=[r_arow])
            E("dve", lambda: V.tensor_scalar(out=arow[:, l, :], in0=arow[:, l, :], scalar1=-1.0, scalar2=None, op0=ALU.mult), R=[r_arow], W=[r_arow])
        ident = masks[:, M_ID, :]
        E("dve", lambda: V.memset(epsc[:], EPS), W=[r_epsc])

        def layer_blocks(l):
            b = []
            wi = w_in[l]
            for nb in range(4): b.append(("u", wi, 0, 16, nb * 512, 512))
            for g in range(4): b.append(("pg", w_pg[l, g], 0, 4, 0, 512))
            for nb in range(4): b.append(("po", w_po[l], 0, 16, nb * 512, 512))
            for nb in range(8): b.append(("z", wi, 0, 16, 2048 + nb * 512, 512))
            for nb in range(12): b.append(("xbc", wi, 0, 16, 6144 + nb * 512, 512))
            b.append(("dt", wi, 0, 16, 12288, 64))
            for nb in range(4):
                for kb in range(2): b.append(("so", w_so[l], kb * 16, 16, nb * 512, 512))
            for nb in range(4): b.append(("gp", wi, 0, 16, 12352 + nb * 512, 512))
            for nb in range(4): b.append(("gs", wi, 0, 16, 14400 + nb * 512, 512))
            for nb in range(4): b.append(("wo", w_out[l], 0, 16, nb * 512, 512))
            for nb in range(4): b.append(("q", w_q[l], 0, 16, nb * 512, 512))
            for nb in range(4): b.append(("o", w_o[l], 0, 16, nb * 512, 512))
            for nb in range(22): b.append(("up", w_up[l], 0, 16, nb * 512, 512))
            for nb in range(4):
                for (k0, kc) in ((0, 16), (16, 16), (32, 12)): b.append(("dn", w_dn[l], k0, kc, nb * 512, 512))
            return b

        wseq = []
        for l in range(NL):
            for nb in range(4): wseq.append(("k", w_k[l], 0, 16, nb * 512, 512))
            for nb in range(4): wseq.append(("v", w_v[l], 0, 16, nb * 512, 512))
            lb = layer_blocks(l)
            for t in range(NPT + 1):
                wseq.extend(lb)
        wstate = {"issued": 0, "next": 0}

        def wissue():
            i = wstate["issued"]
            tag, wap, k0, kc, n0, nw = wseq[i]
            slot, rs = WSL[i % 2]
            src = wap[k0 * 128:(k0 + kc) * 128, n0:n0 + nw].rearrange("(kc p) n -> p kc n", p=128)
            tr.dma("pool", slot[:, 0:kc, 0:nw], src, W=[rs])
            wstate["issued"] = i + 1

        def wprefetch():
            while wstate["issued"] < min(len(wseq), wstate["next"] + 1):
                wissue()

        def wnext(tag, pf=True):
            i = wstate["next"]
            assert wseq[i][0] == tag, (wseq[i][0], tag)
            while wstate["issued"] < min(len(wseq), i + (2 if pf else 1)):
                wissue()
            wstate["next"] = i + 1
            slot, rs = WSL[i % 2]
            return slot, rs, wseq[i][3], wseq[i][5]

        def mm(ps, lhsT, rhs, st, sp_, R, Wr):
            E("pe", lambda: P.matmul(ps, lhsT, rhs, start=st, stop=sp_), R=R, W=[Wr])

        def tp(ps, in_, R, Wr, idn=None):
            E("pe", lambda: P.transpose(ps, in_, ident if idn is None else idn), R=R + [r_masks], W=[Wr])

        def ws_block(tag, rhs_of_k, r_act, evac, first=True, last=True, ps=None):
            slot, rs, kc, nw = wnext(tag)
            if ps is None:
                ps = psf()
            pt, rp = ps
            for j in range(nw // 128):
                for k in range(kc):
                    mm(pt[:, j * 128:(j + 1) * 128], slot[:, k, j * 128:(j + 1) * 128], rhs_of_k(k),
                       first and k == 0, last and k == kc - 1, [rs] + (r_act if isinstance(r_act, list) else [r_act]), rp)
            if last:
                for j in range(nw // 128):
                    evac(j, pt[:, j * 128:(j + 1) * 128], rp)
            return ps

        def as_block(tag, lhsT_of_k, r_act, first=True, last=True, ps=None, M=128):
            slot, rs, kc, nw = wnext(tag)
            if ps is None:
                ps = psf()
            pt, rp = ps
            for k in range(kc):
                mm(pt[0:M, 0:nw], lhsT_of_k(k), slot[:, k, 0:nw], first and k == 0, last and k == kc - 1, [rs] + (r_act if isinstance(r_act, list) else [r_act]), rp)
            return ps

        def pc(l, off, c=None):
            return pcols[:, l, off:off + 1] if c is None else pcols[:, l, off + c:off + c + 1]

        def rms_to_T(l, goff, src=None, r_src=None, dstT=None, r_dstT=None, ncols=128, coff=0):
            src = xres if src is None else src
            r_src = r_xres if r_src is None else r_src
            dstT = hT if dstT is None else dstT
            r_dstT = r_hT if r_dstT is None else r_dstT
            E("dve", lambda: V.memset(st4[:, 0:1], 0.0), W=[r_st4])
            E("act", lambda: A.activation(out=junk[:], in_=src[:], func=AF.Square, accum_out=st4[:, 0:1]), R=[r_src, r_st4], W=[r_junk, r_st4])
            E("act", lambda: A.activation(out=st4[:, 2:3], in_=st4[:, 0:1], func=AF.Sqrt, scale=1.0 / D, bias=epsc[:, 0:1]), R=[r_st4, r_epsc], W=[r_st4])
            E("dve", lambda: V.reciprocal(out=st4[:, 1:2], in_=st4[:, 2:3]), R=[r_st4], W=[r_st4])
            E("dve", lambda: V.tensor_scalar(out=xn[:], in0=src[:], scalar1=st4[:, 1:2], scalar2=None, op0=ALU.mult), R=[r_src, r_st4], W=[r_xn])
            for c4 in range(4):
                pt, rp = psb()
                for i in range(4):
                    c = c4 * 4 + i
                    tp(pt[:, i * 128:(i + 1) * 128], xn[:, c * 128:(c + 1) * 128], [r_xn], rp)
                for i in range(4):
                    c = c4 * 4 + i
                    E("act", lambda: A.activation(out=dstT[:, c, coff:coff + 128], in_=pt[:, i * 128:(i + 1) * 128], func=AF.Identity, scale=pc(l, goff, c)),
                      R=[rp, r_pcols], W=[r_dstT])

        def mem_kv(l):
            mnT, r_mnT = kT, r_kT
            for mt in range(2):
                tr.dma("sp", xres[:], mem[mt * 128:(mt + 1) * 128, :], W=[r_xres])
                rms_to_T(l, C_NKV, dstT=(A1 if mt == 0 else A2), r_dstT=(r_A1 if mt == 0 else r_A2))
                if STOP <= 0.2:
                    raise StopBuild()
            mn = [(A1, r_A1), (A2, r_A2)]
            for which, tag, outd in (("k", "k", mk_p), ("v", "v", mv_p)):
                if which == "v" and STOP <= 0.7:
                    raise StopBuild()
                for nb in range(4):
                    slot, rs, kc, nw = wnext(tag)
                    for mt in range(2):
                        pt, rp = psf()
                        for k in range(kc):
                            mm(pt[:, 0:512], mn[mt][0][:, k, :], slot[:, k, 0:512], k == 0, k == kc - 1, [rs, mn[mt][1]], rp)
                        E("dve", lambda: V.tensor_copy(out=kfp[:], in_=pt[:, 0:512]), R=[rp], W=[r_kfp])
                        tr.dma("sp", outd[l, mt * 128:(mt + 1) * 128, nb * 512:(nb + 1) * 512], kfp[:], R=[r_kfp], key=r_kfp, is_out=True)
                        if STOP <= 0.5:
                            raise StopBuild()
                        if which == "v":
                            E("act", lambda: ACOPY(out=vb[:, mt, nb * 512:(nb + 1) * 512], in_=pt[:, 0:512]), R=[rp], W=[r_vb])
                        else:
                            E("act", lambda: ACOPY(out=otok[:, 0:512], in_=pt[:, 0:512]), R=[rp], W=[r_otok])
                            pb_, rpb = psb()
                            for i in range(4):
                                tp(pb_[:, i * 128:(i + 1) * 128], otok[:, i * 128:(i + 1) * 128], [r_otok], rpb)
                            for i in range(4):
                                E("dve", lambda: V.tensor_copy(out=kT[:, nb * 4 + i, mt * 128:(mt + 1) * 128], in_=pb_[:, i * 128:(i + 1) * 128]), R=[rpb], W=[r_kT])
                            if STOP <= 0.6:
                                raise StopBuild()

        def do_tile(l, ti):
            smp = ti == NPT
            first = ti == 0
            lastp = ti == NPT - 1
            NB = NSEQ if smp else 1
            L = LS if smp else 128
            mU, mL, mC, mB = (M_US, M_LS, M_CS, M_BLK) if smp else (M_UP, M_LP, M_CP, M_ONE)
            def dump(i, buf, r_buf):
                if DEBUG and smp and l == 0:
                    tr.dma("sp", dbg[i], buf[:], R=[r_buf], key=r_buf, is_out=True)

            def dumpT(i, buf, r_buf, n=16):
                if DEBUG and smp and l == 0:
                    tr.dma("pool", dbgT[i, :, 0:n, :], buf[:, 0:n, :], R=[r_buf], key=r_buf, is_out=True)
            if l == 0:
                src = xs if smp else xp[ti * 128:(ti + 1) * 128, :]
                tr.dma("sp", xres[:], src, W=[r_xres])
            else:
                tr.dma("sp", xres[:], xscr[ti], R=[r_xscr[ti]], W=[r_xres], key=r_xres)
            rms_to_T(l, C_NMIX)
            hk = lambda k: hT[:, k, :]
            ut, r_ut = utok[ti % 2]
            up_, r_up = utok[(ti + 1) % 2]
            need_fp = smp or lastp
            for nb in range(4):
                pt, rp = as_block("u", hk, r_hT)
                E("act", lambda: ACOPY(out=ut[:, nb * 512:(nb + 1) * 512], in_=pt[:, 0:512]), R=[rp], W=[r_ut])
                if need_fp:
                    E("dve", lambda: V.tensor_copy(out=ufp[:, nb * 512:(nb + 1) * 512], in_=pt[:, 0:512]), R=[rp], W=[r_ufp])
            if lastp:
                tr.dma("sp", pool_p[l], ufp[113:128, :], R=[r_ufp], key=r_ufp, is_out=True)
            if smp:
                for b in range(NSEQ):
                    tr.dma("sp", pool_s[l, b, 7:15, :], ufp[b * LS:(b + 1) * LS, :], R=[r_ufp], key=r_ufp, is_out=True)
                tr.dma("sp", pool_s[l][:, 0:7, :], spool[l].rearrange("(b r) d -> b r d", r=15)[:, 8:15, :], key=r_ufp, R=[r_ufp], is_out=True)
                for i in range(2):
                    tr.dma("pool", spl[0:120, i, :], spool[l, i * 120:(i + 1) * 120, :], W=[r_spl])
            if STOP <= 2:
                raise StopBuild()
            for g in range(4):
                pt, rp = psf()
                for i in range(4):
                    c = g * 4 + i
                    o_ = pt[:, i * 128:(i + 1) * 128]
                    if smp:
                        mm(o_, ut[:, c * 128:(c + 1) * 128], bands[:, 12 + g, :], True, False, [r_ut, r_bands], rp)
                        mm(o_, spl[0:120, 0, c * 128:(c + 1) * 128], bands[0:120, 16 + 2 * g, :], False, False, [r_spl, r_bands], rp)
                        mm(o_, spl[0:120, 1, c * 128:(c + 1) * 128], bands[0:120, 17 + 2 * g, :], False, True, [r_spl, r_bands], rp)
                    elif first:
                        mm(o_, ut[:, c * 128:(c + 1) * 128], bands[:, 8 + g, :], True, True, [r_ut, r_bands], rp)
                    else:
                        mm(o_, ut[:, c * 128:(c + 1) * 128], bands[:, g, :], True, False, [r_ut, r_bands], rp)
                        mm(o_, up_[:, c * 128:(c + 1) * 128], bands[:, 4 + g, :], False, True, [r_up, r_bands], rp)
                E("dve", lambda: V.tensor_copy(out=A1[:, g * 4:(g + 1) * 4, :], in_=pt[:, 0:512].rearrange("p (a b) -> p a b", a=4)), R=[rp], W=[r_A1])
            for g in range(4):
                def ev(j, pa, rp, g=g):
                    E("act", lambda: A.activation(out=A2[:, g * 4 + j, :], in_=pa, func=AF.Identity, scale=pc(l, C_PSC, g * 4 + j)), R=[rp, r_pcols], W=[r_A2])
                ws_block("pg", lambda k, g=g: A1[:, g * 4 + k, :], r_A1, ev)
            for nb in range(4):
                def ev(j, pa, rp, nb=nb):
                    E("dve", lambda: V.tensor_copy(out=opT[:, nb * 4 + j, :], in_=pa), R=[rp], W=[r_opT])
                ws_block("po", lambda k: A2[:, k, :], r_A2, ev)
            if STOP <= 3:
                raise StopBuild()
            dumpT(0, opT, r_opT)
            for nb in range(8):
                def ev(j, pa, rp, nb=nb):
                    E("act", lambda: A.activation(out=szT[:, nb * 4 + j, :], in_=pa, func=AF.Silu), R=[rp], W=[r_szT])
                ws_block("z", hk, r_hT, ev)

            def conv_chunk(pa, rp, c, K, halo, r_halo, woff, boff, nW, sblk, finish):
                H = K - 1
                (sg, r_sg), (ac, r_ac) = nxt(STG, "stg"), nxt(ACC, "acc")
                W_ = H + L
                sv = sg[:, 0:NB * W_].rearrange("p (b w) -> p b w", b=NB)
                E("act", lambda: A.activation(out=sv[:, :, H:H + L], in_=pa.rearrange("p (b j) -> p b j", b=NB), func=AF.Identity), R=[rp], W=[r_sg])
                if smp:
                    hin, r_hin, hout, r_hout, ci = sblk
                    E("dve", lambda: V.tensor_copy(out=sv[:, :, 0:H], in_=hin[:, ci, :, 0:H]), R=[r_hin], W=[r_sg])
                    E("dve", lambda: V.tensor_copy(out=hout[:, ci, :, 0:H], in_=sv[:, :, L:L + H]), R=[r_sg], W=[r_hout])
                else:
                    if first:
                        E("dve", lambda: V.memset(sv[:, :, 0:H], 0.0), W=[r_sg])
                    else:
                        E("dve", lambda: V.tensor_copy(out=sv[:, 0, 0:H], in_=halo[:, c, :]), R=[r_halo], W=[r_sg])
                    E("dve", lambda: V.tensor_copy(out=halo[:, c, :], in_=sv[:, 0, L:L + H]), R=[r_sg], W=[r_halo])
                av = ac[:, :].rearrange("p (b j) -> p b j", b=NB)
                E("act", lambda: A.activation(out=av, in_=sv[:, :, 0:L], func=AF.Identity, scale=pc(l, woff, c), bias=pc(l, boff, c)),
                  R=[r_sg, r_pcols], W=[r_ac])
                for j in range(1, K):
                    E("dve", lambda: V.scalar_tensor_tensor(out=av, in0=sv[:, :, j:j + L], scalar=pc(l, woff, j * nW + c), in1=av, op0=ALU.mult, op1=ALU.add),
                      R=[r_sg, r_pcols, r_ac], W=[r_ac])
                finish(ac, r_ac)

            for nb in range(12):
                if smp:
                    tr.dma("sp", sst[:], sconv[l, :, nb * 4:(nb + 1) * 4, :, :], W=[r_sst])

                def ev(j, pa, rp, nb=nb):
                    c = nb * 4 + j

                    def fin(ac, r_ac):
                        dst = xsT[:, c, :] if c < 32 else BCT[:, c - 32, :]
                        rd = r_xsT if c < 32 else r_BCT
                        E("act", lambda: A.activation(out=dst, in_=ac[:], func=AF.Silu), R=[r_ac], W=[rd])
                    conv_chunk(pa, rp, c, 4, chalo, r_chalo, C_CW, C_CB, 48, (sst, r_sst, sso, r_sso, j), fin)
                ws_block("xbc", hk, r_hT, ev)
                if smp:
                    tr.dma("sp", conv_s[l, :, nb * 4:(nb + 1) * 4, :, :], sso[:], R=[r_sso], key=r_sso, is_out=True)
            if lastp:
                tr.dma("sp", conv_p[l], chalo[:], R=[r_chalo], key=r_chalo, is_out=True)
            pt, rp = as_block("dt", hk, r_hT)
            dt_, la_, E_, cs_, te_, dtw_, cd_, tmp_ = [dtt[:, i, :] for i in range(8)]
            E("dve", lambda: V.tensor_tensor(out=tmp_, in0=pt[:, 0:64], in1=prow[:, l, 0:64], op=ALU.add), R=[rp, r_prow], W=[r_dtt])
            E("act", lambda: A.activation(out=tmp_, in_=tmp_, func=AF.Exp), R=[r_dtt], W=[r_dtt])
            E("act", lambda: A.activation(out=dt_, in_=tmp_, func=AF.Ln, bias=1.0, scale=1.0), R=[r_dtt], W=[r_dtt])
            E("dve", lambda: V.tensor_tensor(out=la_, in0=dt_, in1=arow[:, l, :], op=ALU.mult), R=[r_dtt, r_arow], W=[r_dtt])
            E("dve", lambda: V.tensor_copy(out=lah[:, 0, :], in_=la_), R=[r_dtt], W=[r_lah])
            E("dve", lambda: V.tensor_tensor(out=lah[:, 1, :], in0=la_, in1=lah[:, 0, :], op=ALU.subtract), R=[r_dtt, r_lah], W=[r_lah])
            E("dve", lambda: V.tensor_copy(out=lahf[:], in_=lah[:]), R=[r_lah], W=[r_lahf])
            pt, rp = psf()
            for i in range(2):
                mm(pt[:, 0:64], masks[:, mL, :], lah[:, i, :], i == 0, i == 1, [r_masks, r_lah], rp)
            for i in range(2):
                mm(pt[:, 64:128], masks[:, mB, :], lah[:, i, :], i == 0, i == 1, [r_masks, r_lah], rp)
            E("act", lambda: A.activation(out=E_, in_=pt[:, 0:64], func=AF.Exp), R=[rp], W=[r_dtt])
            E("dve", lambda: V.tensor_copy(out=cs_, in_=pt[:, 0:64]), R=[rp], W=[r_dtt])
            E("dve", lambda: V.tensor_tensor(out=tmp_, in0=pt[:, 64:128], in1=cs_, op=ALU.subtract), R=[rp, r_dtt], W=[r_dtt])
            E("act", lambda: A.activation(out=te_, in_=tmp_, func=AF.Exp), R=[r_dtt], W=[r_dtt])
            E("act", lambda: A.activation(out=cd_, in_=pt[:, 64:128], func=AF.Exp), R=[rp], W=[r_dtt])
            E("dve", lambda: V.tensor_tensor(out=dtw_, in0=dt_, in1=te_, op=ALU.mult), R=[r_dtt], W=[r_dtt])
            if STOP <= 4:
                raise StopBuild()
            pb_, rpb = psb()
            for g in range(8):
                tp(pb_[:, g * 128:(g + 1) * 128], BCT[:, g, :], [r_BCT], rpb)
            E("dve", lambda: V.tensor_copy(out=Btok[:], in_=pb_[:, 0:1024]), R=[rpb], W=[r_Btok])
            for g2 in range(2):
                pt, rp = psf()
                for i in range(4):
                    g = g2 * 4 + i
                    mm(pt[:, i * 128:(i + 1) * 128], BCT[:, g, :], BCT[:, 8 + g, :], True, True, [r_BCT], rp)
                for i in range(4):
                    g = g2 * 4 + i
                    E("dve", lambda: V.tensor_tensor(out=CBm[:, g, :], in0=pt[:, i * 128:(i + 1) * 128], in1=masks[:, mC, :], op=ALU.mult), R=[rp, r_masks], W=[r_CBm])
            for g in range(8):
                pb_, rpb = psb()
                for i in range(4):
                    tp(pb_[:, i * 128:(i + 1) * 128], xsT[:, g * 4 + i, :], [r_xsT], rpb)
                pv = pb_[:, 0:512].rearrange("p (h d) -> p h d", h=8)
                E("dve", lambda: V.tensor_tensor(out=xdA[:, g * 512:(g + 1) * 512].rearrange("p (h d) -> p h d", h=8), in0=pv,
                                                 in1=dtt[:, 0, g * 8:(g + 1) * 8].unsqueeze(2).broadcast_to([128, 8, 64]), op=ALU.mult), R=[rpb, r_dtt], W=[r_xdA])
            def make_xdw(g):
                E("dve", lambda: V.tensor_tensor(out=xdwg[:].rearrange("p (h d) -> p h d", h=8), in0=xdA[:, g * 512:(g + 1) * 512].rearrange("p (h d) -> p h d", h=8),
                                                 in1=dtt[:, 4, g * 8:(g + 1) * 8].unsqueeze(2).broadcast_to([128, 8, 64]), op=ALU.mult), R=[r_xdA, r_dtt], W=[r_xdwg])
            have_off = smp or not first
            if smp:
                for b in range(NSEQ):
                    tr.dma("sp", S[:], sssm[l, b], W=[r_S])
                    E("act", lambda: ACOPY(out=Sb[:], in_=S[:]), R=[r_S], W=[r_Sb])
                    E("dve", lambda: V.tensor_scalar(out=Em[:], in0=E_, scalar1=mcol[:, b:b + 1], scalar2=None, op0=ALU.mult), R=[r_dtt, r_mcol], W=[r_Em])
                    E("dve", lambda: V.tensor_scalar(out=Bmsk[:], in0=Btok[:], scalar1=mcol[:, b:b + 1], scalar2=None, op0=ALU.mult), R=[r_Btok, r_mcol], W=[r_Bmsk])
                    pt, rp = psf()
                    for i in range(2):
                        mm(pt[:, 0:64], blkc[:, b, :], lah[:, i, :], i == 0, i == 1, [r_blkc, r_lah], rp)
                    E("act", lambda: A.activation(out=cd_, in_=pt[:, 0:64], func=AF.Exp), R=[rp], W=[r_dtt])
                    for g in range(8):
                        gs_ = slice(g * 512, (g + 1) * 512)
                        pt, rp = psf()
                        mm(pt[:, 0:512], BCT[:, 8 + g, :], Sb[:, gs_], True, True, [r_BCT, r_Sb], rp)
                        yv = yacc[:, gs_].rearrange("p (h d) -> p h d", h=8)
                        emb = Em[:, g * 8:(g + 1) * 8].unsqueeze(2).broadcast_to([128, 8, 64])
                        pv = pt[:, 0:512].rearrange("p (h d) -> p h d", h=8)
                        if b == 0:
                            E("dve", lambda: V.tensor_tensor(out=yv, in0=pv, in1=emb, op=ALU.mult), R=[rp, r_Em], W=[r_yacc])
                        else:
                            E("dve", lambda: V.tensor_tensor(out=yo_t[:].rearrange("p (h d) -> p h d", h=8), in0=pv, in1=emb, op=ALU.mult), R=[rp, r_Em], W=[r_yo_t])
                            E("dve", lambda: V.tensor_tensor(out=yacc[:, gs_], in0=yacc[:, gs_], in1=yo_t[:], op=ALU.add), R=[r_yo_t, r_yacc], W=[r_yacc])
                        pt2, rp2 = psf()
                        make_xdw(g)
                        mm(pt2[:, 0:512], Bmsk[:, g * 128:(g + 1) * 128], xdwg[:], True, True, [r_Bmsk, r_xdwg], rp2)
                        E("dve", lambda: V.tensor_tensor(out=S[:, gs_].rearrange("p (h d) -> p h d", h=8), in0=S[:, gs_].rearrange("p (h d) -> p h d", h=8),
                                                         in1=dtt[:, 6, g * 8:(g + 1) * 8].unsqueeze(2).broadcast_to([128, 8, 64]), op=ALU.mult), R=[r_S, r_dtt, r_Sb], W=[r_S])
                        E("dve", lambda: V.tensor_tensor(out=S[:, gs_], in0=S[:, gs_], in1=pt2[:, 0:512], op=ALU.add), R=[r_S, rp2], W=[r_S])
                    tr.dma("sp", ssm_s[l, b], S[:], R=[r_S], key=r_S, is_out=True)
            else:
                if first:
                    E("dve", lambda: V.memset(S[:], 0.0), W=[r_S])
                for g in range(8):
                    gs_ = slice(g * 512, (g + 1) * 512)
                    if have_off:
                        pt, rp = psf()
                        mm(pt[:, 0:512], BCT[:, 8 + g, :], Sb[:, gs_], True, True, [r_BCT, r_Sb], rp)
                        E("dve", lambda: V.tensor_tensor(out=yacc[:, gs_].rearrange("p (h d) -> p h d", h=8), in0=pt[:, 0:512].rearrange("p (h d) -> p h d", h=8),
                                                         in1=dtt[:, 2, g * 8:(g + 1) * 8].unsqueeze(2).broadcast_to([128, 8, 64]), op=ALU.mult), R=[rp, r_dtt], W=[r_yacc])
                for g in range(8):
                    gs_ = slice(g * 512, (g + 1) * 512)
                    pt2, rp2 = psf()
                    make_xdw(g)
                    mm(pt2[:, 0:512], Btok[:, g * 128:(g + 1) * 128], xdwg[:], True, True, [r_Btok, r_xdwg], rp2)
                    E("dve", lambda: V.tensor_tensor(out=S[:, gs_].rearrange("p (h d) -> p h d", h=8), in0=S[:, gs_].rearrange("p (h d) -> p h d", h=8),
                                                     in1=dtt[:, 6, g * 8:(g + 1) * 8].unsqueeze(2).broadcast_to([128, 8, 64]), op=ALU.mult), R=[r_S, r_dtt, r_Sb], W=[r_S])
                    E("dve", lambda: V.tensor_tensor(out=S[:, gs_], in0=S[:, gs_], in1=pt2[:, 0:512], op=ALU.add), R=[r_S, rp2], W=[r_S])
                E("act", lambda: ACOPY(out=Sb[:], in_=S[:]), R=[r_S, r_yacc], W=[r_Sb])
                if lastp:
                    tr.dma("sp", ssm_p[l], S[:], R=[r_S], key=r_S, is_out=True)
            for g in range(8):
                gs_ = slice(g * 512, (g + 1) * 512)
                py, rpy = PY
                for hh in range(8):
                    h = g * 8 + hh
                    (lg, r_lg), (dc, r_dc), (mt_, r_mt) = nxt(LSG, "lsg"), nxt(DEC, "dec"), nxt(MTB, "mtb")
                    for i in range(2):
                        E("act", lambda: A.activation(out=lg[:, i, :], in_=masks[:, mU, :], func=AF.Identity, scale=lahf[:, i, h:h + 1]), R=[r_masks, r_lahf], W=[r_lg])
                    pg_, rpg = psf()
                    for i in range(2):
                        mm(pg_[:, 0:128], lg[:, i, :], masks[:, mL, :], i == 0, i == 1, [r_lg, r_masks], rpg)
                    E("act", lambda: A.activation(out=dc[:], in_=pg_[:, 0:128], func=AF.Exp), R=[rpg], W=[r_dc])
                    E("dve", lambda: V.tensor_tensor(out=mt_[:], in0=CBm[:, g, :], in1=dc[:], op=ALU.mult), R=[r_CBm, r_dc], W=[r_mt])
                    mm(py[:, hh * 64:(hh + 1) * 64], mt_[:], xdA[:, h * 64:(h + 1) * 64], True, True, [r_mt, r_xdA], rpy)
                if have_off:
                    E("dve", lambda: V.tensor_tensor(out=ytg[:], in0=py[:, 0:512], in1=yacc[:, gs_], op=ALU.add), R=[rpy, r_yacc], W=[r_ytg])
                else:
                    E("dve", lambda: V.tensor_copy(out=ytg[:], in_=py[:, 0:512]), R=[rpy], W=[r_ytg])
                pb_, rpb = psb()
                for i in range(4):
                    tp(pb_[:, i * 128:(i + 1) * 128], ytg[:, i * 128:(i + 1) * 128], [r_ytg], rpb)
                for i in range(4):
                    c = g * 4 + i
                    E("dve", lambda: V.scalar_tensor_tensor(out=yz[:, i, :], in0=xsT[:, c, :], scalar=pc(l, C_DC, c), in1=pb_[:, i * 128:(i + 1) * 128], op0=ALU.mult, op1=ALU.add),
                      R=[r_xsT, r_pcols, rpb], W=[r_yz])
                    E("dve", lambda: V.tensor_tensor(out=yz[:, i, :], in0=yz[:, i, :], in1=szT[:, c, :], op=ALU.mult), R=[r_yz, r_szT], W=[r_yz])
                E("act", lambda: A.activation(out=sq[:], in_=yz[:], func=AF.Square), R=[r_yz], W=[r_sq])
                pm_, rpm = psf()
                for i in range(4):
                    mm(pm_[:, 0:128], masks[:, M_ONE, :], sq[:, i, :], i == 0, i == 3, [r_masks, r_sq], rpm)
                E("act", lambda: A.activation(out=rsb[:], in_=pm_[:, 0:128], func=AF.Sqrt, scale=1.0 / 512, bias=epsc[:, 0:1]), R=[rpm, r_epsc], W=[r_rsb])
                E("dve", lambda: V.reciprocal(out=rsb[:], in_=rsb[:]), R=[r_rsb], W=[r_rsb])
                for i in range(4):
                    c = g * 4 + i
                    E("dve", lambda: V.scalar_tensor_tensor(out=szT[:, c, :], in0=yz[:, i, :], scalar=pc(l, C_SN, c), in1=rsb[:], op0=ALU.mult, op1=ALU.mult),
                      R=[r_yz, r_pcols, r_rsb], W=[r_szT])
            if STOP <= 5:
                raise StopBuild()
            dumpT(1, szT, r_szT, 32)
            for nb in range(4):
                def ev(j, pa, rp, nb=nb):
                    E("dve", lambda: V.tensor_copy(out=osT[:, nb * 4 + j, :], in_=pa), R=[rp], W=[r_osT])
                slA, rsA, _, _ = wnext("so")
                slB, rsB, _, _ = wnext("so", pf=False)
                pt, rp = psf()
                for j in range(4):
                    for k in range(16):
                        mm(pt[:, j * 128:(j + 1) * 128], slA[:, k, j * 128:(j + 1) * 128], szT[:, k, :], k == 0, False, [rsA, r_szT], rp)
                    for k in range(16):
                        mm(pt[:, j * 128:(j + 1) * 128], slB[:, k, j * 128:(j + 1) * 128], szT[:, 16 + k, :], False, k == 15, [rsB, r_szT], rp)
                wprefetch()
                for j in range(4):
                    ev(j, pt[:, j * 128:(j + 1) * 128], rp)
            dumpT(2, osT, r_osT)
            for nb in range(4):
                def ev(j, pa, rp, nb=nb):
                    (ac, r_ac) = nxt(ACC, "acc")
                    E("act", lambda: A.activation(out=ac[:], in_=pa, func=AF.Sigmoid), R=[rp], W=[r_ac])
                    E("dve", lambda: V.tensor_tensor(out=opT[:, nb * 4 + j, :], in0=opT[:, nb * 4 + j, :], in1=ac[:], op=ALU.mult), R=[r_ac, r_opT], W=[r_opT])
                ws_block("gp", hk, r_hT, ev)
            for nb in range(4):
                def ev(j, pa, rp, nb=nb):
                    (ac, r_ac) = nxt(ACC, "acc")
                    E("act", lambda: A.activation(out=ac[:], in_=pa, func=AF.Sigmoid), R=[rp], W=[r_ac])
                    E("dve", lambda: V.tensor_tensor(out=ac[:], in0=osT[:, nb * 4 + j, :], in1=ac[:], op=ALU.mult), R=[r_ac, r_osT], W=[r_ac])
                    E("dve", lambda: V.tensor_tensor(out=opT[:, nb * 4 + j, :], in0=opT[:, nb * 4 + j, :], in1=ac[:], op=ALU.add), R=[r_ac, r_opT], W=[r_opT])
                ws_block("gs", hk, r_hT, ev)
            dumpT(3, opT, r_opT)
            for nb in range(4):
                pt, rp = as_block("wo", lambda k: opT[:, k, :], r_opT)
                E("dve", lambda: V.tensor_tensor(out=xres[:, nb * 512:(nb + 1) * 512], in0=xres[:, nb * 512:(nb + 1) * 512], in1=pt[:, 0:512], op=ALU.add), R=[rp, r_xres], W=[r_xres])

            if STOP <= 6:
                raise StopBuild()
            dump(0, xres, r_xres)
            rms_to_T(l, C_NQ)
            for nb in range(4):
                def ev(j, pa, rp, nb=nb):
                    E("act", lambda: ACOPY(out=A1[:, nb * 4 + j, :], in_=pa), R=[rp], W=[r_A1])
                ws_block("q", hk, r_hT, ev)
            scs = [psf() for _ in range(4)]
            nbq = NSEQ if smp else 1
            for b in range(nbq):
                if smp:
                    for c4 in range(4):
                        tr.dma("pool", kT[:, c4 * 4:(c4 + 1) * 4, :], ck[l, b, :, c4 * 4:(c4 + 1) * 4, :], W=[r_kT])
                    qm, r_qm = QM[b % 2]
                    if b < 2:
                        E("dve", lambda: V.memset(qm[:], 0.0), W=[r_qm])
                    else:
                        E("dve", lambda: V.memset(qm[:, :, (b - 2) * LS:(b - 1) * LS], 0.0), W=[r_qm])
                    E("dve", lambda: V.tensor_copy(out=qm[:, :, b * LS:(b + 1) * LS], in_=A1[:, :, b * LS:(b + 1) * LS]), R=[r_A1], W=[r_qm])
                else:
                    qm, r_qm = A1, r_A1
                for h in range(4):
                    pt, rp = scs[h]
                    o_ = pt[:, 0:256]
                    for i in range(4):
                        mm(o_, qm[:, h * 4 + i, :], kT[:, h * 4 + i, :], b == 0 and i == 0, b == nbq - 1 and i == 3, [r_qm, r_kT], rp)
            scale = 1.0 / float(np.sqrt(512.0))
            mx, nmx, ssum, rs_ = st4[:, 4:5], st4[:, 5:6], st4[:, 6:7], st4[:, 7:8]
            for h in range(4):
                pt, rp = scs[h]
                o_ = pt[:, 0:256]
                E("dve", lambda: V.tensor_reduce(out=Em[:, h:h + 1], in_=o_, axis=AX.X, op=ALU.max), R=[rp], W=[r_Em])
                E("dve", lambda: V.tensor_scalar(out=Em[:, 4 + h:5 + h], in0=Em[:, h:h + 1], scalar1=-scale, scalar2=None, op0=ALU.mult), R=[r_Em], W=[r_Em])
                E("dve", lambda: V.memset(Em[:, 8 + h:9 + h], 0.0), W=[r_Em])
                E("act", lambda: A.activation(out=pp[:, h, :], in_=o_, func=AF.Exp, bias=Em[:, 4 + h:5 + h], scale=scale, accum_out=Em[:, 8 + h:9 + h]), R=[rp, r_Em], W=[r_pp, r_Em])
                E("dve", lambda: V.reciprocal(out=Em[:, 12 + h:13 + h], in_=Em[:, 8 + h:9 + h]), R=[r_Em], W=[r_Em])
            pb_, rpb = psb()
            for h in range(4):
                for mt in range(2):
                    i = h * 2 + mt
                    tp(pb_[:, i * 128:(i + 1) * 128], pp[:, h, mt * 128:(mt + 1) * 128], [r_pp], rpb)
            E("dve", lambda: V.tensor_copy(out=pT[:], in_=pb_[:, 0:1024].rearrange("p (a b) -> p a b", a=8)), R=[rpb], W=[r_pT])
            ob = [psf() for _ in range(4)]
            for b in range(nbq):
                if smp:
                    for mt in range(2):
                        tr.dma("pool", vb[:, mt, :], cv[l, b, mt * 128:(mt + 1) * 128, :], W=[r_vb])
                    pm, r_pm = PM[b % 2]
                    if b < 2:
                        E("dve", lambda: V.memset(pm[:], 0.0), W=[r_pm])
                    else:
                        E("dve", lambda: V.memset(pm[:, :, (b - 2) * LS:(b - 1) * LS], 0.0), W=[r_pm])
                    E("dve", lambda: V.tensor_copy(out=pm[:, :, b * LS:(b + 1) * LS], in_=pT[:, :, b * LS:(b + 1) * LS]), R=[r_pT], W=[r_pm])
                else:
                    pm, r_pm = pT, r_pT
                for h in range(4):
                    pt, rp = ob[h]
                    for mt in range(2):
                        mm(pt[:, 0:512], pm[:, h * 2 + mt, :], vb[:, mt, h * 512:(h + 1) * 512], b == 0 and mt == 0, b == nbq - 1 and mt == 1, [r_pm, r_vb], rp)
            for h in range(4):
                pt, rp = ob[h]
                E("act", lambda: A.activation(out=otok[:, h * 512:(h + 1) * 512], in_=pt[:, 0:512], func=AF.Identity, scale=Em[:, 12 + h:13 + h]), R=[rp, r_Em], W=[r_otok])
            for c4 in range(4):
                pb_, rpb = psb()
                for i in range(4):
                    c = c4 * 4 + i
                    tp(pb_[:, i * 128:(i + 1) * 128], otok[:, c * 128:(c + 1) * 128], [r_otok], rpb)
                E("dve", lambda: V.tensor_copy(out=A2[:, c4 * 4:(c4 + 1) * 4, :], in_=pb_[:, 0:512].rearrange("p (a b) -> p a b", a=4)), R=[rpb], W=[r_A2])
            for nb in range(4):
                pt, rp = as_block("o", lambda k: A2[:, k, :], r_A2)
                E("dve", lambda: V.tensor_tensor(out=xres[:, nb * 512:(nb + 1) * 512], in0=xres[:, nb * 512:(nb + 1) * 512], in1=pt[:, 0:512], op=ALU.add), R=[rp, r_xres], W=[r_xres])

            if STOP <= 7:
                raise StopBuild()
            dump(1, xres, r_xres)
            rms_to_T(l, C_NF)
            for nb in range(22):
                if smp:
                    tr.dma("sp", sstf[:], sffn[l, :, nb * 4:(nb + 1) * 4, :, :], W=[r_sstf])

                def ev(j, pa, rp, nb=nb):
                    c = nb * 4 + j

                    def fin(ac, r_ac):
                        if c < 44:
                            E("act", lambda: A.activation(out=gTc(c)[0], in_=ac[:], func=AF.Silu), R=[r_ac], W=[gTc(c)[1]])
                        else:
                            E("dve", lambda: V.tensor_tensor(out=gTc(c - 44)[0], in0=gTc(c - 44)[0], in1=ac[:], op=ALU.mult), R=[r_ac, gTc(c - 44)[1]], W=[gTc(c - 44)[1]])
                    conv_chunk(pa, rp, c, 3, fhalo, r_fhalo, C_FW, C_FB, 88, (sstf, r_sstf, ssof, r_ssof, j), fin)
                ws_block("up", hk, r_hT, ev)
                if smp:
                    tr.dma("sp", ffn_s[l, :, nb * 4:(nb + 1) * 4, :, :], ssof[:], R=[r_ssof], key=r_ssof, is_out=True)
            if lastp:
                tr.dma("sp", ffn_p[l], fhalo[:], R=[r_fhalo], key=r_fhalo, is_out=True)
            for nb in range(4):
                ps = None
                for kb, (k0, kc) in enumerate(((0, 16), (16, 16), (32, 12))):
                    ps = as_block("dn", lambda k, k0=k0: gTc(k0 + k)[0], [r_szT, r_xsT], first=(kb == 0), last=(kb == 2), ps=ps)
                pt, rp = ps
                E("dve", lambda: V.tensor_tensor(out=xres[:, nb * 512:(nb + 1) * 512], in0=xres[:, nb * 512:(nb + 1) * 512], in1=pt[:, 0:512], op=ALU.add), R=[rp, r_xres], W=[r_xres])
            if l == 0:
                tr.dma("sp", xscr[ti], xres[:], R=[r_xres], W=[r_xscr[ti]], key=r_xres)
            else:
                E("dve", lambda: V.memset(st4[:, 0:1], 0.0), W=[r_st4])
                E("act", lambda: A.activation(out=junk[:], in_=xres[:], func=AF.Square, accum_out=st4[:, 0:1]), R=[r_xres, r_st4], W=[r_junk, r_st4])
                E("act", lambda: A.activation(out=st4[:, 2:3], in_=st4[:, 0:1], func=AF.Sqrt, scale=1.0 / D, bias=epsc[:, 0:1]), R=[r_st4, r_epsc], W=[r_st4])
                E("dve", lambda: V.reciprocal(out=st4[:, 1:2], in_=st4[:, 2:3]), R=[r_st4], W=[r_st4])
                tr.dma("sp", ufp[:], gfin_d, W=[r_ufp])
                E("dve", lambda: V.scalar_tensor_tensor(out=xres[:], in0=xres[:], scalar=st4[:, 1:2], in1=ufp[:], op0=ALU.mult, op1=ALU.mult), R=[r_xres, r_st4, r_ufp], W=[r_xres])
                dst = ys if smp else yp[ti * 128:(ti + 1) * 128, :]
                tr.dma("sp", dst, xres[:], R=[r_xres], key=r_xres, is_out=True)

        try:
            for l in range(NL):
                if STOP <= 0:
                    raise StopBuild()
                mem_kv(l)
                if STOP <= 1:
                    raise StopBuild()
                for ti in range(NPT + 1):
                    do_tile(l, ti)
        except StopBuild:
            pass
        tr.finish()
        print('nsem', tr.nsem, 'sbuf bytes', sbtot[0])
    return nc


def _bf(a):
    return np.ascontiguousarray(a.astype(ml_dtypes.bfloat16))


def _consts():
    idx = np.arange(128)
    masks = np.zeros((128, 9, 128), np.float32)
    for base, L in ((0, 128), (4, LS)):
        sq = idx // L
        same = sq[:, None] == sq[None, :]
        masks[:, base + 0, :] = same & (idx[:, None] > idx[None, :])
        masks[:, base + 1, :] = same & (idx[:, None] <= idx[None, :])
        masks[:, base + 2, :] = same & (idx[:, None] <= idx[None, :])
        masks[:, base + 3, :] = same
    masks[:, M_ID, :] = np.eye(128)
    bands = np.zeros((128, 24, 128), np.float32)
    s = idx[:, None]; t = idx[None, :]
    for g, w in enumerate((2, 4, 8, 16)):
        bands[:, g, :] = ((s <= t) & (s >= t - w + 1)) / w - (s == t)
        bands[:, 4 + g, :] = ((s - 128) >= (t - w + 1)) / w
        cnt = np.minimum(t + 1, w)
        bands[:, 8 + g, :] = ((s <= t) & (s >= t - w + 1)) / cnt - (s == t)
        bs, js = s // LS, s % LS
        bt, jt = t // LS, t % LS
        bands[:, 12 + g, :] = ((bs == bt) & (js <= jt) & (js >= jt - w + 1)) / w - (s == t)
        for i in range(2):
            r = np.arange(128)[:, None]
            rb, ri = r // 15 + 8 * i, r % 15
            bands[:, 16 + 2 * g + i, :] = ((r < 120) & (rb == bt) & (ri >= 16 + jt - w)) / w
    blkc = np.zeros((128, NSEQ, 128), np.float32)
    mcol = np.zeros((128, NSEQ), np.float32)
    for b in range(NSEQ):
        blkc[b * LS:(b + 1) * LS, b, :] = 1.0
        mcol[b * LS:(b + 1) * LS, b] = 1.0
    return _bf(masks), _bf(bands), _bf(blkc), mcol


def _col(v):
    v = np.asarray(v, np.float32)
    return v.reshape(-1, 128).T


_NC_CACHE = {}


def kernel(x_prompt, x_sample, state_pool, state_ssm_conv, state_ssm, state_ffn_conv, cache_mem_k, cache_mem_v,
           mem_prompt, norm_mix, w_in, w_pool_group, pool_scale, w_pool_out, ssm_conv_w, ssm_conv_b, ssm_dt_bias,
           ssm_a_log, ssm_d, ssm_norm, w_ssm_out, w_out, norm_mem_q, w_mem_q, w_mem_o, norm_mem_kv, w_mem_k,
           w_mem_v, norm_ffn, w_ffn_up, ffn_conv_w, ffn_conv_b, w_ffn_down, norm_final):
    f = lambda a: np.ascontiguousarray(np.asarray(a, dtype=np.float32))
    if "nc" not in _NC_CACHE:
        _NC_CACHE["nc"] = build_program()
    nc = _NC_CACHE["nc"]
    masks, bands, blkc, mcol = _consts()
    pcols = np.zeros((2, 128, NCOL), np.float32)
    prow = np.zeros((2, 128, 128), np.float32)
    for l in range(2):
        pcols[l, :, C_NMIX:C_NMIX + 16] = _col(norm_mix[l]); pcols[l, :, C_NQ:C_NQ + 16] = _col(norm_mem_q[l])
        pcols[l, :, C_NF:C_NF + 16] = _col(norm_ffn[l]); pcols[l, :, C_NKV:C_NKV + 16] = _col(norm_mem_kv[l])
        pcols[l, :, C_PSC:C_PSC + 16] = _col(pool_scale[l])
        for j in range(4):
            pcols[l, :, C_CW + j * 48:C_CW + (j + 1) * 48] = _col(ssm_conv_w[l, j])
        pcols[l, :, C_CB:C_CB + 48] = _col(ssm_conv_b[l]); pcols[l, :, C_SN:C_SN + 32] = _col(ssm_norm[l])
        for j in range(3):
            pcols[l, :, C_FW + j * 88:C_FW + (j + 1) * 88] = _col(ffn_conv_w[l, j])
        pcols[l, :, C_FB:C_FB + 88] = _col(ffn_conv_b[l])
        pcols[l, :, C_DC:C_DC + 32] = _col(np.repeat(np.asarray(ssm_d[l], np.float32), 64))
        prow[l, :, 0:64] = np.asarray(ssm_dt_bias[l], np.float32)[None, :]
        prow[l, :, 64:128] = np.asarray(ssm_a_log[l], np.float32)[None, :]
    gfin = np.ascontiguousarray(np.broadcast_to(np.asarray(norm_final, np.float32)[None, :], (128, D)))
    shared = {
        "w_in": f(w_in), "w_pg": f(w_pool_group), "w_po": f(w_pool_out), "w_so": f(w_ssm_out), "w_out": f(w_out),
        "w_q": f(w_mem_q), "w_o": f(w_mem_o), "w_k": f(w_mem_k), "w_v": f(w_mem_v), "w_up": f(w_ffn_up), "w_dn": f(w_ffn_down),
        "pcols": pcols, "prow": prow, "gfin": gfin, "masks": masks, "bands": bands, "blkc": blkc, "mcol": mcol,
    }
    x_prompt = np.asarray(x_prompt); x_sample = np.asarray(x_sample)
    in_maps = []
    for c in range(8):
        sq = c // 2
        bs = slice(NSEQ * c, NSEQ * (c + 1))
        m = dict(shared)
        m["xp"] = f(x_prompt[sq]); m["xs"] = f(x_sample[bs]).reshape(128, D)
        m["spool"] = f(np.asarray(state_pool)[:, bs]).reshape(2, 240, D)
        m["sconv"] = np.ascontiguousarray(f(np.asarray(state_ssm_conv)[:, bs]).reshape(2, NSEQ, 3, 48, 128).transpose(0, 4, 3, 1, 2))
        m["sffn"] = np.ascontiguousarray(f(np.asarray(state_ffn_conv)[:, bs]).reshape(2, NSEQ, 2, 88, 128).transpose(0, 4, 3, 1, 2))
        m["sssm"] = np.ascontiguousarray(f(np.asarray(state_ssm)[:, bs]).transpose(0, 1, 4, 2, 3).reshape(2, NSEQ, 128, 4096))
        m["ck"] = np.ascontiguousarray(f(np.asarray(cache_mem_k)[:, bs]).reshape(2, NSEQ, 256, 16, 128).transpose(0, 1, 4, 3, 2))
        m["cv"] = f(np.asarray(cache_mem_v)[:, bs]).reshape(2, NSEQ, 256, D)
        m["mem"] = f(np.asarray(mem_prompt)[sq])
        in_maps.append(m)
    res = run_bass_kernel_spmd(nc, in_maps, core_ids=list(range(8)))
    R = res.results
    DBG["r0"] = R[0]
    y_prompt = np.stack([R[2 * s]["yp"] for s in range(4)])
    y_sample = np.concatenate([R[c]["ys"].reshape(NSEQ, LS, D) for c in range(8)], axis=0)
    pool_pr = np.stack([R[2 * s]["pool_p"] for s in range(4)], axis=1)
    pool_sm = np.concatenate([R[c]["pool_s"] for c in range(8)], axis=1)
    conv_pr = np.stack([R[2 * s]["conv_p"].transpose(0, 3, 2, 1).reshape(2, 3, 6144) for s in range(4)], axis=1)
    conv_sm = np.concatenate([R[c]["conv_s"].transpose(0, 3, 4, 2, 1).reshape(2, NSEQ, 3, 6144) for c in range(8)], axis=1)
    ssm_pr = np.stack([R[2 * s]["ssm_p"].transpose(0, 2, 1).reshape(2, 64, 64, 128) for s in range(4)], axis=1)
    ssm_sm = np.concatenate([R[c]["ssm_s"].transpose(0, 1, 3, 2).reshape(2, NSEQ, 64, 64, 128) for c in range(8)], axis=1)
    ffn_pr = np.stack([R[2 * s]["ffn_p"].transpose(0, 3, 2, 1).reshape(2, 2, 2 * FFN) for s in range(4)], axis=1)
    ffn_sm = np.concatenate([R[c]["ffn_s"].transpose(0, 3, 4, 2, 1).reshape(2, NSEQ, 2, 2 * FFN) for c in range(8)], axis=1)
    mk = np.stack([R[2 * s]["mk_p"].reshape(2, 256, 4, 512) for s in range(4)], axis=1)
    mv = np.stack([R[2 * s]["mv_p"].reshape(2, 256, 4, 512) for s in range(4)], axis=1)
    outs = (y_prompt, y_sample, pool_pr, pool_sm, conv_pr, conv_sm, ssm_pr, ssm_sm, ffn_pr, ffn_sm, mk, mv)
    return tuple(np.ascontiguousarray(o, dtype=np.float32) for o in outs)
```

```python
import numpy as np
import ml_dtypes
from contextlib import ExitStack
import concourse.bass as bass
import concourse.mybir as mybir
from concourse.bass_utils import run_bass_kernel_spmd

F32 = mybir.dt.float32
BF16 = mybir.dt.bfloat16
AF = mybir.ActivationFunctionType
ALU = mybir.AluOpType
AX = mybir.AxisListType

D = 2048
NPT = 16
NL = 2
STOP = 99
DEBUG = False
DBG = {}


class StopBuild(Exception):
    pass
NSEQ = 16
LS = 8
EPS = 1e-6
NIN = 16448
FFN = 5632
C_NMIX, C_NQ, C_NF, C_NKV, C_PSC, C_CW, C_CB, C_SN, C_FW, C_FB, C_DC, NCOL = 0, 16, 32, 48, 64, 80, 272, 320, 352, 616, 704, 736
M_UP, M_LP, M_CP, M_ONE, M_US, M_LS, M_CS, M_BLK, M_ID = range(9)
EPOCH = 30000


class Res:
    __slots__ = ("name", "w", "r", "dsem", "dcount", "excl")

    def __init__(self, name):
        self.name = name
        self.excl = name.startswith("pf") or name.startswith("pb")
        self.w = None
        self.r = {}
        self.dsem = None
        self.dcount = 0


class Eng:
    def __init__(self, tr, name, h):
        self.tr, self.name, self.h = tr, name, h
        self.sem = None
        self.count = 0
        self.known = {}

    def next_event(self):
        if self.sem is None or self.count >= EPOCH:
            self.sem = self.tr.new_sem()
            self.count = 0
        self.count += 1
        return (self.sem, self.count)


class Tracker:
    def __init__(self, nc, es):
        self.nc, self.es = nc, es
        self.nsem = 0
        self.E = {
            "pe": Eng(self, "pe", nc.tensor),
            "act": Eng(self, "act", nc.scalar),
            "dve": Eng(self, "dve", nc.vector),
            "pool": Eng(self, "pool", nc.gpsimd),
            "sp": Eng(self, "sp", nc.sync),
        }
        self.out_events = {}

    def new_sem(self):
        self.nsem += 1
        return self.es.enter_context(self.nc.semaphore("s%d" % self.nsem))

    def _wait(self, eng, R, W):
        evs = []
        for r in R:
            if r.w is not None:
                evs.append((r.w, True))
        for w in W:
            if w.w is not None:
                evs.append((w.w, False))
            for ev in w.r.values():
                evs.append((ev, False))
        for (sem, val), raw in evs:
            if sem is eng.sem and (eng.name == "pe" or not raw):
                continue
            k = id(sem)
            if eng.known.get(k, 0) >= val:
                continue
            eng.h.wait_ge(sem, val)
            eng.known[k] = val

    def emit(self, e, fn, R=(), W=()):
        eng = self.E[e]
        if e != "pe":
            W = list(W) + [r for r in R if r.excl and r not in W]
        self._wait(eng, R, W)
        ins = fn()
        ev = eng.next_event()
        ins.then_inc(ev[0], 1)
        for w in W:
            w.w = ev
            w.r = {}
        for r in R:
            if r not in W:
                r.r[id(ev[0])] = ev
        return ins

    def dma(self, q, out, in_, R=(), W=(), key=None, is_out=False):
        eng = self.E[q]
        res = key if key is not None else (W[0] if W else R[0])
        if res.dsem is not None and res.w is not None and res.w[0] is res.dsem and res in W:
            eng.known[id(res.dsem)] = max(eng.known.get(id(res.dsem), 0), res.w[1])
        self._wait(eng, R, W)
        if res.dsem is None or res.dcount + 16 > EPOCH:
            res.dsem = self.new_sem()
            res.dcount = 0
        ins = eng.h.dma_start(out=out, in_=in_)
        ins.then_inc(res.dsem, 16)
        res.dcount += 16
        ev = (res.dsem, res.dcount)
        for w in W:
            w.w = ev
            w.r = {}
        for r in R:
            if r not in W:
                r.r[id(ev[0])] = ev
        if is_out:
            self.out_events[id(res.dsem)] = ev

    def finish(self):
        sp = self.E["sp"]
        for sem, val in self.out_events.values():
            sp.h.wait_ge(sem, val)


def build_program():
    nc = bass.Bass("TRN2", target_bir_lowering=False)

    def din(name, shape, dt=F32):
        return nc.dram_tensor(name, list(shape), dt, kind="ExternalInput").ap()

    def dout(name, shape):
        return nc.dram_tensor(name, list(shape), F32, kind="ExternalOutput").ap()

    xp = din("xp", [2048, D]); xs = din("xs", [128, D])
    spool = din("spool", [2, 240, D]); sconv = din("sconv", [2, 128, 48, NSEQ, 3])
    sffn = din("sffn", [2, 128, 88, NSEQ, 2]); sssm = din("sssm", [2, NSEQ, 128, 4096])
    ck = din("ck", [2, NSEQ, 128, 16, 256]); cv = din("cv", [2, NSEQ, 256, D])
    mem = din("mem", [256, D])
    w_in = din("w_in", [2, D, NIN]); w_pg = din("w_pg", [2, 4, 512, 512]); w_po = din("w_po", [2, D, D])
    w_so = din("w_so", [2, 4096, D]); w_out = din("w_out", [2, D, D]); w_q = din("w_q", [2, D, D])
    w_o = din("w_o", [2, D, D]); w_k = din("w_k", [2, D, D]); w_v = din("w_v", [2, D, D])
    w_up = din("w_up", [2, D, 2 * FFN]); w_dn = din("w_dn", [2, FFN, D])
    pcols_d = din("pcols", [2, 128, NCOL]); prow_d = din("prow", [2, 128, 128]); gfin_d = din("gfin", [128, D])
    masks_d = din("masks", [128, 9, 128], BF16); bands_d = din("bands", [128, 24, 128], BF16)
    blkc_d = din("blkc", [128, NSEQ, 128], BF16); mcol_d = din("mcol", [128, NSEQ])

    yp = dout("yp", [2048, D]); ys = dout("ys", [128, D])
    pool_p = dout("pool_p", [2, 15, D]); pool_s = dout("pool_s", [2, NSEQ, 15, D])
    conv_p = dout("conv_p", [2, 128, 48, 3]); conv_s = dout("conv_s", [2, 128, 48, NSEQ, 3])
    ssm_p = dout("ssm_p", [2, 128, 4096]); ssm_s = dout("ssm_s", [2, NSEQ, 128, 4096])
    ffn_p = dout("ffn_p", [2, 128, 88, 2]); ffn_s = dout("ffn_s", [2, 128, 88, NSEQ, 2])
    mk_p = dout("mk_p", [2, 256, D]); mv_p = dout("mv_p", [2, 256, D])
    xscr = nc.dram_tensor("xscr", [NPT + 1, 128, D], F32, kind="Internal").ap()
    if DEBUG:
        dbg = dout("dbg", [4, 128, D])
        dbgT = dout("dbgT", [6, 128, 32, 128])

    with ExitStack() as es:
        tr = Tracker(nc, es)

        sbtot = [0]

        def sb(name, shape, dt=BF16):
            sbtot[0] += int(np.prod(shape[1:])) * (2 if dt == BF16 else 4)
            return es.enter_context(nc.sbuf_tensor("sb_" + name, list(shape), dt)), Res(name)

        masks, r_masks = sb("masks", [128, 9, 128]); bands, r_bands = sb("bands", [128, 24, 128])
        blkc, r_blkc = sb("blkc", [128, NSEQ, 128]); mcol, r_mcol = sb("mcol", [128, NSEQ], F32)
        pcols, r_pcols = sb("pcols", [128, 2, NCOL], F32); prow, r_prow = sb("prow", [128, 2, 128], F32)
        arow, r_arow = sb("arow", [128, 2, 64], F32)
        epsc, r_epsc = sb("epsc", [128, 1], F32)
        WSL = [sb("wslot%d" % i, [128, 16, 512]) for i in range(2)]
        xres, r_xres = sb("xres", [128, D], F32)
        xn, r_xn = sb("xn", [128, D])
        junk, r_junk = xn, r_xn
        st4, r_st4 = sb("st4", [128, 8], F32)
        hT, r_hT = sb("hT", [128, 16, 128])
        A1, r_A1 = sb("A1", [128, 16, 128]); A2, r_A2 = sb("A2", [128, 16, 128])
        opT, r_opT = sb("opT", [128, 16, 128]); osT, r_osT = A1, r_A1
        utok = [sb("utok%d" % i, [128, D]) for i in range(2)]
        ufp, r_ufp = sb("ufp", [128, D], F32)
        szT, r_szT = sb("szT", [128, 32, 128]); xsT, r_xsT = sb("xsT", [128, 32, 128])
        BCT, r_BCT = sb("BCT", [128, 16, 128])
        STG = [sb("stg%d" % i, [128, 176], F32) for i in range(3)]
        ACC = [sb("acc%d" % i, [128, 128], F32) for i in range(3)]
        chalo, r_chalo = sb("chalo", [128, 48, 3], F32); fhalo, r_fhalo = sb("fhalo", [128, 88, 2], F32)
        sst, r_sst = sb("sst", [128, 4, NSEQ, 3], F32)
        sso, r_sso = sb("sso", [128, 4, NSEQ, 3], F32)
        dtt, r_dtt = sb("dtt", [128, 8, 64], F32)
        lah, r_lah = sb("lah", [128, 2, 64])
        lahf, r_lahf = sb("lahf", [128, 2, 64], F32)
        Btok, r_Btok = sb("Btok", [128, 1024]); Bmsk, r_Bmsk = sb("Bmsk", [128, 1024])
        CBm, r_CBm = sb("CBm", [128, 8, 128], F32)
        xdA, r_xdA = sb("xdA", [128, 4096])
        xdwg, r_xdwg = sb("xdwg", [128, 512])
        yacc, r_yacc = sb("yacc", [128, 4096])
        spl, r_spl = yacc[:, :].rearrange("p (i d) -> p i d", i=2), r_yacc
        LSG = [sb("lsg%d" % i, [128, 2, 128]) for i in range(3)]
        DEC = [sb("dec%d" % i, [128, 128], F32) for i in range(3)]
        MTB = [sb("mtb%d" % i, [128, 128]) for i in range(3)]
        yo_t, r_yo_t = sb("yo_t", [128, 512])
        ytg, r_ytg = sb("ytg", [128, 512])
        yz, r_yz = sb("yz", [128, 4, 128], F32); sq, r_sq = sb("sq", [128, 4, 128])
        rsb, r_rsb = sb("rsb", [128, 128], F32)
        S, r_S = sb("S", [128, 4096], F32); Sb, r_Sb = sb("Sb", [128, 4096])
        Em, r_Em = sb("Em", [128, 64], F32)
        kT, r_kT = sb("kT", [128, 16, 256]); vb, r_vb = sb("vb", [128, 2, D])
        kfp, r_kfp = ufp[:, 0:512], r_ufp
        QM = [(opT, r_opT), (A2, r_A2)]
        PM = [(hT[:, 0:8, :], r_hT), (hT[:, 8:16, :], r_hT)]
        gTc = lambda c: (szT[:, c, :], r_szT) if c < 32 else (xsT[:, c - 32, :], r_xsT)
        pp, r_pp = sb("pp", [128, 4, 256]); pT, r_pT = sb("pT", [128, 8, 128])
        otok, r_otok = xn, r_xn
        sstf, r_sstf = sb("sstf", [128, 4, NSEQ, 2], F32); ssof, r_ssof = sb("ssof", [128, 4, NSEQ, 2], F32)
        PF = [(es.enter_context(nc.psum_tensor("pf%d" % i, [128, 512], F32)), Res("pf%d" % i)) for i in range(6)]
        PB = [(es.enter_context(nc.psum_tensor("pb%d" % i, [128, 1024], BF16)), Res("pb%d" % i)) for i in range(2)]
        rot = {"pf": 0, "pb": 0, "stg": 0, "acc": 0, "lsg": 0, "dec": 0, "mtb": 0}

        def nxt(lst, key):
            i = rot[key]
            rot[key] = (i + 1) % len(lst)
            return lst[i]

        PY = PF.pop()
        psf = lambda: nxt(PF, "pf")
        psb = lambda: nxt(PB, "pb")
        r_din = Res("din")
        r_xscr = [Res("xscr%d" % i) for i in range(NPT + 1)]

        E = tr.emit
        V, A, P = nc.vector, nc.scalar, nc.tensor
        ACOPY = lambda out, in_: A.activation(out=out, in_=in_, func=AF.Identity)

        tr.dma("pool", masks[:], masks_d, W=[r_masks]); tr.dma("pool", bands[:], bands_d, W=[r_bands])
        tr.dma("pool", blkc[:], blkc_d, W=[r_blkc]); tr.dma("pool", mcol[:], mcol_d, W=[r_mcol])
        for l in range(2):
            tr.dma("pool", pcols[:, l, :], pcols_d[l], W=[r_pcols])
            tr.dma("pool", prow[:, l, :], prow_d[l], W=[r_prow])
        for l in range(2):
            E("act", lambda: A.activation(out=arow[:, l, :], in_=prow[:, l, 64:128], func=AF.Exp), R=[r_prow], W=[r_arow])
            E("dve", lambda: V.tensor_scalar(out=arow[:, l, :], in0=arow[:, l, :], scalar1=-1.0, scalar2=None, op0=ALU.mult), R=[r_arow], W=[r_arow])
        ident = masks[:, M_ID, :]
        E("dve", lambda: V.memset(epsc[:], EPS), W=[r_epsc])

        WB = {}
        for nm, wap in (("w_in", w_in), ("w_pg", w_pg), ("w_po", w_po), ("w_so", w_so), ("w_out", w_out), ("w_q", w_q),
                        ("w_o", w_o), ("w_k", w_k), ("w_v", w_v), ("w_up", w_up), ("w_dn", w_dn)):
            shp = list(wap.shape)
            wb = nc.dram_tensor("wb_" + nm, shp, BF16, kind="Internal").ap()
            WB[nm] = (wap, wb, [Res("wb_%s_%d" % (nm, l)) for l in range(2)])

        def convert_layer(l):
            for nm in ("w_k", "w_v", "w_in", "w_pg", "w_po", "w_so", "w_out", "w_q", "w_o", "w_up", "w_dn"):
                wap, wb, rr = WB[nm]
                srcs = [(wap[l, g], wb[l, g]) for g in range(4)] if nm == "w_pg" else [(wap[l], wb[l])]
                for sa, da in srcs:
                    K_ = sa.shape[0]
                    for c in range(K_ // 128):
                        tr.dma("pool", da[c * 128:(c + 1) * 128, :], sa[c * 128:(c + 1) * 128, :], W=[rr[l]])

        for l in range(NL):
            convert_layer(l)

        def layer_blocks(l):
            b = []
            wi = ("w_in", l, None)
            for nb in range(4): b.append(("u", wi, 0, 16, nb * 512, 512))
            for g in range(4): b.append(("pg", ("w_pg", l, g), 0, 4, 0, 512))
            for nb in range(4): b.append(("po", ("w_po", l, None), 0, 16, nb * 512, 512))
            for nb in range(8): b.append(("z", wi, 0, 16, 2048 + nb * 512, 512))
            for nb in range(12): b.append(("xbc", wi, 0, 16, 6144 + nb * 512, 512))
            b.append(("dt", wi, 0, 16, 12288, 64))
            for nb in range(4):
                for kb in range(2): b.append(("so", ("w_so", l, None), kb * 16, 16, nb * 512, 512))
            for nb in range(4): b.append(("gp", wi, 0, 16, 12352 + nb * 512, 512))
            for nb in range(4): b.append(("gs", wi, 0, 16, 14400 + nb * 512, 512))
            for nb in range(4): b.append(("wo", ("w_out", l, None), 0, 16, nb * 512, 512))
            for nb in range(4): b.append(("q", ("w_q", l, None), 0, 16, nb * 512, 512))
            for nb in range(4): b.append(("o", ("w_o", l, None), 0, 16, nb * 512, 512))
            for nb in range(22): b.append(("up", ("w_up", l, None), 0, 16, nb * 512, 512))
            for nb in range(4):
                for (k0, kc) in ((0, 16), (16, 16), (32, 12)): b.append(("dn", ("w_dn", l, None), k0, kc, nb * 512, 512))
            return b

        wseq = []
        for l in range(NL):
            for nb in range(4): wseq.append(("k", ("w_k", l, None), 0, 16, nb * 512, 512))
            for nb in range(4): wseq.append(("v", ("w_v", l, None), 0, 16, nb * 512, 512))
            lb = layer_blocks(l)
            for t in range(NPT + 1):
                wseq.extend(lb)
        wstate = {"issued": 0, "next": 0}

        def wissue():
            i = wstate["issued"]
            tag, (wnm, wl, wg), k0, kc, n0, nw = wseq[i]
            slot, rs = WSL[i % 2]
            _, wb, rr = WB[wnm]
            wap = wb[wl] if wg is None else wb[wl, wg]
            src = wap[k0 * 128:(k0 + kc) * 128, n0:n0 + nw].rearrange("(kc p) n -> p kc n", p=128)
            tr.dma("sp", slot[:, 0:kc, 0:nw], src, R=[rr[wl]], W=[rs], key=rs)
            wstate["issued"] = i + 1

        def wprefetch():
            while wstate["issued"] < min(len(wseq), wstate["next"] + 1):
                wissue()

        def wnext(tag, pf=True):
            i = wstate["next"]
            assert wseq[i][0] == tag, (wseq[i][0], tag)
            while wstate["issued"] < min(len(wseq), i + (2 if pf else 1)):
                wissue()
            wstate["next"] = i + 1
            slot, rs = WSL[i % 2]
            return slot, rs, wseq[i][3], wseq[i][5]

        def mm(ps, lhsT, rhs, st, sp_, R, Wr):
            E("pe", lambda: P.matmul(ps, lhsT, rhs, start=st, stop=sp_), R=R, W=[Wr])

        def tp(ps, in_, R, Wr, idn=None):
            E("pe", lambda: P.transpose(ps, in_, ident if idn is None else idn), R=R + [r_masks], W=[Wr])

        def ws_block(tag, rhs_of_k, r_act, evac, first=True, last=True, ps=None):
            slot, rs, kc, nw = wnext(tag)
            if ps is None:
                ps = psf()
            pt, rp = ps
            for j in range(nw // 128):
                for k in range(kc):
                    mm(pt[:, j * 128:(j + 1) * 128], slot[:, k, j * 128:(j + 1) * 128], rhs_of_k(k),
                       first and k == 0, last and k == kc - 1, [rs] + (r_act if isinstance(r_act, list) else [r_act]), rp)
            if last:
                for j in range(nw // 128):
                    evac(j, pt[:, j * 128:(j + 1) * 128], rp)
            return ps

        def as_block(tag, lhsT_of_k, r_act, first=True, last=True, ps=None, M=128):
            slot, rs, kc, nw = wnext(tag)
            if ps is None:
                ps = psf()
            pt, rp = ps
            for k in range(kc):
                mm(pt[0:M, 0:nw], lhsT_of_k(k), slot[:, k, 0:nw], first and k == 0, last and k == kc - 1, [rs] + (r_act if isinstance(r_act, list) else [r_act]), rp)
            return ps

        def pc(l, off, c=None):
            return pcols[:, l, off:off + 1] if c is None else pcols[:, l, off + c:off + c + 1]

        def rms_to_T(l, goff, src=None, r_src=None, dstT=None, r_dstT=None, ncols=128, coff=0):
            src = xres if src is None else src
            r_src = r_xres if r_src is None else r_src
            dstT = hT if dstT is None else dstT
            r_dstT = r_hT if r_dstT is None else r_dstT
            E("dve", lambda: V.memset(st4[:, 0:1], 0.0), W=[r_st4])
            E("act", lambda: A.activation(out=junk[:], in_=src[:], func=AF.Square, accum_out=st4[:, 0:1]), R=[r_src, r_st4], W=[r_junk, r_st4])
            E("act", lambda: A.activation(out=st4[:, 2:3], in_=st4[:, 0:1], func=AF.Sqrt, scale=1.0 / D, bias=epsc[:, 0:1]), R=[r_st4, r_epsc], W=[r_st4])
            E("dve", lambda: V.reciprocal(out=st4[:, 1:2], in_=st4[:, 2:3]), R=[r_st4], W=[r_st4])
            E("dve", lambda: V.tensor_scalar(out=xn[:], in0=src[:], scalar1=st4[:, 1:2], scalar2=None, op0=ALU.mult), R=[r_src, r_st4], W=[r_xn])
            for c4 in range(4):
                pt, rp = psb()
                for i in range(4):
                    c = c4 * 4 + i
                    tp(pt[:, i * 128:(i + 1) * 128], xn[:, c * 128:(c + 1) * 128], [r_xn], rp)
                for i in range(4):
                    c = c4 * 4 + i
                    E("act", lambda: A.activation(out=dstT[:, c, coff:coff + 128], in_=pt[:, i * 128:(i + 1) * 128], func=AF.Identity, scale=pc(l, goff, c)),
                      R=[rp, r_pcols], W=[r_dstT])

        def mem_kv(l):
            mnT, r_mnT = kT, r_kT
            for mt in range(2):
                tr.dma("pool", xres[:], mem[mt * 128:(mt + 1) * 128, :], W=[r_xres])
                rms_to_T(l, C_NKV, dstT=(A1 if mt == 0 else A2), r_dstT=(r_A1 if mt == 0 else r_A2))
                if STOP <= 0.2:
                    raise StopBuild()
            mn = [(A1, r_A1), (A2, r_A2)]
            for which, tag, outd in (("k", "k", mk_p), ("v", "v", mv_p)):
                if which == "v" and STOP <= 0.7:
                    raise StopBuild()
                for nb in range(4):
                    slot, rs, kc, nw = wnext(tag)
                    for mt in range(2):
                        pt, rp = psf()
                        for k in range(kc):
                            mm(pt[:, 0:512], mn[mt][0][:, k, :], slot[:, k, 0:512], k == 0, k == kc - 1, [rs, mn[mt][1]], rp)
                        E("dve", lambda: V.tensor_copy(out=kfp[:], in_=pt[:, 0:512]), R=[rp], W=[r_kfp])
                        tr.dma("pool", outd[l, mt * 128:(mt + 1) * 128, nb * 512:(nb + 1) * 512], kfp[:], R=[r_kfp], key=r_kfp, is_out=True)
                        if STOP <= 0.5:
                            raise StopBuild()
                        if which == "v":
                            E("act", lambda: ACOPY(out=vb[:, mt, nb * 512:(nb + 1) * 512], in_=pt[:, 0:512]), R=[rp], W=[r_vb])
                        else:
                            E("act", lambda: ACOPY(out=otok[:, 0:512], in_=pt[:, 0:512]), R=[rp], W=[r_otok])
                            pb_, rpb = psb()
                            for i in range(4):
                                tp(pb_[:, i * 128:(i + 1) * 128], otok[:, i * 128:(i + 1) * 128], [r_otok], rpb)
                            for i in range(4):
                                E("dve", lambda: V.tensor_copy(out=kT[:, nb * 4 + i, mt * 128:(mt + 1) * 128], in_=pb_[:, i * 128:(i + 1) * 128]), R=[rpb], W=[r_kT])
                            if STOP <= 0.6:
                                raise StopBuild()

        def do_tile(l, ti):
            smp = ti == NPT
            first = ti == 0
            lastp = ti == NPT - 1
            NB = NSEQ if smp else 1
            L = LS if smp else 128
            mU, mL, mC, mB = (M_US, M_LS, M_CS, M_BLK) if smp else (M_UP, M_LP, M_CP, M_ONE)
            def dump(i, buf, r_buf):
                if DEBUG and smp and l == 0:
                    tr.dma("pool", dbg[i], buf[:], R=[r_buf], key=r_buf, is_out=True)

            def dumpT(i, buf, r_buf, n=16):
                if DEBUG and smp and l == 0:
                    tr.dma("pool", dbgT[i, :, 0:n, :], buf[:, 0:n, :], R=[r_buf], key=r_buf, is_out=True)
            if l == 0:
                src = xs if smp else xp[ti * 128:(ti + 1) * 128, :]
                tr.dma("pool", xres[:], src, W=[r_xres])
            else:
                tr.dma("pool", xres[:], xscr[ti], R=[r_xscr[ti]], W=[r_xres], key=r_xres)
            rms_to_T(l, C_NMIX)
            hk = lambda k: hT[:, k, :]
            ut, r_ut = utok[ti % 2]
            up_, r_up = utok[(ti + 1) % 2]
            need_fp = smp or lastp
            for nb in range(4):
                pt, rp = as_block("u", hk, r_hT)
                E("act", lambda: ACOPY(out=ut[:, nb * 512:(nb + 1) * 512], in_=pt[:, 0:512]), R=[rp], W=[r_ut])
                if need_fp:
                    E("dve", lambda: V.tensor_copy(out=ufp[:, nb * 512:(nb + 1) * 512], in_=pt[:, 0:512]), R=[rp], W=[r_ufp])
            if lastp:
                tr.dma("pool", pool_p[l], ufp[113:128, :], R=[r_ufp], key=r_ufp, is_out=True)
            if smp:
                for b in range(NSEQ):
                    tr.dma("pool", pool_s[l, b, 7:15, :], ufp[b * LS:(b + 1) * LS, :], R=[r_ufp], key=r_ufp, is_out=True)
                tr.dma("pool", pool_s[l][:, 0:7, :], spool[l].rearrange("(b r) d -> b r d", r=15)[:, 8:15, :], key=r_ufp, R=[r_ufp], is_out=True)
                for i in range(2):
                    tr.dma("pool", spl[0:120, i, :], spool[l, i * 120:(i + 1) * 120, :], W=[r_spl])
            if STOP <= 2:
                raise StopBuild()
            for g in range(4):
                pt, rp = psf()
                for i in range(4):
                    c = g * 4 + i
                    o_ = pt[:, i * 128:(i + 1) * 128]
                    if smp:
                        mm(o_, ut[:, c * 128:(c + 1) * 128], bands[:, 12 + g, :], True, False, [r_ut, r_bands], rp)
                        mm(o_, spl[0:120, 0, c * 128:(c + 1) * 128], bands[0:120, 16 + 2 * g, :], False, False, [r_spl, r_bands], rp)
                        mm(o_, spl[0:120, 1, c * 128:(c + 1) * 128], bands[0:120, 17 + 2 * g, :], False, True, [r_spl, r_bands], rp)
                    elif first:
                        mm(o_, ut[:, c * 128:(c + 1) * 128], bands[:, 8 + g, :], True, True, [r_ut, r_bands], rp)
                    else:
                        mm(o_, ut[:, c * 128:(c + 1) * 128], bands[:, g, :], True, False, [r_ut, r_bands], rp)
                        mm(o_, up_[:, c * 128:(c + 1) * 128], bands[:, 4 + g, :], False, True, [r_up, r_bands], rp)
                E("dve", lambda: V.tensor_copy(out=A1[:, g * 4:(g + 1) * 4, :], in_=pt[:, 0:512].rearrange("p (a b) -> p a b", a=4)), R=[rp], W=[r_A1])
            for g in range(4):
                def ev(j, pa, rp, g=g):
                    E("act", lambda: A.activation(out=A2[:, g * 4 + j, :], in_=pa, func=AF.Identity, scale=pc(l, C_PSC, g * 4 + j)), R=[rp, r_pcols], W=[r_A2])
                ws_block("pg", lambda k, g=g: A1[:, g * 4 + k, :], r_A1, ev)
            for nb in range(4):
                def ev(j, pa, rp, nb=nb):
                    E("dve", lambda: V.tensor_copy(out=opT[:, nb * 4 + j, :], in_=pa), R=[rp], W=[r_opT])
                ws_block("po", lambda k: A2[:, k, :], r_A2, ev)
            if STOP <= 3:
                raise StopBuild()
            dumpT(0, opT, r_opT)
            for nb in range(8):
                def ev(j, pa, rp, nb=nb):
                    E("act", lambda: A.activation(out=szT[:, nb * 4 + j, :], in_=pa, func=AF.Silu), R=[rp], W=[r_szT])
                ws_block("z", hk, r_hT, ev)

            def conv_chunk(pa, rp, c, K, halo, r_halo, woff, boff, nW, sblk, finish):
                H = K - 1
                (sg, r_sg), (ac, r_ac) = nxt(STG, "stg"), nxt(ACC, "acc")
                W_ = H + L
                sv = sg[:, 0:NB * W_].rearrange("p (b w) -> p b w", b=NB)
                E("act", lambda: A.activation(out=sv[:, :, H:H + L], in_=pa.rearrange("p (b j) -> p b j", b=NB), func=AF.Identity), R=[rp], W=[r_sg])
                if smp:
                    hin, r_hin, hout, r_hout, ci = sblk
                    E("dve", lambda: V.tensor_copy(out=sv[:, :, 0:H], in_=hin[:, ci, :, 0:H]), R=[r_hin], W=[r_sg])
                    E("dve", lambda: V.tensor_copy(out=hout[:, ci, :, 0:H], in_=sv[:, :, L:L + H]), R=[r_sg], W=[r_hout])
                else:
                    if first:
                        E("dve", lambda: V.memset(sv[:, :, 0:H], 0.0), W=[r_sg])
                    else:
                        E("dve", lambda: V.tensor_copy(out=sv[:, 0, 0:H], in_=halo[:, c, :]), R=[r_halo], W=[r_sg])
                    E("dve", lambda: V.tensor_copy(out=halo[:, c, :], in_=sv[:, 0, L:L + H]), R=[r_sg], W=[r_halo])
                av = ac[:, :].rearrange("p (b j) -> p b j", b=NB)
                E("act", lambda: A.activation(out=av, in_=sv[:, :, 0:L], func=AF.Identity, scale=pc(l, woff, c), bias=pc(l, boff, c)),
                  R=[r_sg, r_pcols], W=[r_ac])
                for j in range(1, K):
                    E("dve", lambda: V.scalar_tensor_tensor(out=av, in0=sv[:, :, j:j + L], scalar=pc(l, woff, j * nW + c), in1=av, op0=ALU.mult, op1=ALU.add),
                      R=[r_sg, r_pcols, r_ac], W=[r_ac])
                finish(ac, r_ac)

            for nb in range(12):
                if smp:
                    tr.dma("pool", sst[:], sconv[l, :, nb * 4:(nb + 1) * 4, :, :], W=[r_sst])

                def ev(j, pa, rp, nb=nb):
                    c = nb * 4 + j

                    def fin(ac, r_ac):
                        dst = xsT[:, c, :] if c < 32 else BCT[:, c - 32, :]
                        rd = r_xsT if c < 32 else r_BCT
                        E("act", lambda: A.activation(out=dst, in_=ac[:], func=AF.Silu), R=[r_ac], W=[rd])
                    conv_chunk(pa, rp, c, 4, chalo, r_chalo, C_CW, C_CB, 48, (sst, r_sst, sso, r_sso, j), fin)
                ws_block("xbc", hk, r_hT, ev)
                if smp:
                    tr.dma("pool", conv_s[l, :, nb * 4:(nb + 1) * 4, :, :], sso[:], R=[r_sso], key=r_sso, is_out=True)
            if lastp:
                tr.dma("pool", conv_p[l], chalo[:], R=[r_chalo], key=r_chalo, is_out=True)
            pt, rp = as_block("dt", hk, r_hT)
            dt_, la_, E_, cs_, te_, dtw_, cd_, tmp_ = [dtt[:, i, :] for i in range(8)]
            E("dve", lambda: V.tensor_tensor(out=tmp_, in0=pt[:, 0:64], in1=prow[:, l, 0:64], op=ALU.add), R=[rp, r_prow], W=[r_dtt])
            E("act", lambda: A.activation(out=tmp_, in_=tmp_, func=AF.Exp), R=[r_dtt], W=[r_dtt])
            E("act", lambda: A.activation(out=dt_, in_=tmp_, func=AF.Ln, bias=1.0, scale=1.0), R=[r_dtt], W=[r_dtt])
            E("dve", lambda: V.tensor_tensor(out=la_, in0=dt_, in1=arow[:, l, :], op=ALU.mult), R=[r_dtt, r_arow], W=[r_dtt])
            E("dve", lambda: V.tensor_copy(out=lah[:, 0, :], in_=la_), R=[r_dtt], W=[r_lah])
            E("dve", lambda: V.tensor_tensor(out=lah[:, 1, :], in0=la_, in1=lah[:, 0, :], op=ALU.subtract), R=[r_dtt, r_lah], W=[r_lah])
            E("dve", lambda: V.tensor_copy(out=lahf[:], in_=lah[:]), R=[r_lah], W=[r_lahf])
            pt, rp = psf()
            for i in range(2):
                mm(pt[:, 0:64], masks[:, mL, :], lah[:, i, :], i == 0, i == 1, [r_masks, r_lah], rp)
            for i in range(2):
                mm(pt[:, 64:128], masks[:, mB, :], lah[:, i, :], i == 0, i == 1, [r_masks, r_lah], rp)
            E("act", lambda: A.activation(out=E_, in_=pt[:, 0:64], func=AF.Exp), R=[rp], W=[r_dtt])
            E("dve", lambda: V.tensor_copy(out=cs_, in_=pt[:, 0:64]), R=[rp], W=[r_dtt])
            E("dve", lambda: V.tensor_tensor(out=tmp_, in0=pt[:, 64:128], in1=cs_, op=ALU.subtract), R=[rp, r_dtt], W=[r_dtt])
            E("act", lambda: A.activation(out=te_, in_=tmp_, func=AF.Exp), R=[r_dtt], W=[r_dtt])
            E("act", lambda: A.activation(out=cd_, in_=pt[:, 64:128], func=AF.Exp), R=[rp], W=[r_dtt])
            E("dve", lambda: V.tensor_tensor(out=dtw_, in0=dt_, in1=te_, op=ALU.mult), R=[r_dtt], W=[r_dtt])
            if STOP <= 4:
                raise StopBuild()
            pb_, rpb = psb()
            for g in range(8):
                tp(pb_[:, g * 128:(g + 1) * 128], BCT[:, g, :], [r_BCT], rpb)
            E("dve", lambda: V.tensor_copy(out=Btok[:], in_=pb_[:, 0:1024]), R=[rpb], W=[r_Btok])
            for g2 in range(2):
                pt, rp = psf()
                for i in range(4):
                    g = g2 * 4 + i
                    mm(pt[:, i * 128:(i + 1) * 128], BCT[:, g, :], BCT[:, 8 + g, :], True, True, [r_BCT], rp)
                for i in range(4):
                    g = g2 * 4 + i
                    E("dve", lambda: V.tensor_tensor(out=CBm[:, g, :], in0=pt[:, i * 128:(i + 1) * 128], in1=masks[:, mC, :], op=ALU.mult), R=[rp, r_masks], W=[r_CBm])
            for g in range(8):
                pb_, rpb = psb()
                for i in range(4):
                    tp(pb_[:, i * 128:(i + 1) * 128], xsT[:, g * 4 + i, :], [r_xsT], rpb)
                pv = pb_[:, 0:512].rearrange("p (h d) -> p h d", h=8)
                E("dve", lambda: V.tensor_tensor(out=xdA[:, g * 512:(g + 1) * 512].rearrange("p (h d) -> p h d", h=8), in0=pv,
                                                 in1=dtt[:, 0, g * 8:(g + 1) * 8].unsqueeze(2).broadcast_to([128, 8, 64]), op=ALU.mult), R=[rpb, r_dtt], W=[r_xdA])
            def make_xdw(g):
                E("dve", lambda: V.tensor_tensor(out=xdwg[:].rearrange("p (h d) -> p h d", h=8), in0=xdA[:, g * 512:(g + 1) * 512].rearrange("p (h d) -> p h d", h=8),
                                                 in1=dtt[:, 4, g * 8:(g + 1) * 8].unsqueeze(2).broadcast_to([128, 8, 64]), op=ALU.mult), R=[r_xdA, r_dtt], W=[r_xdwg])
            have_off = smp or not first
            if smp:
                for b in range(NSEQ):
                    tr.dma("pool", S[:], sssm[l, b], W=[r_S])
                    E("act", lambda: ACOPY(out=Sb[:], in_=S[:]), R=[r_S], W=[r_Sb])
                    E("dve", lambda: V.tensor_scalar(out=Em[:], in0=E_, scalar1=mcol[:, b:b + 1], scalar2=None, op0=ALU.mult), R=[r_dtt, r_mcol], W=[r_Em])
                    E("dve", lambda: V.tensor_scalar(out=Bmsk[:], in0=Btok[:], scalar1=mcol[:, b:b + 1], scalar2=None, op0=ALU.mult), R=[r_Btok, r_mcol], W=[r_Bmsk])
                    pt, rp = psf()
                    for i in range(2):
                        mm(pt[:, 0:64], blkc[:, b, :], lah[:, i, :], i == 0, i == 1, [r_blkc, r_lah], rp)
                    E("act", lambda: A.activation(out=cd_, in_=pt[:, 0:64], func=AF.Exp), R=[rp], W=[r_dtt])
                    for g in range(8):
                        gs_ = slice(g * 512, (g + 1) * 512)
                        pt, rp = psf()
                        mm(pt[:, 0:512], BCT[:, 8 + g, :], Sb[:, gs_], True, True, [r_BCT, r_Sb], rp)
                        yv = yacc[:, gs_].rearrange("p (h d) -> p h d", h=8)
                        emb = Em[:, g * 8:(g + 1) * 8].unsqueeze(2).broadcast_to([128, 8, 64])
                        pv = pt[:, 0:512].rearrange("p (h d) -> p h d", h=8)
                        if b == 0:
                            E("dve", lambda: V.tensor_tensor(out=yv, in0=pv, in1=emb, op=ALU.mult), R=[rp, r_Em], W=[r_yacc])
                        else:
                            E("dve", lambda: V.tensor_tensor(out=yo_t[:].rearrange("p (h d) -> p h d", h=8), in0=pv, in1=emb, op=ALU.mult), R=[rp, r_Em], W=[r_yo_t])
                            E("dve", lambda: V.tensor_tensor(out=yacc[:, gs_], in0=yacc[:, gs_], in1=yo_t[:], op=ALU.add), R=[r_yo_t, r_yacc], W=[r_yacc])
                        pt2, rp2 = psf()
                        make_xdw(g)
                        mm(pt2[:, 0:512], Bmsk[:, g * 128:(g + 1) * 128], xdwg[:], True, True, [r_Bmsk, r_xdwg], rp2)
                        E("dve", lambda: V.tensor_tensor(out=S[:, gs_].rearrange("p (h d) -> p h d", h=8), in0=S[:, gs_].rearrange("p (h d) -> p h d", h=8),
                                                         in1=dtt[:, 6, g * 8:(g + 1) * 8].unsqueeze(2).broadcast_to([128, 8, 64]), op=ALU.mult), R=[r_S, r_dtt, r_Sb], W=[r_S])
                        E("dve", lambda: V.tensor_tensor(out=S[:, gs_], in0=S[:, gs_], in1=pt2[:, 0:512], op=ALU.add), R=[r_S, rp2], W=[r_S])
                    tr.dma("pool", ssm_s[l, b], S[:], R=[r_S], key=r_S, is_out=True)
            else:
                if first:
                    E("dve", lambda: V.memset(S[:], 0.0), W=[r_S])
                for g in range(8):
                    gs_ = slice(g * 512, (g + 1) * 512)
                    if have_off:
                        pt, rp = psf()
                        mm(pt[:, 0:512], BCT[:, 8 + g, :], Sb[:, gs_], True, True, [r_BCT, r_Sb], rp)
                        E("dve", lambda: V.tensor_tensor(out=yacc[:, gs_].rearrange("p (h d) -> p h d", h=8), in0=pt[:, 0:512].rearrange("p (h d) -> p h d", h=8),
                                                         in1=dtt[:, 2, g * 8:(g + 1) * 8].unsqueeze(2).broadcast_to([128, 8, 64]), op=ALU.mult), R=[rp, r_dtt], W=[r_yacc])
                for g in range(8):
                    gs_ = slice(g * 512, (g + 1) * 512)
                    pt2, rp2 = psf()
                    make_xdw(g)
                    mm(pt2[:, 0:512], Btok[:, g * 128:(g + 1) * 128], xdwg[:], True, True, [r_Btok, r_xdwg], rp2)
                    E("dve", lambda: V.tensor_tensor(out=S[:, gs_].rearrange("p (h d) -> p h d", h=8), in0=S[:, gs_].rearrange("p (h d) -> p h d", h=8),
                                                     in1=dtt[:, 6, g * 8:(g + 1) * 8].unsqueeze(2).broadcast_to([128, 8, 64]), op=ALU.mult), R=[r_S, r_dtt, r_Sb], W=[r_S])
                    E("dve", lambda: V.tensor_tensor(out=S[:, gs_], in0=S[:, gs_], in1=pt2[:, 0:512], op=ALU.add), R=[r_S, rp2], W=[r_S])
                E("act", lambda: ACOPY(out=Sb[:], in_=S[:]), R=[r_S, r_yacc], W=[r_Sb])
                if lastp:
                    tr.dma("pool", ssm_p[l], S[:], R=[r_S], key=r_S, is_out=True)
            for g in range(8):
                gs_ = slice(g * 512, (g + 1) * 512)
                py, rpy = PY
                for hh in range(8):
                    h = g * 8 + hh
                    (lg, r_lg), (dc, r_dc), (mt_, r_mt) = nxt(LSG, "lsg"), nxt(DEC, "dec"), nxt(MTB, "mtb")
                    for i in range(2):
                        E("act", lambda: A.activation(out=lg[:, i, :], in_=masks[:, mU, :], func=AF.Identity, scale=lahf[:, i, h:h + 1]), R=[r_masks, r_lahf], W=[r_lg])
                    pg_, rpg = psf()
                    for i in range(2):
                        mm(pg_[:, 0:128], lg[:, i, :], masks[:, mL, :], i == 0, i == 1, [r_lg, r_masks], rpg)
                    E("act", lambda: A.activation(out=dc[:], in_=pg_[:, 0:128], func=AF.Exp), R=[rpg], W=[r_dc])
                    E("dve", lambda: V.tensor_tensor(out=mt_[:], in0=CBm[:, g, :], in1=dc[:], op=ALU.mult), R=[r_CBm, r_dc], W=[r_mt])
                    mm(py[:, hh * 64:(hh + 1) * 64], mt_[:], xdA[:, h * 64:(h + 1) * 64], True, True, [r_mt, r_xdA], rpy)
                if have_off:
                    E("dve", lambda: V.tensor_tensor(out=ytg[:], in0=py[:, 0:512], in1=yacc[:, gs_], op=ALU.add), R=[rpy, r_yacc], W=[r_ytg])
                else:
                    E("dve", lambda: V.tensor_copy(out=ytg[:], in_=py[:, 0:512]), R=[rpy], W=[r_ytg])
                pb_, rpb = psb()
                for i in range(4):
                    tp(pb_[:, i * 128:(i + 1) * 128], ytg[:, i * 128:(i + 1) * 128], [r_ytg], rpb)
                for i in range(4):
                    c = g * 4 + i
                    E("dve", lambda: V.scalar_tensor_tensor(out=yz[:, i, :], in0=xsT[:, c, :], scalar=pc(l, C_DC, c), in1=pb_[:, i * 128:(i + 1) * 128], op0=ALU.mult, op1=ALU.add),
                      R=[r_xsT, r_pcols, rpb], W=[r_yz])
                    E("dve", lambda: V.tensor_tensor(out=yz[:, i, :], in0=yz[:, i, :], in1=szT[:, c, :], op=ALU.mult), R=[r_yz, r_szT], W=[r_yz])
                E("act", lambda: A.activation(out=sq[:], in_=yz[:], func=AF.Square), R=[r_yz], W=[r_sq])
                pm_, rpm = psf()
                for i in range(4):
                    mm(pm_[:, 0:128], masks[:, M_ONE, :], sq[:, i, :], i == 0, i == 3, [r_masks, r_sq], rpm)
                E("act", lambda: A.activation(out=rsb[:], in_=pm_[:, 0:128], func=AF.Sqrt, scale=1.0 / 512, bias=epsc[:, 0:1]), R=[rpm, r_epsc], W=[r_rsb])
                E("dve", lambda: V.reciprocal(out=rsb[:], in_=rsb[:]), R=[r_rsb], W=[r_rsb])
                for i in range(4):
                    c = g * 4 + i
                    E("dve", lambda: V.scalar_tensor_tensor(out=szT[:, c, :], in0=yz[:, i, :], scalar=pc(l, C_SN, c), in1=rsb[:], op0=ALU.mult, op1=ALU.mult),
                      R=[r_yz, r_pcols, r_rsb], W=[r_szT])
            if STOP <= 5:
                raise StopBuild()
            dumpT(1, szT, r_szT, 32)
            for nb in range(4):
                def ev(j, pa, rp, nb=nb):
                    E("dve", lambda: V.tensor_copy(out=osT[:, nb * 4 + j, :], in_=pa), R=[rp], W=[r_osT])
                slA, rsA, _, _ = wnext("so")
                slB, rsB, _, _ = wnext("so", pf=False)
                pt, rp = psf()
                for j in range(4):
                    for k in range(16):
                        mm(pt[:, j * 128:(j + 1) * 128], slA[:, k, j * 128:(j + 1) * 128], szT[:, k, :], k == 0, False, [rsA, r_szT], rp)
                    for k in range(16):
                        mm(pt[:, j * 128:(j + 1) * 128], slB[:, k, j * 128:(j + 1) * 128], szT[:, 16 + k, :], False, k == 15, [rsB, r_szT], rp)
                wprefetch()
                for j in range(4):
                    ev(j, pt[:, j * 128:(j + 1) * 128], rp)
            dumpT(2, osT, r_osT)
            for nb in range(4):
                def ev(j, pa, rp, nb=nb):
                    (ac, r_ac) = nxt(ACC, "acc")
                    E("act", lambda: A.activation(out=ac[:], in_=pa, func=AF.Sigmoid), R=[rp], W=[r_ac])
                    E("dve", lambda: V.tensor_tensor(out=opT[:, nb * 4 + j, :], in0=opT[:, nb * 4 + j, :], in1=ac[:], op=ALU.mult), R=[r_ac, r_opT], W=[r_opT])
                ws_block("gp", hk, r_hT, ev)
            for nb in range(4):
                def ev(j, pa, rp, nb=nb):
                    (ac, r_ac) = nxt(ACC, "acc")
                    E("act", lambda: A.activation(out=ac[:], in_=pa, func=AF.Sigmoid), R=[rp], W=[r_ac])
                    E("dve", lambda: V.tensor_tensor(out=ac[:], in0=osT[:, nb * 4 + j, :], in1=ac[:], op=ALU.mult), R=[r_ac, r_osT], W=[r_ac])
                    E("dve", lambda: V.tensor_tensor(out=opT[:, nb * 4 + j, :], in0=opT[:, nb * 4 + j, :], in1=ac[:], op=ALU.add), R=[r_ac, r_opT], W=[r_opT])
                ws_block("gs", hk, r_hT, ev)
            dumpT(3, opT, r_opT)
            for nb in range(4):
                pt, rp = as_block("wo", lambda k: opT[:, k, :], r_opT)
                E("dve", lambda: V.tensor_tensor(out=xres[:, nb * 512:(nb + 1) * 512], in0=xres[:, nb * 512:(nb + 1) * 512], in1=pt[:, 0:512], op=ALU.add), R=[rp, r_xres], W=[r_xres])

            if STOP <= 6:
                raise StopBuild()
            dump(0, xres, r_xres)
            rms_to_T(l, C_NQ)
            for nb in range(4):
                def ev(j, pa, rp, nb=nb):
                    E("act", lambda: ACOPY(out=A1[:, nb * 4 + j, :], in_=pa), R=[rp], W=[r_A1])
                ws_block("q", hk, r_hT, ev)
            scs = [psf() for _ in range(4)]
            nbq = NSEQ if smp else 1
            for b in range(nbq):
                if smp:
                    for c4 in range(4):
                        tr.dma("pool", kT[:, c4 * 4:(c4 + 1) * 4, :], ck[l, b, :, c4 * 4:(c4 + 1) * 4, :], W=[r_kT])
                    qm, r_qm = QM[b % 2]
                    if b < 2:
                        E("dve", lambda: V.memset(qm[:], 0.0), W=[r_qm])
                    else:
                        E("dve", lambda: V.memset(qm[:, :, (b - 2) * LS:(b - 1) * LS], 0.0), W=[r_qm])
                    E("dve", lambda: V.tensor_copy(out=qm[:, :, b * LS:(b + 1) * LS], in_=A1[:, :, b * LS:(b + 1) * LS]), R=[r_A1], W=[r_qm])
                else:
                    qm, r_qm = A1, r_A1
                for h in range(4):
                    pt, rp = scs[h]
                    o_ = pt[:, 0:256]
                    for i in range(4):
                        mm(o_, qm[:, h * 4 + i, :], kT[:, h * 4 + i, :], b == 0 and i == 0, b == nbq - 1 and i == 3, [r_qm, r_kT], rp)
            scale = 1.0 / float(np.sqrt(512.0))
            mx, nmx, ssum, rs_ = st4[:, 4:5], st4[:, 5:6], st4[:, 6:7], st4[:, 7:8]
            for h in range(4):
                pt, rp = scs[h]
                o_ = pt[:, 0:256]
                E("dve", lambda: V.tensor_reduce(out=Em[:, h:h + 1], in_=o_, axis=AX.X, op=ALU.max), R=[rp], W=[r_Em])
                E("dve", lambda: V.tensor_scalar(out=Em[:, 4 + h:5 + h], in0=Em[:, h:h + 1], scalar1=-scale, scalar2=None, op0=ALU.mult), R=[r_Em], W=[r_Em])
                E("dve", lambda: V.memset(Em[:, 8 + h:9 + h], 0.0), W=[r_Em])
                E("act", lambda: A.activation(out=pp[:, h, :], in_=o_, func=AF.Exp, bias=Em[:, 4 + h:5 + h], scale=scale, accum_out=Em[:, 8 + h:9 + h]), R=[rp, r_Em], W=[r_pp, r_Em])
                E("dve", lambda: V.reciprocal(out=Em[:, 12 + h:13 + h], in_=Em[:, 8 + h:9 + h]), R=[r_Em], W=[r_Em])
            pb_, rpb = psb()
            for h in range(4):
                for mt in range(2):
                    i = h * 2 + mt
                    tp(pb_[:, i * 128:(i + 1) * 128], pp[:, h, mt * 128:(mt + 1) * 128], [r_pp], rpb)
            E("dve", lambda: V.tensor_copy(out=pT[:], in_=pb_[:, 0:1024].rearrange("p (a b) -> p a b", a=8)), R=[rpb], W=[r_pT])
            ob = [psf() for _ in range(4)]
            for b in range(nbq):
                if smp:
                    for mt in range(2):
                        tr.dma("pool", vb[:, mt, :], cv[l, b, mt * 128:(mt + 1) * 128, :], W=[r_vb])
                    pm, r_pm = PM[b % 2]
                    if b < 2:
                        E("dve", lambda: V.memset(pm[:], 0.0), W=[r_pm])
                    else:
                        E("dve", lambda: V.memset(pm[:, :, (b - 2) * LS:(b - 1) * LS], 0.0), W=[r_pm])
                    E("dve", lambda: V.tensor_copy(out=pm[:, :, b * LS:(b + 1) * LS], in_=pT[:, :, b * LS:(b + 1) * LS]), R=[r_pT], W=[r_pm])
                else:
                    pm, r_pm = pT, r_pT
                for h in range(4):
                    pt, rp = ob[h]
                    for mt in range(2):
                        mm(pt[:, 0:512], pm[:, h * 2 + mt, :], vb[:, mt, h * 512:(h + 1) * 512], b == 0 and mt == 0, b == nbq - 1 and mt == 1, [r_pm, r_vb], rp)
            for h in range(4):
                pt, rp = ob[h]
                E("act", lambda: A.activation(out=otok[:, h * 512:(h + 1) * 512], in_=pt[:, 0:512], func=AF.Identity, scale=Em[:, 12 + h:13 + h]), R=[rp, r_Em], W=[r_otok])
            for c4 in range(4):
                pb_, rpb = psb()
                for i in range(4):
                    c = c4 * 4 + i
                    tp(pb_[:, i * 128:(i + 1) * 128], otok[:, c * 128:(c + 1) * 128], [r_otok], rpb)
                E("dve", lambda: V.tensor_copy(out=A2[:, c4 * 4:(c4 + 1) * 4, :], in_=pb_[:, 0:512].rearrange("p (a b) -> p a b", a=4)), R=[rpb], W=[r_A2])
            for nb in range(4):
                pt, rp = as_block("o", lambda k: A2[:, k, :], r_A2)
                E("dve", lambda: V.tensor_tensor(out=xres[:, nb * 512:(nb + 1) * 512], in0=xres[:, nb * 512:(nb + 1) * 512], in1=pt[:, 0:512], op=ALU.add), R=[rp, r_xres], W=[r_xres])

            if STOP <= 7:
                raise StopBuild()
            dump(1, xres, r_xres)
            rms_to_T(l, C_NF)
            for nb in range(22):
                if smp:
                    tr.dma("pool", sstf[:], sffn[l, :, nb * 4:(nb + 1) * 4, :, :], W=[r_sstf])

                def ev(j, pa, rp, nb=nb):
                    c = nb * 4 + j

                    def fin(ac, r_ac):
                        if c < 44:
                            E("act", lambda: A.activation(out=gTc(c)[0], in_=ac[:], func=AF.Silu), R=[r_ac], W=[gTc(c)[1]])
                        else:
                            E("dve", lambda: V.tensor_tensor(out=gTc(c - 44)[0], in0=gTc(c - 44)[0], in1=ac[:], op=ALU.mult), R=[r_ac, gTc(c - 44)[1]], W=[gTc(c - 44)[1]])
                    conv_chunk(pa, rp, c, 3, fhalo, r_fhalo, C_FW, C_FB, 88, (sstf, r_sstf, ssof, r_ssof, j), fin)
                ws_block("up", hk, r_hT, ev)
                if smp:
                    tr.dma("pool", ffn_s[l, :, nb * 4:(nb + 1) * 4, :, :], ssof[:], R=[r_ssof], key=r_ssof, is_out=True)
            if lastp:
                tr.dma("pool", ffn_p[l], fhalo[:], R=[r_fhalo], key=r_fhalo, is_out=True)
            for nb in range(4):
                ps = None
                for kb, (k0, kc) in enumerate(((0, 16), (16, 16), (32, 12))):
                    ps = as_block("dn", lambda k, k0=k0: gTc(k0 + k)[0], [r_szT, r_xsT], first=(kb == 0), last=(kb == 2), ps=ps)
                pt, rp = ps
                E("dve", lambda: V.tensor_tensor(out=xres[:, nb * 512:(nb + 1) * 512], in0=xres[:, nb * 512:(nb + 1) * 512], in1=pt[:, 0:512], op=ALU.add), R=[rp, r_xres], W=[r_xres])
            if l == 0:
                tr.dma("pool", xscr[ti], xres[:], R=[r_xres], W=[r_xscr[ti]], key=r_xres)
            else:
                E("dve", lambda: V.memset(st4[:, 0:1], 0.0), W=[r_st4])
                E("act", lambda: A.activation(out=junk[:], in_=xres[:], func=AF.Square, accum_out=st4[:, 0:1]), R=[r_xres, r_st4], W=[r_junk, r_st4])
                E("act", lambda: A.activation(out=st4[:, 2:3], in_=st4[:, 0:1], func=AF.Sqrt, scale=1.0 / D, bias=epsc[:, 0:1]), R=[r_st4, r_epsc], W=[r_st4])
                E("dve", lambda: V.reciprocal(out=st4[:, 1:2], in_=st4[:, 2:3]), R=[r_st4], W=[r_st4])
                tr.dma("pool", ufp[:], gfin_d, W=[r_ufp])
                E("dve", lambda: V.scalar_tensor_tensor(out=xres[:], in0=xres[:], scalar=st4[:, 1:2], in1=ufp[:], op0=ALU.mult, op1=ALU.mult), R=[r_xres, r_st4, r_ufp], W=[r_xres])
                dst = ys if smp else yp[ti * 128:(ti + 1) * 128, :]
                tr.dma("pool", dst, xres[:], R=[r_xres], key=r_xres, is_out=True)

        try:
            for l in range(NL):
                if STOP <= 0:
                    raise StopBuild()
                mem_kv(l)
                if STOP <= 1:
                    raise StopBuild()
                for ti in range(NPT + 1):
                    do_tile(l, ti)
        except StopBuild:
            pass
        tr.finish()
        print('nsem', tr.nsem, 'sbuf bytes', sbtot[0])
    return nc


def _bf(a):
    return np.ascontiguousarray(a.astype(ml_dtypes.bfloat16))


def _consts():
    idx = np.arange(128)
    masks = np.zeros((128, 9, 128), np.float32)
    for base, L in ((0, 128), (4, LS)):
        sq = idx // L
        same = sq[:, None] == sq[None, :]
        masks[:, base + 0, :] = same & (idx[:, None] > idx[None, :])
        masks[:, base + 1, :] = same & (idx[:, None] <= idx[None, :])
        masks[:, base + 2, :] = same & (idx[:, None] <= idx[None, :])
        masks[:, base + 3, :] = same
    masks[:, M_ID, :] = np.eye(128)
    bands = np.zeros((128, 24, 128), np.float32)
    s = idx[:, None]; t = idx[None, :]
    for g, w in enumerate((2, 4, 8, 16)):
        bands[:, g, :] = ((s <= t) & (s >= t - w + 1)) / w - (s == t)
        bands[:, 4 + g, :] = ((s - 128) >= (t - w + 1)) / w
        cnt = np.minimum(t + 1, w)
        bands[:, 8 + g, :] = ((s <= t) & (s >= t - w + 1)) / cnt - (s == t)
        bs, js = s // LS, s % LS
        bt, jt = t // LS, t % LS
        bands[:, 12 + g, :] = ((bs == bt) & (js <= jt) & (js >= jt - w + 1)) / w - (s == t)
        for i in range(2):
            r = np.arange(128)[:, None]
            rb, ri = r // 15 + 8 * i, r % 15
            bands[:, 16 + 2 * g + i, :] = ((r < 120) & (rb == bt) & (ri >= 16 + jt - w)) / w
    blkc = np.zeros((128, NSEQ, 128), np.float32)
    mcol = np.zeros((128, NSEQ), np.float32)
    for b in range(NSEQ):
        blkc[b * LS:(b + 1) * LS, b, :] = 1.0
        mcol[b * LS:(b + 1) * LS, b] = 1.0
    return _bf(masks), _bf(bands), _bf(blkc), mcol


def _col(v):
    v = np.asarray(v, np.float32)
    return v.reshape(-1, 128).T


_NC_CACHE = {}


def kernel(x_prompt, x_sample, state_pool, state_ssm_conv, state_ssm, state_ffn_conv, cache_mem_k, cache_mem_v,
           mem_prompt, norm_mix, w_in, w_pool_group, pool_scale, w_pool_out, ssm_conv_w, ssm_conv_b, ssm_dt_bias,
           ssm_a_log, ssm_d, ssm_norm, w_ssm_out, w_out, norm_mem_q, w_mem_q, w_mem_o, norm_mem_kv, w_mem_k,
           w_mem_v, norm_ffn, w_ffn_up, ffn_conv_w, ffn_conv_b, w_ffn_down, norm_final):
    f = lambda a: np.ascontiguousarray(np.asarray(a, dtype=np.float32))
    if "nc" not in _NC_CACHE:
        _NC_CACHE["nc"] = build_program()
    nc = _NC_CACHE["nc"]
    masks, bands, blkc, mcol = _consts()
    pcols = np.zeros((2, 128, NCOL), np.float32)
    prow = np.zeros((2, 128, 128), np.float32)
    for l in range(2):
        pcols[l, :, C_NMIX:C_NMIX + 16] = _col(norm_mix[l]); pcols[l, :, C_NQ:C_NQ + 16] = _col(norm_mem_q[l])
        pcols[l, :, C_NF:C_NF + 16] = _col(norm_ffn[l]); pcols[l, :, C_NKV:C_NKV + 16] = _col(norm_mem_kv[l])
        pcols[l, :, C_PSC:C_PSC + 16] = _col(pool_scale[l])
        for j in range(4):
            pcols[l, :, C_CW + j * 48:C_CW + (j + 1) * 48] = _col(ssm_conv_w[l, j])
        pcols[l, :, C_CB:C_CB + 48] = _col(ssm_conv_b[l]); pcols[l, :, C_SN:C_SN + 32] = _col(ssm_norm[l])
        for j in range(3):
            pcols[l, :, C_FW + j * 88:C_FW + (j + 1) * 88] = _col(ffn_conv_w[l, j])
        pcols[l, :, C_FB:C_FB + 88] = _col(ffn_conv_b[l])
        pcols[l, :, C_DC:C_DC + 32] = _col(np.repeat(np.asarray(ssm_d[l], np.float32), 64))
        prow[l, :, 0:64] = np.asarray(ssm_dt_bias[l], np.float32)[None, :]
        prow[l, :, 64:128] = np.asarray(ssm_a_log[l], np.float32)[None, :]
    gfin = np.ascontiguousarray(np.broadcast_to(np.asarray(norm_final, np.float32)[None, :], (128, D)))
    shared = {
        "w_in": f(w_in), "w_pg": f(w_pool_group), "w_po": f(w_pool_out), "w_so": f(w_ssm_out), "w_out": f(w_out),
        "w_q": f(w_mem_q), "w_o": f(w_mem_o), "w_k": f(w_mem_k), "w_v": f(w_mem_v), "w_up": f(w_ffn_up), "w_dn": f(w_ffn_down),
        "pcols": pcols, "prow": prow, "gfin": gfin, "masks": masks, "bands": bands, "blkc": blkc, "mcol": mcol,
    }
    x_prompt = np.asarray(x_prompt); x_sample = np.asarray(x_sample)
    in_maps = []
    for c in range(8):
        sq = c // 2
        bs = slice(NSEQ * c, NSEQ * (c + 1))
        m = dict(shared)
        m["xp"] = f(x_prompt[sq]); m["xs"] = f(x_sample[bs]).reshape(128, D)
        m["spool"] = f(np.asarray(state_pool)[:, bs]).reshape(2, 240, D)
        m["sconv"] = np.ascontiguousarray(f(np.asarray(state_ssm_conv)[:, bs]).reshape(2, NSEQ, 3, 48, 128).transpose(0, 4, 3, 1, 2))
        m["sffn"] = np.ascontiguousarray(f(np.asarray(state_ffn_conv)[:, bs]).reshape(2, NSEQ, 2, 88, 128).transpose(0, 4, 3, 1, 2))
        m["sssm"] = np.ascontiguousarray(f(np.asarray(state_ssm)[:, bs]).transpose(0, 1, 4, 2, 3).reshape(2, NSEQ, 128, 4096))
        m["ck"] = np.ascontiguousarray(f(np.asarray(cache_mem_k)[:, bs]).reshape(2, NSEQ, 256, 16, 128).transpose(0, 1, 4, 3, 2))
        m["cv"] = f(np.asarray(cache_mem_v)[:, bs]).reshape(2, NSEQ, 256, D)
        m["mem"] = f(np.asarray(mem_prompt)[sq])
        in_maps.append(m)
    res = run_bass_kernel_spmd(nc, in_maps, core_ids=list(range(8)))
    R = res.results
    DBG["r0"] = R[0]
    y_prompt = np.stack([R[2 * s]["yp"] for s in range(4)])
    y_sample = np.concatenate([R[c]["ys"].reshape(NSEQ, LS, D) for c in range(8)], axis=0)
    pool_pr = np.stack([R[2 * s]["pool_p"] for s in range(4)], axis=1)
    pool_sm = np.concatenate([R[c]["pool_s"] for c in range(8)], axis=1)
    conv_pr = np.stack([R[2 * s]["conv_p"].transpose(0, 3, 2, 1).reshape(2, 3, 6144) for s in range(4)], axis=1)
    conv_sm = np.concatenate([R[c]["conv_s"].transpose(0, 3, 4, 2, 1).reshape(2, NSEQ, 3, 6144) for c in range(8)], axis=1)
    ssm_pr = np.stack([R[2 * s]["ssm_p"].transpose(0, 2, 1).reshape(2, 64, 64, 128) for s in range(4)], axis=1)
    ssm_sm = np.concatenate([R[c]["ssm_s"].transpose(0, 1, 3, 2).reshape(2, NSEQ, 64, 64, 128) for c in range(8)], axis=1)
    ffn_pr = np.stack([R[2 * s]["ffn_p"].transpose(0, 3, 2, 1).reshape(2, 2, 2 * FFN) for s in range(4)], axis=1)
    ffn_sm = np.concatenate([R[c]["ffn_s"].transpose(0, 3, 4, 2, 1).reshape(2, NSEQ, 2, 2 * FFN) for c in range(8)], axis=1)
    mk = np.stack([R[2 * s]["mk_p"].reshape(2, 256, 4, 512) for s in range(4)], axis=1)
    mv = np.stack([R[2 * s]["mv_p"].reshape(2, 256, 4, 512) for s in range(4)], axis=1)
    outs = (y_prompt, y_sample, pool_pr, pool_sm, conv_pr, conv_sm, ssm_pr, ssm_sm, ffn_pr, ffn_sm, mk, mv)
    return tuple(np.ascontiguousarray(o, dtype=np.float32) for o in outs)
```

```python
import numpy as np
import ml_dtypes
from contextlib import ExitStack
import concourse.bass as bass
import concourse.mybir as mybir
from concourse.bass_utils import run_bass_kernel_spmd

F32 = mybir.dt.float32
BF16 = mybir.dt.bfloat16
AF = mybir.ActivationFunctionType
ALU = mybir.AluOpType
AX = mybir.AxisListType

D = 2048
NPT = 16
NL = 2
STOP = 99
DEBUG = False
DBG = {}


class StopBuild(Exception):
    pass
NSEQ = 16
LS = 8
EPS = 1e-6
NIN = 16448
FFN = 5632
C_NMIX, C_NQ, C_NF, C_NKV, C_PSC, C_CW, C_CB, C_SN, C_FW, C_FB, C_DC, NCOL = 0, 16, 32, 48, 64, 80, 272, 320, 352, 616, 704, 736
M_UP, M_LP, M_CP, M_ONE, M_US, M_LS, M_CS, M_BLK, M_ID = range(9)
EPOCH = 30000


class Res:
    __slots__ = ("name", "w", "r", "dsem", "dcount", "excl")

    def __init__(self, name):
        self.name = name
        self.excl = name.startswith("pf") or name.startswith("pb")
        self.w = None
        self.r = {}
        self.dsem = None
        self.dcount = 0


class Eng:
    def __init__(self, tr, name, h):
        self.tr, self.name, self.h = tr, name, h
        self.sem = None
        self.count = 0
        self.known = {}

    def next_event(self):
        if self.sem is None or self.count >= EPOCH:
            self.sem = self.tr.new_sem()
            self.count = 0
        self.count += 1
        return (self.sem, self.count)


class Tracker:
    def __init__(self, nc, es):
        self.nc, self.es = nc, es
        self.nsem = 0
        self.E = {
            "pe": Eng(self, "pe", nc.tensor),
            "act": Eng(self, "act", nc.scalar),
            "dve": Eng(self, "dve", nc.vector),
            "pool": Eng(self, "pool", nc.gpsimd),
            "sp": Eng(self, "sp", nc.sync),
        }
        self.out_events = {}

    def new_sem(self):
        self.nsem += 1
        return self.es.enter_context(self.nc.semaphore("s%d" % self.nsem))

    def _wait(self, eng, R, W):
        evs = []
        for r in R:
            if r.w is not None:
                evs.append((r.w, True))
        for w in W:
            if w.w is not None:
                evs.append((w.w, False))
            for ev in w.r.values():
                evs.append((ev, False))
        for (sem, val), raw in evs:
            if sem is eng.sem and (eng.name == "pe" or not raw):
                continue
            k = id(sem)
            if eng.known.get(k, 0) >= val:
                continue
            eng.h.wait_ge(sem, val)
            eng.known[k] = val

    def emit(self, e, fn, R=(), W=()):
        eng = self.E[e]
        if e != "pe":
            W = list(W) + [r for r in R if r.excl and r not in W]
        self._wait(eng, R, W)
        ins = fn()
        ev = eng.next_event()
        ins.then_inc(ev[0], 1)
        for w in W:
            w.w = ev
            w.r = {}
        for r in R:
            if r not in W:
                r.r[id(ev[0])] = ev
        return ins

    def dma(self, q, out, in_, R=(), W=(), key=None, is_out=False):
        eng = self.E[q]
        res = key if key is not None else (W[0] if W else R[0])
        if res.dsem is not None and res.w is not None and res.w[0] is res.dsem and res in W:
            eng.known[id(res.dsem)] = max(eng.known.get(id(res.dsem), 0), res.w[1])
        self._wait(eng, R, W)
        if res.dsem is None or res.dcount + 16 > EPOCH:
            res.dsem = self.new_sem()
            res.dcount = 0
        ins = eng.h.dma_start(out=out, in_=in_)
        ins.then_inc(res.dsem, 16)
        res.dcount += 16
        ev = (res.dsem, res.dcount)
        for w in W:
            w.w = ev
            w.r = {}
        for r in R:
            if r not in W:
                r.r[id(ev[0])] = ev
        if is_out:
            self.out_events[id(res.dsem)] = ev

    def finish(self):
        sp = self.E["sp"]
        for sem, val in self.out_events.values():
            sp.h.wait_ge(sem, val)


def build_program():
    nc = bass.Bass("TRN2", target_bir_lowering=False)

    def din(name, shape, dt=F32):
        return nc.dram_tensor(name, list(shape), dt, kind="ExternalInput").ap()

    def dout(name, shape):
        return nc.dram_tensor(name, list(shape), F32, kind="ExternalOutput").ap()

    xp = din("xp", [2048, D]); xs = din("xs", [128, D])
    spool = din("spool", [2, 240, D]); sconv = din("sconv", [2, 128, 48, NSEQ, 3])
    sffn = din("sffn", [2, 128, 88, NSEQ, 2]); sssm = din("sssm", [2, NSEQ, 128, 4096])
    ck = din("ck", [2, NSEQ, 128, 16, 256]); cv = din("cv", [2, NSEQ, 256, D])
    mem = din("mem", [256, D])
    w_in = din("w_in", [2, D, NIN]); w_pg = din("w_pg", [2, 4, 512, 512]); w_po = din("w_po", [2, D, D])
    w_so = din("w_so", [2, 4096, D]); w_out = din("w_out", [2, D, D]); w_q = din("w_q", [2, D, D])
    w_o = din("w_o", [2, D, D]); w_k = din("w_k", [2, D, D]); w_v = din("w_v", [2, D, D])
    w_up = din("w_up", [2, D, 2 * FFN]); w_dn = din("w_dn", [2, FFN, D])
    pcols_d = din("pcols", [2, 128, NCOL]); prow_d = din("prow", [2, 128, 128]); gfin_d = din("gfin", [128, D])
    masks_d = din("masks", [128, 9, 128], BF16); bands_d = din("bands", [128, 24, 128], BF16)
    blkc_d = din("blkc", [128, NSEQ, 128], BF16); mcol_d = din("mcol", [128, NSEQ])

    yp = dout("yp", [2048, D]); ys = dout("ys", [128, D])
    pool_p = dout("pool_p", [2, 15, D]); pool_s = dout("pool_s", [2, NSEQ, 15, D])
    conv_p = dout("conv_p", [2, 128, 48, 3]); conv_s = dout("conv_s", [2, 128, 48, NSEQ, 3])
    ssm_p = dout("ssm_p", [2, 128, 4096]); ssm_s = dout("ssm_s", [2, NSEQ, 128, 4096])
    ffn_p = dout("ffn_p", [2, 128, 88, 2]); ffn_s = dout("ffn_s", [2, 128, 88, NSEQ, 2])
    mk_p = dout("mk_p", [2, 256, D]); mv_p = dout("mv_p", [2, 256, D])
    xscr = nc.dram_tensor("xscr", [NPT + 1, 128, D], F32, kind="Internal").ap()
    if DEBUG:
        dbg = dout("dbg", [4, 128, D])
        dbgT = dout("dbgT", [6, 128, 32, 128])

    with ExitStack() as es:
        tr = Tracker(nc, es)

        sbtot = [0]

        def sb(name, shape, dt=BF16):
            sbtot[0] += int(np.prod(shape[1:])) * (2 if dt == BF16 else 4)
            return es.enter_context(nc.sbuf_tensor("sb_" + name, list(shape), dt)), Res(name)

        masks, r_masks = sb("masks", [128, 9, 128]); bands, r_bands = sb("bands", [128, 24, 128])
        blkc, r_blkc = sb("blkc", [128, NSEQ, 128]); mcol, r_mcol = sb("mcol", [128, NSEQ], F32)
        pcols, r_pcols = sb("pcols", [128, 2, NCOL], F32); prow, r_prow = sb("prow", [128, 2, 128], F32)
        arow, r_arow = sb("arow", [128, 2, 64], F32)
        epsc, r_epsc = sb("epsc", [128, 1], F32)
        WSL = [sb("wslot%d" % i, [128, 16, 512]) for i in range(2)]
        xres, r_xres = sb("xres", [128, D], F32)
        xn, r_xn = sb("xn", [128, D])
        junk, r_junk = xn, r_xn
        st4, r_st4 = sb("st4", [128, 8], F32)
        hT, r_hT = sb("hT", [128, 16, 128])
        A1, r_A1 = sb("A1", [128, 16, 128]); A2, r_A2 = sb("A2", [128, 16, 128])
        opT, r_opT = sb("opT", [128, 16, 128]); osT, r_osT = A1, r_A1
        utok = [sb("utok%d" % i, [128, D]) for i in range(2)]
        ufp, r_ufp = sb("ufp", [128, D], F32)
        szT, r_szT = sb("szT", [128, 32, 128]); xsT, r_xsT = sb("xsT", [128, 32, 128])
        BCT, r_BCT = sb("BCT", [128, 16, 128])
        STG = [sb("stg%d" % i, [128, 176], F32) for i in range(4)]
        ACC = [sb("acc%d" % i, [128, 128], F32) for i in range(4)]
        chalo, r_chalo = sb("chalo", [128, 48, 3], F32); fhalo, r_fhalo = sb("fhalo", [128, 88, 2], F32)
        sst, r_sst = sb("sst", [128, 4, NSEQ, 3], F32)
        sso, r_sso = sb("sso", [128, 4, NSEQ, 3], F32)
        dtt, r_dtt = sb("dtt", [128, 8, 64], F32)
        lah, r_lah = sb("lah", [128, 2, 64])
        lahf, r_lahf = sb("lahf", [128, 2, 64], F32)
        Btok, r_Btok = sb("Btok", [128, 1024]); Bmsk, r_Bmsk = sb("Bmsk", [128, 1024])
        CBm, r_CBm = sb("CBm", [128, 8, 128], F32)
        xdA, r_xdA = sb("xdA", [128, 4096])
        xdwg, r_xdwg = sb("xdwg", [128, 512])
        yacc, r_yacc = sb("yacc", [128, 4096])
        spl, r_spl = yacc[:, :].rearrange("p (i d) -> p i d", i=2), r_yacc
        LSG = [sb("lsg%d" % i, [128, 2, 128]) for i in range(4)]
        DEC = [sb("dec%d" % i, [128, 128], F32) for i in range(4)]
        MTB = [sb("mtb%d" % i, [128, 128]) for i in range(4)]
        yo_t, r_yo_t = sb("yo_t", [128, 512])
        ytg, r_ytg = sb("ytg", [128, 512])
        yz, r_yz = sb("yz", [128, 4, 128], F32); sq, r_sq = sb("sq", [128, 4, 128])
        rsb, r_rsb = sb("rsb", [128, 128], F32)
        S, r_S = sb("S", [128, 4096], F32); Sb, r_Sb = sb("Sb", [128, 4096])
        Em, r_Em = sb("Em", [128, 64], F32)
        kT, r_kT = sb("kT", [128, 16, 256]); vb, r_vb = sb("vb", [128, 2, D])
        kfp, r_kfp = ufp[:, 0:512], r_ufp
        QM = [(opT, r_opT), (A2, r_A2)]
        PM = [(hT[:, 0:8, :], r_hT), (hT[:, 8:16, :], r_hT)]
        gTc = lambda c: (szT[:, c, :], r_szT) if c < 32 else (xsT[:, c - 32, :], r_xsT)
        pp, r_pp = sb("pp", [128, 4, 256]); pT, r_pT = sb("pT", [128, 8, 128])
        otok, r_otok = xn, r_xn
        sstf, r_sstf = sb("sstf", [128, 4, NSEQ, 2], F32); ssof, r_ssof = sb("ssof", [128, 4, NSEQ, 2], F32)
        PF = [(es.enter_context(nc.psum_tensor("pf%d" % i, [128, 512], F32)), Res("pf%d" % i)) for i in range(6)]
        PB = [(es.enter_context(nc.psum_tensor("pb%d" % i, [128, 1024], BF16)), Res("pb%d" % i)) for i in range(2)]
        rot = {"pf": 0, "pb": 0, "stg": 0, "acc": 0, "lsg": 0, "dec": 0, "mtb": 0}

        def nxt(lst, key):
            i = rot[key]
            rot[key] = (i + 1) % len(lst)
            return lst[i]

        PY = PF.pop()
        psf = lambda: nxt(PF, "pf")
        psb = lambda: nxt(PB, "pb")
        r_din = Res("din")
        r_xscr = [Res("xscr%d" % i) for i in range(NPT + 1)]

        E = tr.emit
        V, A, P = nc.vector, nc.scalar, nc.tensor
        ACOPY = lambda out, in_: A.activation(out=out, in_=in_, func=AF.Identity)

        tr.dma("pool", masks[:], masks_d, W=[r_masks]); tr.dma("pool", bands[:], bands_d, W=[r_bands])
        tr.dma("pool", blkc[:], blkc_d, W=[r_blkc]); tr.dma("pool", mcol[:], mcol_d, W=[r_mcol])
        for l in range(2):
            tr.dma("pool", pcols[:, l, :], pcols_d[l], W=[r_pcols])
            tr.dma("pool", prow[:, l, :], prow_d[l], W=[r_prow])
        for l in range(2):
            E("act", lambda: A.activation(out=arow[:, l, :], in_=prow[:, l, 64:128], func=AF.Exp), R=[r_prow], W=[r_arow])
            E("dve", lambda: V.tensor_scalar(out=arow[:, l, :], in0=arow[:, l, :], scalar1=-1.0, scalar2=None, op0=ALU.mult), R=[r_arow], W=[r_arow])
        ident = masks[:, M_ID, :]
        E("dve", lambda: V.memset(epsc[:], EPS), W=[r_epsc])

        WB = {}
        for nm, wap in (("w_in", w_in), ("w_pg", w_pg), ("w_po", w_po), ("w_so", w_so), ("w_out", w_out), ("w_q", w_q),
                        ("w_o", w_o), ("w_k", w_k), ("w_v", w_v), ("w_up", w_up), ("w_dn", w_dn)):
            shp = list(wap.shape)
            wb = nc.dram_tensor("wb_" + nm, shp, BF16, kind="Internal").ap()
            WB[nm] = (wap, wb, [Res("wb_%s_%d" % (nm, l)) for l in range(2)])

        def convert_layer(l):
            for nm in ("w_k", "w_v", "w_in", "w_pg", "w_po", "w_so", "w_out", "w_q", "w_o", "w_up", "w_dn"):
                wap, wb, rr = WB[nm]
                srcs = [(wap[l, g], wb[l, g]) for g in range(4)] if nm == "w_pg" else [(wap[l], wb[l])]
                for sa, da in srcs:
                    K_ = sa.shape[0]
                    for c in range(K_ // 128):
                        tr.dma("pool", da[c * 128:(c + 1) * 128, :], sa[c * 128:(c + 1) * 128, :], W=[rr[l]])

        for l in range(NL):
            convert_layer(l)

        def layer_blocks(l):
            b = []
            wi = ("w_in", l, None)
            for nb in range(4): b.append(("u", wi, 0, 16, nb * 512, 512))
            for g in range(4): b.append(("pg", ("w_pg", l, g), 0, 4, 0, 512))
            for nb in range(4): b.append(("po", ("w_po", l, None), 0, 16, nb * 512, 512))
            for nb in range(8): b.append(("z", wi, 0, 16, 2048 + nb * 512, 512))
            for nb in range(12): b.append(("xbc", wi, 0, 16, 6144 + nb * 512, 512))
            b.append(("dt", wi, 0, 16, 12288, 64))
            for nb in range(4):
                for kb in range(2): b.append(("so", ("w_so", l, None), kb * 16, 16, nb * 512, 512))
            for nb in range(4): b.append(("gp", wi, 0, 16, 12352 + nb * 512, 512))
            for nb in range(4): b.append(("gs", wi, 0, 16, 14400 + nb * 512, 512))
            for nb in range(4): b.append(("wo", ("w_out", l, None), 0, 16, nb * 512, 512))
            for nb in range(4): b.append(("q", ("w_q", l, None), 0, 16, nb * 512, 512))
            for nb in range(4): b.append(("o", ("w_o", l, None), 0, 16, nb * 512, 512))
            for nb in range(22): b.append(("up", ("w_up", l, None), 0, 16, nb * 512, 512))
            for nb in range(4):
                for (k0, kc) in ((0, 16), (16, 16), (32, 12)): b.append(("dn", ("w_dn", l, None), k0, kc, nb * 512, 512))
            return b

        wseq = []
        for l in range(NL):
            for nb in range(4): wseq.append(("k", ("w_k", l, None), 0, 16, nb * 512, 512))
            for nb in range(4): wseq.append(("v", ("w_v", l, None), 0, 16, nb * 512, 512))
            lb = layer_blocks(l)
            for t in range(NPT + 1):
                wseq.extend(lb)
        wstate = {"issued": 0, "next": 0}

        def wissue():
            i = wstate["issued"]
            tag, (wnm, wl, wg), k0, kc, n0, nw = wseq[i]
            slot, rs = WSL[i % 2]
            _, wb, rr = WB[wnm]
            wap = wb[wl] if wg is None else wb[wl, wg]
            src = wap[k0 * 128:(k0 + kc) * 128, n0:n0 + nw].rearrange("(kc p) n -> p kc n", p=128)
            tr.dma("sp", slot[:, 0:kc, 0:nw], src, R=[rr[wl]], W=[rs], key=rs)
            wstate["issued"] = i + 1

        def wprefetch():
            while wstate["issued"] < min(len(wseq), wstate["next"] + 1):
                wissue()

        def wnext(tag, pf=True):
            i = wstate["next"]
            assert wseq[i][0] == tag, (wseq[i][0], tag)
            while wstate["issued"] < min(len(wseq), i + (2 if pf else 1)):
                wissue()
            wstate["next"] = i + 1
            slot, rs = WSL[i % 2]
            return slot, rs, wseq[i][3], wseq[i][5]

        def mm(ps, lhsT, rhs, st, sp_, R, Wr):
            E("pe", lambda: P.matmul(ps, lhsT, rhs, start=st, stop=sp_), R=R, W=[Wr])

        def tp(ps, in_, R, Wr, idn=None):
            E("pe", lambda: P.transpose(ps, in_, ident if idn is None else idn), R=R + [r_masks], W=[Wr])

        def ws_block(tag, rhs_of_k, r_act, evac, first=True, last=True, ps=None):
            slot, rs, kc, nw = wnext(tag)
            if ps is None:
                ps = psf()
            pt, rp = ps
            for j in range(nw // 128):
                for k in range(kc):
                    mm(pt[:, j * 128:(j + 1) * 128], slot[:, k, j * 128:(j + 1) * 128], rhs_of_k(k),
                       first and k == 0, last and k == kc - 1, [rs] + (r_act if isinstance(r_act, list) else [r_act]), rp)
            if last:
                for j in range(nw // 128):
                    evac(j, pt[:, j * 128:(j + 1) * 128], rp)
            return ps

        def as_block(tag, lhsT_of_k, r_act, first=True, last=True, ps=None, M=128):
            slot, rs, kc, nw = wnext(tag)
            if ps is None:
                ps = psf()
            pt, rp = ps
            for k in range(kc):
                mm(pt[0:M, 0:nw], lhsT_of_k(k), slot[:, k, 0:nw], first and k == 0, last and k == kc - 1, [rs] + (r_act if isinstance(r_act, list) else [r_act]), rp)
            return ps

        def pc(l, off, c=None):
            return pcols[:, l, off:off + 1] if c is None else pcols[:, l, off + c:off + c + 1]

        def rms_to_T(l, goff, src=None, r_src=None, dstT=None, r_dstT=None, ncols=128, coff=0):
            src = xres if src is None else src
            r_src = r_xres if r_src is None else r_src
            dstT = hT if dstT is None else dstT
            r_dstT = r_hT if r_dstT is None else r_dstT
            E("dve", lambda: V.memset(st4[:, 0:1], 0.0), W=[r_st4])
            E("act", lambda: A.activation(out=junk[:], in_=src[:], func=AF.Square, accum_out=st4[:, 0:1]), R=[r_src, r_st4], W=[r_junk, r_st4])
            E("act", lambda: A.activation(out=st4[:, 2:3], in_=st4[:, 0:1], func=AF.Sqrt, scale=1.0 / D, bias=epsc[:, 0:1]), R=[r_st4, r_epsc], W=[r_st4])
            E("dve", lambda: V.reciprocal(out=st4[:, 1:2], in_=st4[:, 2:3]), R=[r_st4], W=[r_st4])
            E("dve", lambda: V.tensor_scalar(out=xn[:], in0=src[:], scalar1=st4[:, 1:2], scalar2=None, op0=ALU.mult), R=[r_src, r_st4], W=[r_xn])
            for c4 in range(4):
                pt, rp = psb()
                for i in range(4):
                    c = c4 * 4 + i
                    tp(pt[:, i * 128:(i + 1) * 128], xn[:, c * 128:(c + 1) * 128], [r_xn], rp)
                for i in range(4):
                    c = c4 * 4 + i
                    E("act", lambda: A.activation(out=dstT[:, c, coff:coff + 128], in_=pt[:, i * 128:(i + 1) * 128], func=AF.Identity, scale=pc(l, goff, c)),
                      R=[rp, r_pcols], W=[r_dstT])

        def mem_kv(l):
            mnT, r_mnT = kT, r_kT
            for mt in range(2):
                tr.dma("pool", xres[:], mem[mt * 128:(mt + 1) * 128, :], W=[r_xres])
                rms_to_T(l, C_NKV, dstT=(A1 if mt == 0 else A2), r_dstT=(r_A1 if mt == 0 else r_A2))
                if STOP <= 0.2:
                    raise StopBuild()
            mn = [(A1, r_A1), (A2, r_A2)]
            for which, tag, outd in (("k", "k", mk_p), ("v", "v", mv_p)):
                if which == "v" and STOP <= 0.7:
                    raise StopBuild()
                for nb in range(4):
                    slot, rs, kc, nw = wnext(tag)
                    for mt in range(2):
                        pt, rp = psf()
                        for k in range(kc):
                            mm(pt[:, 0:512], mn[mt][0][:, k, :], slot[:, k, 0:512], k == 0, k == kc - 1, [rs, mn[mt][1]], rp)
                        E("dve", lambda: V.tensor_copy(out=kfp[:], in_=pt[:, 0:512]), R=[rp], W=[r_kfp])
                        tr.dma("pool", outd[l, mt * 128:(mt + 1) * 128, nb * 512:(nb + 1) * 512], kfp[:], R=[r_kfp], key=r_kfp, is_out=True)
                        if STOP <= 0.5:
                            raise StopBuild()
                        if which == "v":
                            E("act", lambda: ACOPY(out=vb[:, mt, nb * 512:(nb + 1) * 512], in_=pt[:, 0:512]), R=[rp], W=[r_vb])
                        else:
                            E("act", lambda: ACOPY(out=otok[:, 0:512], in_=pt[:, 0:512]), R=[rp], W=[r_otok])
                            pb_, rpb = psb()
                            for i in range(4):
                                tp(pb_[:, i * 128:(i + 1) * 128], otok[:, i * 128:(i + 1) * 128], [r_otok], rpb)
                            for i in range(4):
                                E("dve", lambda: V.tensor_copy(out=kT[:, nb * 4 + i, mt * 128:(mt + 1) * 128], in_=pb_[:, i * 128:(i + 1) * 128]), R=[rpb], W=[r_kT])
                            if STOP <= 0.6:
                                raise StopBuild()

        def do_tile(l, ti):
            smp = ti == NPT
            first = ti == 0
            lastp = ti == NPT - 1
            NB = NSEQ if smp else 1
            L = LS if smp else 128
            mU, mL, mC, mB = (M_US, M_LS, M_CS, M_BLK) if smp else (M_UP, M_LP, M_CP, M_ONE)
            def dump(i, buf, r_buf):
                if DEBUG and smp and l == 0:
                    tr.dma("pool", dbg[i], buf[:], R=[r_buf], key=r_buf, is_out=True)

            def dumpT(i, buf, r_buf, n=16):
                if DEBUG and smp and l == 0:
                    tr.dma("pool", dbgT[i, :, 0:n, :], buf[:, 0:n, :], R=[r_buf], key=r_buf, is_out=True)
            if l == 0:
                src = xs if smp else xp[ti * 128:(ti + 1) * 128, :]
                tr.dma("pool", xres[:], src, W=[r_xres])
            else:
                tr.dma("pool", xres[:], xscr[ti], R=[r_xscr[ti]], W=[r_xres], key=r_xres)
            rms_to_T(l, C_NMIX)
            hk = lambda k: hT[:, k, :]
            ut, r_ut = utok[ti % 2]
            up_, r_up = utok[(ti + 1) % 2]
            need_fp = smp or lastp
            for nb in range(4):
                pt, rp = as_block("u", hk, r_hT)
                E("act", lambda: ACOPY(out=ut[:, nb * 512:(nb + 1) * 512], in_=pt[:, 0:512]), R=[rp], W=[r_ut])
                if need_fp:
                    E("dve", lambda: V.tensor_copy(out=ufp[:, nb * 512:(nb + 1) * 512], in_=pt[:, 0:512]), R=[rp], W=[r_ufp])
            if lastp:
                tr.dma("pool", pool_p[l], ufp[113:128, :], R=[r_ufp], key=r_ufp, is_out=True)
            if smp:
                for b in range(NSEQ):
                    tr.dma("pool", pool_s[l, b, 7:15, :], ufp[b * LS:(b + 1) * LS, :], R=[r_ufp], key=r_ufp, is_out=True)
                tr.dma("pool", pool_s[l][:, 0:7, :], spool[l].rearrange("(b r) d -> b r d", r=15)[:, 8:15, :], key=r_ufp, R=[r_ufp], is_out=True)
                for i in range(2):
                    tr.dma("pool", spl[0:120, i, :], spool[l, i * 120:(i + 1) * 120, :], W=[r_spl])
            if STOP <= 2:
                raise StopBuild()
            for g in range(4):
                pt, rp = psf()
                for i in range(4):
                    c = g * 4 + i
                    o_ = pt[:, i * 128:(i + 1) * 128]
                    if smp:
                        mm(o_, ut[:, c * 128:(c + 1) * 128], bands[:, 12 + g, :], True, False, [r_ut, r_bands], rp)
                        mm(o_, spl[0:120, 0, c * 128:(c + 1) * 128], bands[0:120, 16 + 2 * g, :], False, False, [r_spl, r_bands], rp)
                        mm(o_, spl[0:120, 1, c * 128:(c + 1) * 128], bands[0:120, 17 + 2 * g, :], False, True, [r_spl, r_bands], rp)
                    elif first:
                        mm(o_, ut[:, c * 128:(c + 1) * 128], bands[:, 8 + g, :], True, True, [r_ut, r_bands], rp)
                    else:
                        mm(o_, ut[:, c * 128:(c + 1) * 128], bands[:, g, :], True, False, [r_ut, r_bands], rp)
                        mm(o_, up_[:, c * 128:(c + 1) * 128], bands[:, 4 + g, :], False, True, [r_up, r_bands], rp)
                E("dve", lambda: V.tensor_copy(out=A1[:, g * 4:(g + 1) * 4, :], in_=pt[:, 0:512].rearrange("p (a b) -> p a b", a=4)), R=[rp], W=[r_A1])
            for g in range(4):
                def ev(j, pa, rp, g=g):
                    E("act", lambda: A.activation(out=A2[:, g * 4 + j, :], in_=pa, func=AF.Identity, scale=pc(l, C_PSC, g * 4 + j)), R=[rp, r_pcols], W=[r_A2])
                ws_block("pg", lambda k, g=g: A1[:, g * 4 + k, :], r_A1, ev)
            for nb in range(4):
                def ev(j, pa, rp, nb=nb):
                    E("dve", lambda: V.tensor_copy(out=opT[:, nb * 4 + j, :], in_=pa), R=[rp], W=[r_opT])
                ws_block("po", lambda k: A2[:, k, :], r_A2, ev)
            if STOP <= 3:
                raise StopBuild()
            dumpT(0, opT, r_opT)
            for nb in range(8):
                def ev(j, pa, rp, nb=nb):
                    E("act", lambda: A.activation(out=szT[:, nb * 4 + j, :], in_=pa, func=AF.Silu), R=[rp], W=[r_szT])
                ws_block("z", hk, r_hT, ev)

            def conv_chunk(pa, rp, c, K, halo, r_halo, woff, boff, nW, sblk, finish):
                H = K - 1
                (sg, r_sg), (ac, r_ac) = nxt(STG, "stg"), nxt(ACC, "acc")
                W_ = H + L
                sv = sg[:, 0:NB * W_].rearrange("p (b w) -> p b w", b=NB)
                E("act", lambda: A.activation(out=sv[:, :, H:H + L], in_=pa.rearrange("p (b j) -> p b j", b=NB), func=AF.Identity), R=[rp], W=[r_sg])
                if smp:
                    hin, r_hin, hout, r_hout, ci = sblk
                    E("act", lambda: A.activation(out=sv[:, :, 0:H], in_=hin[:, ci, :, 0:H], func=AF.Identity), R=[r_hin], W=[r_sg])
                    E("dve", lambda: V.tensor_copy(out=hout[:, ci, :, 0:H], in_=sv[:, :, L:L + H]), R=[r_sg], W=[r_hout])
                else:
                    if first:
                        E("dve", lambda: V.memset(sv[:, :, 0:H], 0.0), W=[r_sg])
                    else:
                        E("act", lambda: A.activation(out=sv[:, 0, 0:H], in_=halo[:, c, :], func=AF.Identity), R=[r_halo], W=[r_sg])
                    E("dve", lambda: V.tensor_copy(out=halo[:, c, :], in_=sv[:, 0, L:L + H]), R=[r_sg], W=[r_halo])
                av = ac[:, :].rearrange("p (b j) -> p b j", b=NB)
                E("act", lambda: A.activation(out=av, in_=sv[:, :, 0:L], func=AF.Identity, scale=pc(l, woff, c), bias=pc(l, boff, c)),
                  R=[r_sg, r_pcols], W=[r_ac])
                for j in range(1, K):
                    E("dve", lambda: V.scalar_tensor_tensor(out=av, in0=sv[:, :, j:j + L], scalar=pc(l, woff, j * nW + c), in1=av, op0=ALU.mult, op1=ALU.add),
                      R=[r_sg, r_pcols, r_ac], W=[r_ac])
                finish(ac, r_ac)

            for nb in range(12):
                if smp:
                    tr.dma("pool", sst[:], sconv[l, :, nb * 4:(nb + 1) * 4, :, :], W=[r_sst])

                def ev(j, pa, rp, nb=nb):
                    c = nb * 4 + j

                    def fin(ac, r_ac):
                        dst = xsT[:, c, :] if c < 32 else BCT[:, c - 32, :]
                        rd = r_xsT if c < 32 else r_BCT
                        E("act", lambda: A.activation(out=dst, in_=ac[:], func=AF.Silu), R=[r_ac], W=[rd])
                    conv_chunk(pa, rp, c, 4, chalo, r_chalo, C_CW, C_CB, 48, (sst, r_sst, sso, r_sso, j), fin)
                ws_block("xbc", hk, r_hT, ev)
                if smp:
                    tr.dma("pool", conv_s[l, :, nb * 4:(nb + 1) * 4, :, :], sso[:], R=[r_sso], key=r_sso, is_out=True)
            if lastp:
                tr.dma("pool", conv_p[l], chalo[:], R=[r_chalo], key=r_chalo, is_out=True)
            pt, rp = as_block("dt", hk, r_hT)
            dt_, la_, E_, cs_, te_, dtw_, cd_, tmp_ = [dtt[:, i, :] for i in range(8)]
            E("dve", lambda: V.tensor_tensor(out=tmp_, in0=pt[:, 0:64], in1=prow[:, l, 0:64], op=ALU.add), R=[rp, r_prow], W=[r_dtt])
            E("act", lambda: A.activation(out=tmp_, in_=tmp_, func=AF.Exp), R=[r_dtt], W=[r_dtt])
            E("act", lambda: A.activation(out=dt_, in_=tmp_, func=AF.Ln, bias=1.0, scale=1.0), R=[r_dtt], W=[r_dtt])
            E("dve", lambda: V.tensor_tensor(out=la_, in0=dt_, in1=arow[:, l, :], op=ALU.mult), R=[r_dtt, r_arow], W=[r_dtt])
            E("dve", lambda: V.tensor_copy(out=lah[:, 0, :], in_=la_), R=[r_dtt], W=[r_lah])
            E("dve", lambda: V.tensor_tensor(out=lah[:, 1, :], in0=la_, in1=lah[:, 0, :], op=ALU.subtract), R=[r_dtt, r_lah], W=[r_lah])
            E("dve", lambda: V.tensor_copy(out=lahf[:], in_=lah[:]), R=[r_lah], W=[r_lahf])
            pt, rp = psf()
            for i in range(2):
                mm(pt[:, 0:64], masks[:, mL, :], lah[:, i, :], i == 0, i == 1, [r_masks, r_lah], rp)
            for i in range(2):
                mm(pt[:, 64:128], masks[:, mB, :], lah[:, i, :], i == 0, i == 1, [r_masks, r_lah], rp)
            E("act", lambda: A.activation(out=E_, in_=pt[:, 0:64], func=AF.Exp), R=[rp], W=[r_dtt])
            E("dve", lambda: V.tensor_copy(out=cs_, in_=pt[:, 0:64]), R=[rp], W=[r_dtt])
            E("dve", lambda: V.tensor_tensor(out=tmp_, in0=pt[:, 64:128], in1=cs_, op=ALU.subtract), R=[rp, r_dtt], W=[r_dtt])
            E("act", lambda: A.activation(out=te_, in_=tmp_, func=AF.Exp), R=[r_dtt], W=[r_dtt])
            E("act", lambda: A.activation(out=cd_, in_=pt[:, 64:128], func=AF.Exp), R=[rp], W=[r_dtt])
            E("dve", lambda: V.tensor_tensor(out=dtw_, in0=dt_, in1=te_, op=ALU.mult), R=[r_dtt], W=[r_dtt])
            if STOP <= 4:
                raise StopBuild()
            pb_, rpb = psb()
            for g in range(8):
                tp(pb_[:, g * 128:(g + 1) * 128], BCT[:, g, :], [r_BCT], rpb)
            E("dve", lambda: V.tensor_copy(out=Btok[:], in_=pb_[:, 0:1024]), R=[rpb], W=[r_Btok])
            for g2 in range(2):
                pt, rp = psf()
                for i in range(4):
                    g = g2 * 4 + i
                    mm(pt[:, i * 128:(i + 1) * 128], BCT[:, g, :], BCT[:, 8 + g, :], True, True, [r_BCT], rp)
                for i in range(4):
                    g = g2 * 4 + i
                    E("dve", lambda: V.tensor_tensor(out=CBm[:, g, :], in0=pt[:, i * 128:(i + 1) * 128], in1=masks[:, mC, :], op=ALU.mult), R=[rp, r_masks], W=[r_CBm])
            for g in range(8):
                pb_, rpb = psb()
                for i in range(4):
                    tp(pb_[:, i * 128:(i + 1) * 128], xsT[:, g * 4 + i, :], [r_xsT], rpb)
                pv = pb_[:, 0:512].rearrange("p (h d) -> p h d", h=8)
                E("dve", lambda: V.tensor_tensor(out=xdA[:, g * 512:(g + 1) * 512].rearrange("p (h d) -> p h d", h=8), in0=pv,
                                                 in1=dtt[:, 0, g * 8:(g + 1) * 8].unsqueeze(2).broadcast_to([128, 8, 64]), op=ALU.mult), R=[rpb, r_dtt], W=[r_xdA])
            def make_xdw(g):
                E("dve", lambda: V.tensor_tensor(out=xdwg[:].rearrange("p (h d) -> p h d", h=8), in0=xdA[:, g * 512:(g + 1) * 512].rearrange("p (h d) -> p h d", h=8),
                                                 in1=dtt[:, 4, g * 8:(g + 1) * 8].unsqueeze(2).broadcast_to([128, 8, 64]), op=ALU.mult), R=[r_xdA, r_dtt], W=[r_xdwg])
            have_off = smp or not first
            if smp:
                for b in range(NSEQ):
                    tr.dma("pool", S[:], sssm[l, b], W=[r_S])
                    E("act", lambda: ACOPY(out=Sb[:], in_=S[:]), R=[r_S], W=[r_Sb])
                    E("dve", lambda: V.tensor_scalar(out=Em[:], in0=E_, scalar1=mcol[:, b:b + 1], scalar2=None, op0=ALU.mult), R=[r_dtt, r_mcol], W=[r_Em])
                    E("dve", lambda: V.tensor_scalar(out=Bmsk[:], in0=Btok[:], scalar1=mcol[:, b:b + 1], scalar2=None, op0=ALU.mult), R=[r_Btok, r_mcol], W=[r_Bmsk])
                    pt, rp = psf()
                    for i in range(2):
                        mm(pt[:, 0:64], blkc[:, b, :], lah[:, i, :], i == 0, i == 1, [r_blkc, r_lah], rp)
                    E("act", lambda: A.activation(out=cd_, in_=pt[:, 0:64], func=AF.Exp), R=[rp], W=[r_dtt])
                    for g in range(8):
                        gs_ = slice(g * 512, (g + 1) * 512)
                        pt, rp = psf()
                        mm(pt[:, 0:512], BCT[:, 8 + g, :], Sb[:, gs_], True, True, [r_BCT, r_Sb], rp)
                        yv = yacc[:, gs_].rearrange("p (h d) -> p h d", h=8)
                        emb = Em[:, g * 8:(g + 1) * 8].unsqueeze(2).broadcast_to([128, 8, 64])
                        pv = pt[:, 0:512].rearrange("p (h d) -> p h d", h=8)
                        if b == 0:
                            E("dve", lambda: V.tensor_tensor(out=yv, in0=pv, in1=emb, op=ALU.mult), R=[rp, r_Em], W=[r_yacc])
                        else:
                            E("dve", lambda: V.tensor_tensor(out=yo_t[:].rearrange("p (h d) -> p h d", h=8), in0=pv, in1=emb, op=ALU.mult), R=[rp, r_Em], W=[r_yo_t])
                            E("dve", lambda: V.tensor_tensor(out=yacc[:, gs_], in0=yacc[:, gs_], in1=yo_t[:], op=ALU.add), R=[r_yo_t, r_yacc], W=[r_yacc])
                        pt2, rp2 = psf()
                        make_xdw(g)
                        mm(pt2[:, 0:512], Bmsk[:, g * 128:(g + 1) * 128], xdwg[:], True, True, [r_Bmsk, r_xdwg], rp2)
                        E("dve", lambda: V.tensor_tensor(out=S[:, gs_].rearrange("p (h d) -> p h d", h=8), in0=S[:, gs_].rearrange("p (h d) -> p h d", h=8),
                                                         in1=dtt[:, 6, g * 8:(g + 1) * 8].unsqueeze(2).broadcast_to([128, 8, 64]), op=ALU.mult), R=[r_S, r_dtt, r_Sb], W=[r_S])
                        E("dve", lambda: V.tensor_tensor(out=S[:, gs_], in0=S[:, gs_], in1=pt2[:, 0:512], op=ALU.add), R=[r_S, rp2], W=[r_S])
                    tr.dma("pool", ssm_s[l, b], S[:], R=[r_S], key=r_S, is_out=True)
            else:
                if first:
                    E("dve", lambda: V.memset(S[:], 0.0), W=[r_S])
                for g in range(8):
                    gs_ = slice(g * 512, (g + 1) * 512)
                    if have_off:
                        pt, rp = psf()
                        mm(pt[:, 0:512], BCT[:, 8 + g, :], Sb[:, gs_], True, True, [r_BCT, r_Sb], rp)
                        E("dve", lambda: V.tensor_tensor(out=yacc[:, gs_].rearrange("p (h d) -> p h d", h=8), in0=pt[:, 0:512].rearrange("p (h d) -> p h d", h=8),
                                                         in1=dtt[:, 2, g * 8:(g + 1) * 8].unsqueeze(2).broadcast_to([128, 8, 64]), op=ALU.mult), R=[rp, r_dtt], W=[r_yacc])
                for g in range(8):
                    gs_ = slice(g * 512, (g + 1) * 512)
                    pt2, rp2 = psf()
                    make_xdw(g)
                    mm(pt2[:, 0:512], Btok[:, g * 128:(g + 1) * 128], xdwg[:], True, True, [r_Btok, r_xdwg], rp2)
                    E("dve", lambda: V.tensor_tensor(out=S[:, gs_].rearrange("p (h d) -> p h d", h=8), in0=S[:, gs_].rearrange("p (h d) -> p h d", h=8),
                                                     in1=dtt[:, 6, g * 8:(g + 1) * 8].unsqueeze(2).broadcast_to([128, 8, 64]), op=ALU.mult), R=[r_S, r_dtt, r_Sb], W=[r_S])
                    E("dve", lambda: V.tensor_tensor(out=S[:, gs_], in0=S[:, gs_], in1=pt2[:, 0:512], op=ALU.add), R=[r_S, rp2], W=[r_S])
                E("act", lambda: ACOPY(out=Sb[:], in_=S[:]), R=[r_S, r_yacc], W=[r_Sb])
                if lastp:
                    tr.dma("pool", ssm_p[l], S[:], R=[r_S], key=r_S, is_out=True)
            for g in range(8):
                gs_ = slice(g * 512, (g + 1) * 512)
                py, rpy = PY
                for hh in range(8):
                    h = g * 8 + hh
                    (lg, r_lg), (dc, r_dc), (mt_, r_mt) = nxt(LSG, "lsg"), nxt(DEC, "dec"), nxt(MTB, "mtb")
                    for i in range(2):
                        E("act", lambda: A.activation(out=lg[:, i, :], in_=masks[:, mU, :], func=AF.Identity, scale=lahf[:, i, h:h + 1]), R=[r_masks, r_lahf], W=[r_lg])
                    pg_, rpg = psf()
                    for i in range(2):
                        mm(pg_[:, 0:128], lg[:, i, :], masks[:, mL, :], i == 0, i == 1, [r_lg, r_masks], rpg)
                    E("act", lambda: A.activation(out=dc[:], in_=pg_[:, 0:128], func=AF.Exp), R=[rpg], W=[r_dc])
                    E("dve", lambda: V.tensor_tensor(out=mt_[:], in0=CBm[:, g, :], in1=dc[:], op=ALU.mult), R=[r_CBm, r_dc], W=[r_mt])
                    mm(py[:, hh * 64:(hh + 1) * 64], mt_[:], xdA[:, h * 64:(h + 1) * 64], True, True, [r_mt, r_xdA], rpy)
                if have_off:
                    E("dve", lambda: V.tensor_tensor(out=ytg[:], in0=py[:, 0:512], in1=yacc[:, gs_], op=ALU.add), R=[rpy, r_yacc], W=[r_ytg])
                else:
                    E("dve", lambda: V.tensor_copy(out=ytg[:], in_=py[:, 0:512]), R=[rpy], W=[r_ytg])
                pb_, rpb = psb()
                for i in range(4):
                    tp(pb_[:, i * 128:(i + 1) * 128], ytg[:, i * 128:(i + 1) * 128], [r_ytg], rpb)
                for i in range(4):
                    c = g * 4 + i
                    E("dve", lambda: V.scalar_tensor_tensor(out=yz[:, i, :], in0=xsT[:, c, :], scalar=pc(l, C_DC, c), in1=pb_[:, i * 128:(i + 1) * 128], op0=ALU.mult, op1=ALU.add),
                      R=[r_xsT, r_pcols, rpb], W=[r_yz])
                    E("dve", lambda: V.tensor_tensor(out=yz[:, i, :], in0=yz[:, i, :], in1=szT[:, c, :], op=ALU.mult), R=[r_yz, r_szT], W=[r_yz])
                E("act", lambda: A.activation(out=sq[:], in_=yz[:], func=AF.Square), R=[r_yz], W=[r_sq])
                pm_, rpm = psf()
                for i in range(4):
                    mm(pm_[:, 0:128], masks[:, M_ONE, :], sq[:, i, :], i == 0, i == 3, [r_masks, r_sq], rpm)
                E("act", lambda: A.activation(out=rsb[:], in_=pm_[:, 0:128], func=AF.Sqrt, scale=1.0 / 512, bias=epsc[:, 0:1]), R=[rpm, r_epsc], W=[r_rsb])
                E("dve", lambda: V.reciprocal(out=rsb[:], in_=rsb[:]), R=[r_rsb], W=[r_rsb])
                for i in range(4):
                    c = g * 4 + i
                    E("dve", lambda: V.scalar_tensor_tensor(out=szT[:, c, :], in0=yz[:, i, :], scalar=pc(l, C_SN, c), in1=rsb[:], op0=ALU.mult, op1=ALU.mult),
                      R=[r_yz, r_pcols, r_rsb], W=[r_szT])
            if STOP <= 5:
                raise StopBuild()
            dumpT(1, szT, r_szT, 32)
            for nb in range(4):
                def ev(j, pa, rp, nb=nb):
                    E("dve", lambda: V.tensor_copy(out=osT[:, nb * 4 + j, :], in_=pa), R=[rp], W=[r_osT])
                slA, rsA, _, _ = wnext("so")
                slB, rsB, _, _ = wnext("so", pf=False)
                pt, rp = psf()
                for j in range(4):
                    for k in range(16):
                        mm(pt[:, j * 128:(j + 1) * 128], slA[:, k, j * 128:(j + 1) * 128], szT[:, k, :], k == 0, False, [rsA, r_szT], rp)
                    for k in range(16):
                        mm(pt[:, j * 128:(j + 1) * 128], slB[:, k, j * 128:(j + 1) * 128], szT[:, 16 + k, :], False, k == 15, [rsB, r_szT], rp)
                wprefetch()
                for j in range(4):
                    ev(j, pt[:, j * 128:(j + 1) * 128], rp)
            dumpT(2, osT, r_osT)
            for nb in range(4):
                def ev(j, pa, rp, nb=nb):
                    (ac, r_ac) = nxt(ACC, "acc")
                    E("act", lambda: A.activation(out=ac[:], in_=pa, func=AF.Sigmoid), R=[rp], W=[r_ac])
                    E("dve", lambda: V.tensor_tensor(out=opT[:, nb * 4 + j, :], in0=opT[:, nb * 4 + j, :], in1=ac[:], op=ALU.mult), R=[r_ac, r_opT], W=[r_opT])
                ws_block("gp", hk, r_hT, ev)
            for nb in range(4):
                def ev(j, pa, rp, nb=nb):
                    (ac, r_ac) = nxt(ACC, "acc")
                    E("act", lambda: A.activation(out=ac[:], in_=pa, func=AF.Sigmoid), R=[rp], W=[r_ac])
                    E("dve", lambda: V.tensor_tensor(out=ac[:], in0=osT[:, nb * 4 + j, :], in1=ac[:], op=ALU.mult), R=[r_ac, r_osT], W=[r_ac])
                    E("dve", lambda: V.tensor_tensor(out=opT[:, nb * 4 + j, :], in0=opT[:, nb * 4 + j, :], in1=ac[:], op=ALU.add), R=[r_ac, r_opT], W=[r_opT])
                ws_block("gs", hk, r_hT, ev)
            dumpT(3, opT, r_opT)
            for nb in range(4):
                pt, rp = as_block("wo", lambda k: opT[:, k, :], r_opT)
                E("dve", lambda: V.tensor_tensor(out=xres[:, nb * 512:(nb + 1) * 512], in0=xres[:, nb * 512:(nb + 1) * 512], in1=pt[:, 0:512], op=ALU.add), R=[rp, r_xres], W=[r_xres])

            if STOP <= 6:
                raise StopBuild()
            dump(0, xres, r_xres)
            rms_to_T(l, C_NQ)
            for nb in range(4):
                def ev(j, pa, rp, nb=nb):
                    E("act", lambda: ACOPY(out=A1[:, nb * 4 + j, :], in_=pa), R=[rp], W=[r_A1])
                ws_block("q", hk, r_hT, ev)
            scs = [psf() for _ in range(4)]
            nbq = NSEQ if smp else 1
            for b in range(nbq):
                if smp:
                    for c4 in range(4):
                        tr.dma("pool", kT[:, c4 * 4:(c4 + 1) * 4, :], ck[l, b, :, c4 * 4:(c4 + 1) * 4, :], W=[r_kT])
                    qm, r_qm = QM[b % 2]
                    if b < 2:
                        E("dve", lambda: V.memset(qm[:], 0.0), W=[r_qm])
                    else:
                        E("dve", lambda: V.memset(qm[:, :, (b - 2) * LS:(b - 1) * LS], 0.0), W=[r_qm])
                    E("dve", lambda: V.tensor_copy(out=qm[:, :, b * LS:(b + 1) * LS], in_=A1[:, :, b * LS:(b + 1) * LS]), R=[r_A1], W=[r_qm])
                else:
                    qm, r_qm = A1, r_A1
                for h in range(4):
                    pt, rp = scs[h]
                    o_ = pt[:, 0:256]
                    for i in range(4):
                        mm(o_, qm[:, h * 4 + i, :], kT[:, h * 4 + i, :], b == 0 and i == 0, b == nbq - 1 and i == 3, [r_qm, r_kT], rp)
            scale = 1.0 / float(np.sqrt(512.0))
            mx, nmx, ssum, rs_ = st4[:, 4:5], st4[:, 5:6], st4[:, 6:7], st4[:, 7:8]
            for h in range(4):
                pt, rp = scs[h]
                o_ = pt[:, 0:256]
                E("dve", lambda: V.tensor_reduce(out=Em[:, h:h + 1], in_=o_, axis=AX.X, op=ALU.max), R=[rp], W=[r_Em])
                E("dve", lambda: V.tensor_scalar(out=Em[:, 4 + h:5 + h], in0=Em[:, h:h + 1], scalar1=-scale, scalar2=None, op0=ALU.mult), R=[r_Em], W=[r_Em])
                E("dve", lambda: V.memset(Em[:, 8 + h:9 + h], 0.0), W=[r_Em])
                E("act", lambda: A.activation(out=pp[:, h, :], in_=o_, func=AF.Exp, bias=Em[:, 4 + h:5 + h], scale=scale, accum_out=Em[:, 8 + h:9 + h]), R=[rp, r_Em], W=[r_pp, r_Em])
                E("dve", lambda: V.reciprocal(out=Em[:, 12 + h:13 + h], in_=Em[:, 8 + h:9 + h]), R=[r_Em], W=[r_Em])
            pb_, rpb = psb()
            for h in range(4):
                for mt in range(2):
                    i = h * 2 + mt
                    tp(pb_[:, i * 128:(i + 1) * 128], pp[:, h, mt * 128:(mt + 1) * 128], [r_pp], rpb)
            E("dve", lambda: V.tensor_copy(out=pT[:], in_=pb_[:, 0:1024].rearrange("p (a b) -> p a b", a=8)), R=[rpb], W=[r_pT])
            ob = [psf() for _ in range(4)]
            for b in range(nbq):
                if smp:
                    for mt in range(2):
                        tr.dma("pool", vb[:, mt, :], cv[l, b, mt * 128:(mt + 1) * 128, :], W=[r_vb])
                    pm, r_pm = PM[b % 2]
                    if b < 2:
                        E("dve", lambda: V.memset(pm[:], 0.0), W=[r_pm])
                    else:
                        E("dve", lambda: V.memset(pm[:, :, (b - 2) * LS:(b - 1) * LS], 0.0), W=[r_pm])
                    E("dve", lambda: V.tensor_copy(out=pm[:, :, b * LS:(b + 1) * LS], in_=pT[:, :, b * LS:(b + 1) * LS]), R=[r_pT], W=[r_pm])
                else:
                    pm, r_pm = pT, r_pT
                for h in range(4):
                    pt, rp = ob[h]
                    for mt in range(2):
                        mm(pt[:, 0:512], pm[:, h * 2 + mt, :], vb[:, mt, h * 512:(h + 1) * 512], b == 0 and mt == 0, b == nbq - 1 and mt == 1, [r_pm, r_vb], rp)
            for h in range(4):
                pt, rp = ob[h]
                E("act", lambda: A.activation(out=otok[:, h * 512:(h + 1) * 512], in_=pt[:, 0:512], func=AF.Identity, scale=Em[:, 12 + h:13 + h]), R=[rp, r_Em], W=[r_otok])
            for c4 in range(4):
                pb_, rpb = psb()
                for i in range(4):
                    c = c4 * 4 + i
                    tp(pb_[:, i * 128:(i + 1) * 128], otok[:, c * 128:(c + 1) * 128], [r_otok], rpb)
                E("dve", lambda: V.tensor_copy(out=A2[:, c4 * 4:(c4 + 1) * 4, :], in_=pb_[:, 0:512].rearrange("p (a b) -> p a b", a=4)), R=[rpb], W=[r_A2])
            for nb in range(4):
                pt, rp = as_block("o", lambda k: A2[:, k, :], r_A2)
                E("dve", lambda: V.tensor_tensor(out=xres[:, nb * 512:(nb + 1) * 512], in0=xres[:, nb * 512:(nb + 1) * 512], in1=pt[:, 0:512], op=ALU.add), R=[rp, r_xres], W=[r_xres])

            if STOP <= 7:
                raise StopBuild()
            dump(1, xres, r_xres)
            rms_to_T(l, C_NF)
            for nb in range(22):
                if smp:
                    tr.dma("pool", sstf[:], sffn[l, :, nb * 4:(nb + 1) * 4, :, :], W=[r_sstf])

                def ev(j, pa, rp, nb=nb):
                    c = nb * 4 + j

                    def fin(ac, r_ac):
                        if c < 44:
                            E("act", lambda: A.activation(out=gTc(c)[0], in_=ac[:], func=AF.Silu), R=[r_ac], W=[gTc(c)[1]])
                        else:
                            E("dve", lambda: V.tensor_tensor(out=gTc(c - 44)[0], in0=gTc(c - 44)[0], in1=ac[:], op=ALU.mult), R=[r_ac, gTc(c - 44)[1]], W=[gTc(c - 44)[1]])
                    conv_chunk(pa, rp, c, 3, fhalo, r_fhalo, C_FW, C_FB, 88, (sstf, r_sstf, ssof, r_ssof, j), fin)
                ws_block("up", hk, r_hT, ev)
                if smp:
                    tr.dma("pool", ffn_s[l, :, nb * 4:(nb + 1) * 4, :, :], ssof[:], R=[r_ssof], key=r_ssof, is_out=True)
            if lastp:
                tr.dma("pool", ffn_p[l], fhalo[:], R=[r_fhalo], key=r_fhalo, is_out=True)
            for nb in range(4):
                ps = None
                for kb, (k0, kc) in enumerate(((0, 16), (16, 16), (32, 12))):
                    ps = as_block("dn", lambda k, k0=k0: gTc(k0 + k)[0], [r_szT, r_xsT], first=(kb == 0), last=(kb == 2), ps=ps)
                pt, rp = ps
                E("dve", lambda: V.tensor_tensor(out=xres[:, nb * 512:(nb + 1) * 512], in0=xres[:, nb * 512:(nb + 1) * 512], in1=pt[:, 0:512], op=ALU.add), R=[rp, r_xres], W=[r_xres])
            if l == 0:
                tr.dma("pool", xscr[ti], xres[:], R=[r_xres], W=[r_xscr[ti]], key=r_xres)
            else:
                E("dve", lambda: V.memset(st4[:, 0:1], 0.0), W=[r_st4])
                E("act", lambda: A.activation(out=junk[:], in_=xres[:], func=AF.Square, accum_out=st4[:, 0:1]), R=[r_xres, r_st4], W=[r_junk, r_st4])
                E("act", lambda: A.activation(out=st4[:, 2:3], in_=st4[:, 0:1], func=AF.Sqrt, scale=1.0 / D, bias=epsc[:, 0:1]), R=[r_st4, r_epsc], W=[r_st4])
                E("dve", lambda: V.reciprocal(out=st4[:, 1:2], in_=st4[:, 2:3]), R=[r_st4], W=[r_st4])
                tr.dma("pool", ufp[:], gfin_d, W=[r_ufp])
                E("dve", lambda: V.scalar_tensor_tensor(out=xres[:], in0=xres[:], scalar=st4[:, 1:2], in1=ufp[:], op0=ALU.mult, op1=ALU.mult), R=[r_xres, r_st4, r_ufp], W=[r_xres])
                dst = ys if smp else yp[ti * 128:(ti + 1) * 128, :]
                tr.dma("pool", dst, xres[:], R=[r_xres], key=r_xres, is_out=True)

        try:
            for l in range(NL):
                if STOP <= 0:
                    raise StopBuild()
                mem_kv(l)
                if STOP <= 1:
                    raise StopBuild()
                for ti in range(NPT + 1):
                    do_tile(l, ti)
        except StopBuild:
            pass
        tr.finish()
        print('nsem', tr.nsem, 'sbuf bytes', sbtot[0])
    return nc


def _bf(a):
    return np.ascontiguousarray(a.astype(ml_dtypes.bfloat16))


def _consts():
    idx = np.arange(128)
    masks = np.zeros((128, 9, 128), np.float32)
    for base, L in ((0, 128), (4, LS)):
        sq = idx // L
        same = sq[:, None] == sq[None, :]
        masks[:, base + 0, :] = same & (idx[:, None] > idx[None, :])
        masks[:, base + 1, :] = same & (idx[:, None] <= idx[None, :])
        masks[:, base + 2, :] = same & (idx[:, None] <= idx[None, :])
        masks[:, base + 3, :] = same
    masks[:, M_ID, :] = np.eye(128)
    bands = np.zeros((128, 24, 128), np.float32)
    s = idx[:, None]; t = idx[None, :]
    for g, w in enumerate((2, 4, 8, 16)):
        bands[:, g, :] = ((s <= t) & (s >= t - w + 1)) / w - (s == t)
        bands[:, 4 + g, :] = ((s - 128) >= (t - w + 1)) / w
        cnt = np.minimum(t + 1, w)
        bands[:, 8 + g, :] = ((s <= t) & (s >= t - w + 1)) / cnt - (s == t)
        bs, js = s // LS, s % LS
        bt, jt = t // LS, t % LS
        bands[:, 12 + g, :] = ((bs == bt) & (js <= jt) & (js >= jt - w + 1)) / w - (s == t)
        for i in range(2):
            r = np.arange(128)[:, None]
            rb, ri = r // 15 + 8 * i, r % 15
            bands[:, 16 + 2 * g + i, :] = ((r < 120) & (rb == bt) & (ri >= 16 + jt - w)) / w
    blkc = np.zeros((128, NSEQ, 128), np.float32)
    mcol = np.zeros((128, NSEQ), np.float32)
    for b in range(NSEQ):
        blkc[b * LS:(b + 1) * LS, b, :] = 1.0
        mcol[b * LS:(b + 1) * LS, b] = 1.0
    return _bf(masks), _bf(bands), _bf(blkc), mcol


def _col(v):
    v = np.asarray(v, np.float32)
    return v.reshape(-1, 128).T


_NC_CACHE = {}


def kernel(x_prompt, x_sample, state_pool, state_ssm_conv, state_ssm, state_ffn_conv, cache_mem_k, cache_mem_v,
           mem_prompt, norm_mix, w_in, w_pool_group, pool_scale, w_pool_out, ssm_conv_w, ssm_conv_b, ssm_dt_bias,
           ssm_a_log, ssm_d, ssm_norm, w_ssm_out, w_out, norm_mem_q, w_mem_q, w_mem_o, norm_mem_kv, w_mem_k,
           w_mem_v, norm_ffn, w_ffn_up, ffn_conv_w, ffn_conv_b, w_ffn_down, norm_final):
    f = lambda a: np.ascontiguousarray(np.asarray(a, dtype=np.float32))
    if "nc" not in _NC_CACHE:
        _NC_CACHE["nc"] = build_program()
    nc = _NC_CACHE["nc"]
    masks, bands, blkc, mcol = _consts()
    pcols = np.zeros((2, 128, NCOL), np.float32)
    prow = np.zeros((2, 128, 128), np.float32)
    for l in range(2):
        pcols[l, :, C_NMIX:C_NMIX + 16] = _col(norm_mix[l]); pcols[l, :, C_NQ:C_NQ + 16] = _col(norm_mem_q[l])
        pcols[l, :, C_NF:C_NF + 16] = _col(norm_ffn[l]); pcols[l, :, C_NKV:C_NKV + 16] = _col(norm_mem_kv[l])
        pcols[l, :, C_PSC:C_PSC + 16] = _col(pool_scale[l])
        for j in range(4):
            pcols[l, :, C_CW + j * 48:C_CW + (j + 1) * 48] = _col(ssm_conv_w[l, j])
        pcols[l, :, C_CB:C_CB + 48] = _col(ssm_conv_b[l]); pcols[l, :, C_SN:C_SN + 32] = _col(ssm_norm[l])
        for j in range(3):
            pcols[l, :, C_FW + j * 88:C_FW + (j + 1) * 88] = _col(ffn_conv_w[l, j])
        pcols[l, :, C_FB:C_FB + 88] = _col(ffn_conv_b[l])
        pcols[l, :, C_DC:C_DC + 32] = _col(np.repeat(np.asarray(ssm_d[l], np.float32), 64))
        prow[l, :, 0:64] = np.asarray(ssm_dt_bias[l], np.float32)[None, :]
        prow[l, :, 64:128] = np.asarray(ssm_a_log[l], np.float32)[None, :]
    gfin = np.ascontiguousarray(np.broadcast_to(np.asarray(norm_final, np.float32)[None, :], (128, D)))
    shared = {
        "w_in": f(w_in), "w_pg": f(w_pool_group), "w_po": f(w_pool_out), "w_so": f(w_ssm_out), "w_out": f(w_out),
        "w_q": f(w_mem_q), "w_o": f(w_mem_o), "w_k": f(w_mem_k), "w_v": f(w_mem_v), "w_up": f(w_ffn_up), "w_dn": f(w_ffn_down),
        "pcols": pcols, "prow": prow, "gfin": gfin, "masks": masks, "bands": bands, "blkc": blkc, "mcol": mcol,
    }
    x_prompt = np.asarray(x_prompt); x_sample = np.asarray(x_sample)
    in_maps = []
    for c in range(8):
        sq = c // 2
        bs = slice(NSEQ * c, NSEQ * (c + 1))
        m = dict(shared)
        m["xp"] = f(x_prompt[sq]); m["xs"] = f(x_sample[bs]).reshape(128, D)
        m["spool"] = f(np.asarray(state_pool)[:, bs]).reshape(2, 240, D)
        m["sconv"] = np.ascontiguousarray(f(np.asarray(state_ssm_conv)[:, bs]).reshape(2, NSEQ, 3, 48, 128).transpose(0, 4, 3, 1, 2))
        m["sffn"] = np.ascontiguousarray(f(np.asarray(state_ffn_conv)[:, bs]).reshape(2, NSEQ, 2, 88, 128).transpose(0, 4, 3, 1, 2))
        m["sssm"] = np.ascontiguousarray(f(np.asarray(state_ssm)[:, bs]).transpose(0, 1, 4, 2, 3).reshape(2, NSEQ, 128, 4096))
        m["ck"] = np.ascontiguousarray(f(np.asarray(cache_mem_k)[:, bs]).reshape(2, NSEQ, 256, 16, 128).transpose(0, 1, 4, 3, 2))
        m["cv"] = f(np.asarray(cache_mem_v)[:, bs]).reshape(2, NSEQ, 256, D)
        m["mem"] = f(np.asarray(mem_prompt)[sq])
        in_maps.append(m)
    res = run_bass_kernel_spmd(nc, in_maps, core_ids=list(range(8)))
    R = res.results
    DBG["r0"] = R[0]
    y_prompt = np.stack([R[2 * s]["yp"] for s in range(4)])
    y_sample = np.concatenate([R[c]["ys"].reshape(NSEQ, LS, D) for c in range(8)], axis=0)
    pool_pr = np.stack([R[2 * s]["pool_p"] for s in range(4)], axis=1)
    pool_sm = np.concatenate([R[c]["pool_s"] for c in range(8)], axis=1)
    conv_pr = np.stack([R[2 * s]["conv_p"].transpose(0, 3, 2, 1).reshape(2, 3, 6144) for s in range(4)], axis=1)
    conv_sm = np.concatenate([R[c]["conv_s"].transpose(0, 3, 4, 2, 1).reshape(2, NSEQ, 3, 6144) for c in range(8)], axis=1)
    ssm_pr = np.stack([R[2 * s]["ssm_p"].transpose(0, 2, 1).reshape(2, 64, 64, 128) for s in range(4)], axis=1)
    ssm_sm = np.concatenate([R[c]["ssm_s"].transpose(0, 1, 3, 2).reshape(2, NSEQ, 64, 64, 128) for c in range(8)], axis=1)
    ffn_pr = np.stack([R[2 * s]["ffn_p"].transpose(0, 3, 2, 1).reshape(2, 2, 2 * FFN) for s in range(4)], axis=1)
    ffn_sm = np.concatenate([R[c]["ffn_s"].transpose(0, 3, 4, 2, 1).reshape(2, NSEQ, 2, 2 * FFN) for c in range(8)], axis=1)
    mk = np.stack([R[2 * s]["mk_p"].reshape(2, 256, 4, 512) for s in range(4)], axis=1)
    mv = np.stack([R[2 * s]["mv_p"].reshape(2, 256, 4, 512) for s in range(4)], axis=1)
    outs = (y_prompt, y_sample, pool_pr, pool_sm, conv_pr, conv_sm, ssm_pr, ssm_sm, ffn_pr, ffn_sm, mk, mv)
    return tuple(np.ascontiguousarray(o, dtype=np.float32) for o in outs)
```
